# Optimizing a Trainium2 kernel written in Bass

```python
import math
import jax, jax.numpy as jnp
from jax import lax
import numpy as np

D_MODEL = 1024
BATCH = 8
SEQ = 2048
DEPTH = 4
DEC_BATCH = 128
DEC_SEQ = 4
PAST_LEN = 16384
PAGE_SIZE = 128

N_META = 16
N_MIXERS = 2
N_GDN = (DEPTH + 1) // 2
N_RWKV = DEPTH // 2
GDN_HEADS = 8
GDN_DK = 128
GDN_DV = 128
GDN_QK = GDN_HEADS * GDN_DK
GDN_V = GDN_HEADS * GDN_DV
GDN_CONV = 4
GDN_CONV_DIM = 2 * GDN_QK + GDN_V
GDN_IN = GDN_CONV_DIM + GDN_V + 2 * GDN_HEADS
GDN_CHUNK = 64
RWKV_N = 64
RWKV_HEADS = D_MODEL // RWKV_N
DECAY_LORA = 64
AAA_LORA = 64
MV_LORA = 32
GATE_LORA = 160
D_FF = 2816
FFN_CONV = 3
RMS_EPS = 1e-6
L2_EPS = 1e-6
GN_EPS = 64e-5

kernel_name = 'hybrid_gdn_rwkv7_convffn_meta_step'

F32 = jnp.float32


def rmsnorm(x, g):
    xf = x.astype(F32)
    y = xf * lax.rsqrt(jnp.mean(xf * xf, axis=-1, keepdims=True) + RMS_EPS)
    return (y * g.astype(F32)).astype(x.dtype)


def l2norm(x):
    xf = x.astype(F32)
    return xf * lax.rsqrt(jnp.sum(xf * xf, axis=-1, keepdims=True) + L2_EPS)


def causal_dwconv(buf, x, w):
    width = w.shape[0]
    L = x.shape[1]
    xc = jnp.concatenate([buf.astype(x.dtype), x], axis=1)
    out = xc[:, 0:L] * w[0]
    for j in range(1, width):
        out = out + xc[:, j:j + L] * w[j]
    return out, xc[:, L:]


def gdn_segment(q, k, v, g, beta, S):
    B, L, H, _ = q.shape
    c = min(GDN_CHUNK, L)
    n = -(-L // c)
    pad = n * c - L

    def blocks(t):
        t = jnp.pad(t, [(0, 0), (0, pad)] + [(0, 0)] * (t.ndim - 2))
        t = t.reshape((B, n, c) + t.shape[2:])
        return jnp.moveaxis(t, (1, 3), (0, 2))

    q, k, v, g, beta = blocks(q), blocks(k), blocks(v), blocks(g), blocks(beta)
    G = jnp.cumsum(g, axis=-1)
    idx = jnp.arange(c)
    causal = idx[:, None] >= idx[None, :]
    strict = idx[:, None] > idx[None, :]
    decay_mat = jnp.exp(jnp.where(causal, G[..., :, None] - G[..., None, :], -jnp.inf))
    kb = k * beta[..., None]
    kk = jnp.einsum('nbhid,nbhjd->nbhij', kb, k) * decay_mat
    M = jnp.eye(c, dtype=F32) + jnp.where(strict, kk, 0.0)
    rhs = jnp.concatenate([v * beta[..., None], kb * jnp.exp(G)[..., None]], axis=-1)
    sol = lax.linalg.triangular_solve(M, rhs, left_side=True, lower=True, unit_diagonal=True)
    u, w = sol[..., :GDN_DV], sol[..., GDN_DV:]
    qk = jnp.einsum('nbhid,nbhjd->nbhij', q, k) * decay_mat

    def step(S, xs):
        q_c, k_c, u_c, w_c, G_c, qk_c = xs
        v_new = u_c - jnp.einsum('bhik,bhkv->bhiv', w_c, S)
        o = (jnp.einsum('bhik,bhkv->bhiv', q_c * jnp.exp(G_c)[..., None], S)
             + jnp.einsum('bhij,bhjv->bhiv', qk_c, v_new))
        g_last = G_c[..., -1]
        k_dec = k_c * jnp.exp(g_last[..., None] - G_c)[..., None]
        S = S * jnp.exp(g_last)[..., None, None] + jnp.einsum('bhik,bhiv->bhkv', k_dec, v_new)
        return S, o

    S, o = lax.scan(step, S, (q, k, u, w, G, qk))
    o = jnp.transpose(o, (1, 0, 3, 2, 4)).reshape(B, n * c, H, GDN_DV)[:, :L]
    return o, S


def gdn_mixer(h, conv_buf, S0, w_in, conv_w, A_log, dt_bias, norm_w, w_out, segments):
    B, L, _ = h.shape
    proj = jnp.einsum('bld,de->ble', h, w_in)
    qkv = proj[..., :GDN_CONV_DIM]
    z = proj[..., GDN_CONV_DIM:GDN_CONV_DIM + GDN_V]
    a = proj[..., GDN_CONV_DIM + GDN_V:GDN_CONV_DIM + GDN_V + GDN_HEADS].astype(F32)
    b = proj[..., GDN_CONV_DIM + GDN_V + GDN_HEADS:].astype(F32)
    qkv_c, new_buf = causal_dwconv(conv_buf, qkv, conv_w)
    qkv_c = jax.nn.silu(qkv_c)
    q = l2norm(qkv_c[..., :GDN_QK].reshape(B, L, GDN_HEADS, GDN_DK)) * (GDN_DK ** -0.5)
    k = l2norm(qkv_c[..., GDN_QK:2 * GDN_QK].reshape(B, L, GDN_HEADS, GDN_DK))
    v = qkv_c[..., 2 * GDN_QK:].reshape(B, L, GDN_HEADS, GDN_DV).astype(F32)
    g = -jnp.exp(A_log.astype(F32)) * jax.nn.softplus(a + dt_bias.astype(F32))
    beta = jax.nn.sigmoid(b)
    S = S0.astype(F32)
    outs = []
    start = 0
    for seg in segments:
        sl = slice(start, start + seg)
        o_seg, S = gdn_segment(q[:, sl], k[:, sl], v[:, sl], g[:, sl], beta[:, sl], S)
        outs.append(o_seg)
        start += seg
    o = jnp.concatenate(outs, axis=1) if len(outs) > 1 else outs[0]
    o = o * lax.rsqrt(jnp.mean(o * o, axis=-1, keepdims=True) + RMS_EPS) * norm_w.astype(F32)
    o = o * jax.nn.silu(z.reshape(B, L, GDN_HEADS, GDN_DV).astype(F32))
    y = jnp.einsum('ble,ed->bld', o.reshape(B, L, GDN_V).astype(h.dtype), w_out)
    return y, new_buf.astype(conv_buf.dtype), S.astype(S0.dtype)


def rwkv_step(S, xs):
    r_t, w_t, k_t, v_t, kk_t, a_t = xs
    sa = jnp.einsum('bhvk,bhk->bhv', S, kk_t)
    S = (S * w_t[:, :, None, :] - sa[..., None] * (kk_t * a_t)[:, :, None, :]
         + v_t[..., None] * k_t[:, :, None, :])
    y = jnp.einsum('bhvk,bhk->bhv', S, r_t)
    return S, y


def rwkv_mixer(h, shift, S0, v_first, vres, mix, wr, wk, wv, wo, w0, w1, w2, a0, a1, a2,
               g1, g2, k_k, k_a, r_k, ln_w, ln_b):
    B, L, D = h.shape
    prev = jnp.concatenate([shift[:, None].astype(h.dtype), h[:, :-1]], axis=1)
    xx = prev - h
    xr = h + xx * mix[0]
    xw = h + xx * mix[1]
    xk = h + xx * mix[2]
    xv = h + xx * mix[3]
    xa = h + xx * mix[4]
    xg = h + xx * mix[5]
    r = xr @ wr
    k = xk @ wk
    v = xv @ wv
    w_log = -jax.nn.softplus(-(w0 + jnp.tanh(xw @ w1) @ w2).astype(F32)) - 0.5
    decay = jnp.exp(-jnp.exp(w_log))
    if vres is None:
        v_first = v
    else:
        v0, v1, v2 = vres
        v = v + (v_first - v) * jax.nn.sigmoid(v0 + (xv @ v1) @ v2)
    a = jax.nn.sigmoid((a0 + (xa @ a1) @ a2).astype(F32))
    gate = jax.nn.sigmoid(xg @ g1) @ g2

    def heads(t):
        return t.reshape(B, L, RWKV_HEADS, RWKV_N).astype(F32)

    kk = l2norm(heads(k * k_k))
    kf = k.astype(F32) * (1.0 + (a - 1.0) * k_a.astype(F32))
    rh, kh, vh, ah, dh = heads(r), heads(kf), heads(v), heads(a), heads(decay)
    tm = lambda t: jnp.moveaxis(t, 1, 0)
    S, y = lax.scan(rwkv_step, S0.astype(F32), (tm(rh), tm(dh), tm(kh), tm(vh), tm(kk), tm(ah)))
    y = jnp.moveaxis(y, 0, 1)
    mu = jnp.mean(y, axis=-1, keepdims=True)
    var = jnp.mean(jnp.square(y - mu), axis=-1, keepdims=True)
    y = ((y - mu) * lax.rsqrt(var + GN_EPS)).reshape(B, L, D) * ln_w.astype(F32) + ln_b.astype(F32)
    bonus = jnp.sum(rh * kh * r_k.astype(F32), axis=-1, keepdims=True) * vh
    y = y + bonus.reshape(B, L, D)
    out = (y * gate.astype(F32)).astype(h.dtype) @ wo
    return out, h[:, -1].astype(shift.dtype), S.astype(S0.dtype), v_first


def conv_ffn(h, buf, w_up, conv_w, w_down):
    up2 = h @ w_up
    gt, up = up2[..., :D_FF], up2[..., D_FF:]
    gc, new_buf = causal_dwconv(buf, gt, conv_w)
    y = (jax.nn.silu(gc) * up) @ w_down
    return y, new_buf.astype(buf.dtype)


def trunk(x, gdn_S, gdn_conv, rwkv_S, rwkv_shift, ffn_conv, segments, p):
    gS, gC, rS, rSh, fC = [], [], [], [], []
    v_first = None
    for i in range(DEPTH):
        hn = rmsnorm(x, p['norm_mix'][i])
        j = i // N_MIXERS
        if i % N_MIXERS == 0:
            y, cb, S = gdn_mixer(hn, gdn_conv[j], gdn_S[j], p['gdn_w_in'][j], p['gdn_conv_w'][j],
                                 p['gdn_A_log'][j], p['gdn_dt_bias'][j], p['gdn_norm_w'][j],
                                 p['gdn_w_out'][j], segments)
            gS.append(S)
            gC.append(cb)
        else:
            vres = None if j == 0 else (p['rwkv_v0'][j - 1], p['rwkv_v1'][j - 1], p['rwkv_v2'][j - 1])
            y, sh, S, v_first = rwkv_mixer(
                hn, rwkv_shift[j], rwkv_S[j], v_first, vres, p['rwkv_mix'][j], p['rwkv_wr'][j],
                p['rwkv_wk'][j], p['rwkv_wv'][j], p['rwkv_wo'][j], p['rwkv_w0'][j], p['rwkv_w1'][j],
                p['rwkv_w2'][j], p['rwkv_a0'][j], p['rwkv_a1'][j], p['rwkv_a2'][j], p['rwkv_g1'][j],
                p['rwkv_g2'][j], p['rwkv_k_k'][j], p['rwkv_k_a'][j], p['rwkv_r_k'][j],
                p['rwkv_ln_w'][j], p['rwkv_ln_b'][j])
            rS.append(S)
            rSh.append(sh)
        x = x + y
        hn = rmsnorm(x, p['norm_ffn'][i])
        y, fb = conv_ffn(hn, ffn_conv[i], p['ffn_w_up'][i], p['ffn_conv_w'][i], p['ffn_w_down'][i])
        fC.append(fb)
        x = x + y
    x = rmsnorm(x, p['norm_final'])
    return x, jnp.stack(gS), jnp.stack(gC), jnp.stack(rS), jnp.stack(rSh), jnp.stack(fC)


def setup_inputs(seed: int = 0) -> dict:
    key = jax.random.key(seed)
    ks = iter(jax.random.split(key, 64))
    D = D_MODEL

    def nrm(shape, scale):
        return jax.random.normal(next(ks), shape, F32) * scale

    def gain(shape):
        return 1.0 + nrm(shape, 0.02)

    def unif(shape, lo, hi):
        return jax.random.uniform(next(ks), shape, F32, lo, hi)

    dt = jnp.exp(unif((N_GDN, GDN_HEADS), math.log(1e-3), math.log(1e-1)))
    gdn_dt_bias = dt + jnp.log(-jnp.expm1(-dt))
    return {
        'x_prompt': nrm((BATCH, SEQ, D), 1.0),
        'x_sample': nrm((DEC_BATCH, DEC_SEQ, D), 1.0),
        'state_gdn_S': nrm((N_GDN, DEC_BATCH, GDN_HEADS, GDN_DK, GDN_DV), 0.1),
        'state_gdn_conv': nrm((N_GDN, DEC_BATCH, GDN_CONV - 1, GDN_CONV_DIM), 1.0),
        'state_rwkv_S': nrm((N_RWKV, DEC_BATCH, RWKV_HEADS, RWKV_N, RWKV_N), 0.3),
        'state_rwkv_shift': nrm((N_RWKV, DEC_BATCH, D), 1.0),
        'state_ffn_conv': nrm((DEPTH, DEC_BATCH, FFN_CONV - 1, D_FF), 1.0),
        'meta_tokens': nrm((N_META, D), 1.0),
        'norm_mix': gain((DEPTH, D)),
        'norm_ffn': gain((DEPTH, D)),
        'norm_final': gain((D,)),
        'gdn_w_in': nrm((N_GDN, D, GDN_IN), D ** -0.5),
        'gdn_conv_w': nrm((N_GDN, GDN_CONV, GDN_CONV_DIM), GDN_CONV ** -0.5),
        'gdn_A_log': jnp.log(unif((N_GDN, GDN_HEADS), 1.0, 16.0)),
        'gdn_dt_bias': gdn_dt_bias,
        'gdn_norm_w': gain((N_GDN, GDN_DV)),
        'gdn_w_out': nrm((N_GDN, GDN_V, D), GDN_V ** -0.5),
        'rwkv_mix': unif((N_RWKV, 6, D), 0.0, 1.0),
        'rwkv_wr': nrm((N_RWKV, D, D), D ** -0.5),
        'rwkv_wk': nrm((N_RWKV, D, D), D ** -0.5),
        'rwkv_wv': nrm((N_RWKV, D, D), D ** -0.5),
        'rwkv_wo': nrm((N_RWKV, D, D), D ** -0.5),
        'rwkv_w0': unif((N_RWKV, D), -4.0, 0.0),
        'rwkv_w1': nrm((N_RWKV, D, DECAY_LORA), D ** -0.5),
        'rwkv_w2': nrm((N_RWKV, DECAY_LORA, D), 0.1 * DECAY_LORA ** -0.5),
        'rwkv_a0': nrm((N_RWKV, D), 0.1),
        'rwkv_a1': nrm((N_RWKV, D, AAA_LORA), D ** -0.5),
        'rwkv_a2': nrm((N_RWKV, AAA_LORA, D), 0.1 * AAA_LORA ** -0.5),
        'rwkv_g1': nrm((N_RWKV, D, GATE_LORA), D ** -0.5),
        'rwkv_g2': nrm((N_RWKV, GATE_LORA, D), GATE_LORA ** -0.5),
        'rwkv_k_k': 0.85 + nrm((N_RWKV, D), 0.02),
        'rwkv_k_a': gain((N_RWKV, D)),
        'rwkv_r_k': nrm((N_RWKV, RWKV_HEADS, RWKV_N), 0.1),
        'rwkv_ln_w': gain((N_RWKV, D)),
        'rwkv_ln_b': nrm((N_RWKV, D), 0.01),
        'rwkv_v0': nrm((N_RWKV - 1, D), 0.1),
        'rwkv_v1': nrm((N_RWKV - 1, D, MV_LORA), D ** -0.5),
        'rwkv_v2': nrm((N_RWKV - 1, MV_LORA, D), 0.1 * MV_LORA ** -0.5),
        'ffn_w_up': nrm((DEPTH, D, 2 * D_FF), D ** -0.5),
        'ffn_conv_w': nrm((DEPTH, FFN_CONV, D_FF), FFN_CONV ** -0.5),
        'ffn_w_down': nrm((DEPTH, D_FF, D), D_FF ** -0.5),
    }


def reference(x_prompt, x_sample, state_gdn_S, state_gdn_conv, state_rwkv_S, state_rwkv_shift,
              state_ffn_conv, meta_tokens, norm_mix, norm_ffn, norm_final, gdn_w_in, gdn_conv_w,
              gdn_A_log, gdn_dt_bias, gdn_norm_w, gdn_w_out, rwkv_mix, rwkv_wr, rwkv_wk, rwkv_wv,
              rwkv_wo, rwkv_w0, rwkv_w1, rwkv_w2, rwkv_a0, rwkv_a1, rwkv_a2, rwkv_g1, rwkv_g2,
              rwkv_k_k, rwkv_k_a, rwkv_r_k, rwkv_ln_w, rwkv_ln_b, rwkv_v0, rwkv_v1, rwkv_v2,
              ffn_w_up, ffn_conv_w, ffn_w_down):
    p = dict(norm_mix=norm_mix, norm_ffn=norm_ffn, norm_final=norm_final, gdn_w_in=gdn_w_in,
             gdn_conv_w=gdn_conv_w, gdn_A_log=gdn_A_log, gdn_dt_bias=gdn_dt_bias,
             gdn_norm_w=gdn_norm_w, gdn_w_out=gdn_w_out, rwkv_mix=rwkv_mix, rwkv_wr=rwkv_wr,
             rwkv_wk=rwkv_wk, rwkv_wv=rwkv_wv, rwkv_wo=rwkv_wo, rwkv_w0=rwkv_w0, rwkv_w1=rwkv_w1,
             rwkv_w2=rwkv_w2, rwkv_a0=rwkv_a0, rwkv_a1=rwkv_a1, rwkv_a2=rwkv_a2, rwkv_g1=rwkv_g1,
             rwkv_g2=rwkv_g2, rwkv_k_k=rwkv_k_k, rwkv_k_a=rwkv_k_a, rwkv_r_k=rwkv_r_k,
             rwkv_ln_w=rwkv_ln_w, rwkv_ln_b=rwkv_ln_b, rwkv_v0=rwkv_v0, rwkv_v1=rwkv_v1,
             rwkv_v2=rwkv_v2, ffn_w_up=ffn_w_up, ffn_conv_w=ffn_conv_w, ffn_w_down=ffn_w_down)
    B = x_prompt.shape[0]
    dt = x_prompt.dtype
    meta = jnp.broadcast_to(meta_tokens.astype(dt)[None], (B, N_META, D_MODEL))
    xp = jnp.concatenate([meta, x_prompt], axis=1)
    Lp = xp.shape[1]
    z_gS = jnp.zeros((N_GDN, B, GDN_HEADS, GDN_DK, GDN_DV), dt)
    z_gC = jnp.zeros((N_GDN, B, GDN_CONV - 1, GDN_CONV_DIM), dt)
    z_rS = jnp.zeros((N_RWKV, B, RWKV_HEADS, RWKV_N, RWKV_N), dt)
    z_rSh = jnp.zeros((N_RWKV, B, D_MODEL), dt)
    z_fC = jnp.zeros((DEPTH, B, FFN_CONV - 1, D_FF), dt)
    yp, gS_p, gC_p, rS_p, rSh_p, fC_p = trunk(xp, z_gS, z_gC, z_rS, z_rSh, z_fC,
                                               [N_META, Lp - N_META], p)
    y_prompt = yp[:, N_META:]
    y_sample, gS_s, gC_s, rS_s, rSh_s, fC_s = trunk(x_sample, state_gdn_S, state_gdn_conv,
                                                     state_rwkv_S, state_rwkv_shift,
                                                     state_ffn_conv, [x_sample.shape[1]], p)
    return (y_prompt, y_sample, gS_p, gC_p, rS_p, rSh_p, fC_p, gS_s, gC_s, rS_s, rSh_s, fC_s)
```

```python
import numpy as np
from contextlib import ExitStack
import concourse.bass as bass
import concourse.mybir as mybir
from concourse.bass_utils import run_bass_kernel_spmd

F32 = mybir.dt.float32
BF16 = mybir.dt.bfloat16
AF = mybir.ActivationFunctionType
ALU = mybir.AluOpType

NCORE = 8
D = 1024
T = 2176
PAD = 48
P0 = 64
S0 = 2112
DFF = 2816
NF = 22
TILES = [(0, 512), (512, 1024), (1024, 1536), (1536, 2048), (2048, 2176)]
FGROUPS = [(0, 768), (768, 1536), (1536, 2176)]


class Trk:
    __slots__ = ("w", "r")

    def __init__(self):
        self.w = None
        self.r = {}


class Sched:
    ENGS = ("pe", "act", "dve", "pool", "sp")

    def __init__(self, nc):
        self.nc = nc
        self.q = {e: [] for e in self.ENGS}
        self.cnt = {}
        self.sem = {}
        self.seen = {e: {} for e in self.ENGS}

    NDS = 24

    def alloc_sems(self, stack):
        self.dslot = {}
        for e in self.ENGS:
            self.sem[e] = stack.enter_context(self.nc.semaphore("s_" + e))
            self.cnt[e] = 0
        for e in ("sp", "pool"):
            self.dslot[e] = 0
            for i in range(self.NDS):
                d = "%s.d%d" % (e, i)
                self.sem[d] = stack.enter_context(self.nc.semaphore("d_%s_%d" % (e, i)))
                self.cnt[d] = 0

    def _waits(self, eng, reads, writes):
        need = {}
        for t in reads:
            if t.w is not None:
                p, c = t.w
                if p == eng and eng == "pe":
                    continue
                if need.get(p, 0) < c:
                    need[p] = c
        for t in writes:
            if t.w is not None:
                p, c = t.w
                if p != eng and need.get(p, 0) < c:
                    need[p] = c
            for p, c in t.r.items():
                if p != eng and need.get(p, 0) < c:
                    need[p] = c
        out = []
        for p, c in need.items():
            if self.seen[eng].get(p, 0) >= c:
                continue
            self.seen[eng][p] = c
            out.append((p, c))
        return out

    @staticmethod
    def _mark(prod, c, reads, writes):
        for t in reads:
            if t.r.get(prod, 0) < c:
                t.r[prod] = c
        for t in writes:
            t.w = (prod, c)
            t.r = {}

    def op(self, eng, fn, reads=(), writes=(), inc=True):
        waits = self._waits(eng, reads, writes)
        c = self.cnt[eng] + 1
        if inc:
            self.cnt[eng] = c
        sem = self.sem

        def run(h):
            for p, cc in waits:
                h.wait_ge(sem[p], cc * 16 if ".d" in p else cc)
            ins = fn(h)
            if inc:
                ins.then_inc(sem[eng], 1)
        self.q[eng].append(run)
        self._mark(eng, c, reads, writes)

    def dma(self, eng, out, in_, reads=(), writes=(), **kw):
        waits = self._waits(eng, reads, writes)
        d = "%s.d%d" % (eng, self.dslot[eng] % self.NDS)
        self.dslot[eng] += 1
        if self.cnt[d] > 0 and self.seen[eng].get(d, 0) < self.cnt[d]:
            self.seen[eng][d] = self.cnt[d]
            waits = waits + [(d, self.cnt[d])]
        self.cnt[d] += 1
        c = self.cnt[d]
        sem = self.sem

        def run(h):
            for p, cc in waits:
                h.wait_ge(sem[p], cc * 16 if ".d" in p else cc)
            h.dma_start(out=out, in_=in_, **kw).then_inc(sem[d], 16)
        self.q[eng].append(run)
        self._mark(d, c, reads, writes)

    def barrier(self):
        sem = self.sem
        finals = [(p, c) for p, c in self.cnt.items() if c > 0]
        for e in self.ENGS:
            mine = []
            for p, c in finals:
                if self.seen[e].get(p, 0) < c:
                    self.seen[e][p] = c
                    mine.append((p, c))

            def run(h, mine=mine):
                for p, cc in mine:
                    h.wait_ge(sem[p], cc * 16 if ".d" in p else cc)
            self.q[e].append(run)

    def emit(self, block):
        for e, deco in (("pe", block.tensor), ("act", block.scalar), ("dve", block.vector),
                        ("pool", block.gpsimd), ("sp", block.sync)):
            lst = self.q[e]

            def body(h, lst=lst):
                for r in lst:
                    r(h)
            deco(body)


def _feat(v):
    v = np.asarray(v, np.float32)
    n = v.shape[-1] // 128
    v = v.reshape(v.shape[:-1] + (n, 128))
    return np.ascontiguousarray(np.moveaxis(v, -1, 0))


class PTab:
    def __init__(self):
        self.cols = []
        self.off = {}
        self.n = 0

    def add(self, name, arr):
        arr = np.asarray(arr, np.float32).reshape(128, -1)
        self.off[name] = self.n
        self.cols.append(arr)
        self.n += arr.shape[1]

    def build(self):
        return np.ascontiguousarray(np.concatenate(self.cols, axis=1))


def make_ptab(inp):
    pt = PTab()
    pt.add("norm_mix", _feat(inp["norm_mix"]))
    pt.add("norm_ffn", _feat(inp["norm_ffn"]))
    pt.add("norm_final", _feat(inp["norm_final"]))
    pt.add("ffn_cw", _feat(inp["ffn_conv_w"]))
    pt.add("gdn_cw", _feat(inp["gdn_conv_w"]))
    pt.add("gdn_nw", np.asarray(inp["gdn_norm_w"], np.float32).T)
    z = np.zeros((128, 2), np.float32)
    ga = z.copy(); ga[64:72] = np.asarray(inp["gdn_A_log"], np.float32).T
    gd = z.copy(); gd[64:72] = np.asarray(inp["gdn_dt_bias"], np.float32).T
    pt.add("rw_mix", _feat(inp["rwkv_mix"]))
    for nm_, key_ in (("rw_w0", "rwkv_w0"), ("rw_a0", "rwkv_a0"), ("rw_kk", "rwkv_k_k"), ("rw_ka", "rwkv_k_a"),
                      ("rw_lnw", "rwkv_ln_w"), ("rw_lnb", "rwkv_ln_b"), ("rw_v0", "rwkv_v0")):
        pt.add(nm_, _feat(inp[key_]))
    pt.add("rw_rk", _feat(np.asarray(inp["rwkv_r_k"], np.float32).reshape(2, D)))
    pt.add("gdn_A", ga)
    pt.add("gdn_dt", gd)
    return pt


PT_OFF = None


class KB:
    def __init__(self, nc, st, S):
        self.nc = nc
        self.st = st
        self.S = S
        self.psb = []
        self.psi = 0

    def sb(self, name, shape, dt):
        return self.st.enter_context(self.nc.sbuf_tensor(name, shape, dt))

    def init_psum(self):
        for i in range(8):
            t = self.st.enter_context(self.nc.psum_tensor("ps%d" % i, [128, 512], F32))
            self.psb.append((t, Trk()))

    def ps(self):
        r = self.psb[self.psi]
        self.psi = (self.psi + 1) % 6
        return r


def build(cfg):
    nc = bass.Bass("TRN2", target_bir_lowering=False)
    dram_in = lambda n, s: nc.dram_tensor(n, list(s), F32, kind="ExternalInput").ap()
    dram_out = lambda n, s: nc.dram_tensor(n, list(s), F32, kind="ExternalOutput").ap()
    nL = cfg.get("nlayers", 4)
    pt_off = cfg["pt_off"]
    pt_n = cfg["pt_n"]

    xin = dram_in("xin", (T, D))
    ptab_d = dram_in("ptab", (128, pt_n))
    ffn_conv_in = dram_in("ffn_conv_in", (4, 16, 2, DFF))
    w_up = dram_in("ffn_w_up", (4, D, 2 * DFF))
    w_down = dram_in("ffn_w_down", (4, DFF, D))
    gdn_w_in = dram_in("gdn_w_in", (2, D, 4112))
    gdn_w_out = dram_in("gdn_w_out", (2, D, D))
    gdn_S_in = dram_in("gdn_S_in", (2, 16, 8, 128, 128))
    gdn_conv_in = dram_in("gdn_conv_in", (2, 16, 3, 3072))
    gS_p = dram_out("gS_p", (2, 8, 128, 128))
    gC_p = dram_out("gC_p", (2, 3, 3072))
    gS_s = dram_out("gS_s", (2, 16, 8, 128, 128))
    gC_s = dram_out("gC_s", (2, 16, 3, 3072))
    rwkv_S_in = dram_in("rwkv_S_in", (2, 16, 16, 64, 64))
    rwkv_shift_in = dram_in("rwkv_shift_in", (2, 16, D))
    rwkv_wr = dram_in("rwkv_wr", (2, D, D)); rwkv_wk = dram_in("rwkv_wk", (2, D, D))
    rwkv_wv = dram_in("rwkv_wv", (2, D, D)); rwkv_wo = dram_in("rwkv_wo", (2, D, D))
    rwkv_w1 = dram_in("rwkv_w1", (2, D, 64)); rwkv_w2 = dram_in("rwkv_w2", (2, 64, D))
    rwkv_a1 = dram_in("rwkv_a1", (2, D, 64)); rwkv_a2 = dram_in("rwkv_a2", (2, 64, D))
    rwkv_g1 = dram_in("rwkv_g1", (2, D, 160)); rwkv_g2 = dram_in("rwkv_g2", (2, 160, D))
    rwkv_v1 = dram_in("rwkv_v1", (1, D, 32)); rwkv_v2 = dram_in("rwkv_v2", (1, 32, D))
    vf_d = nc.dram_tensor("vfirst_scratch", [8, 128, T], F32).ap()
    rS_p = dram_out("rS_p", (2, 16, 64, 64))
    rSh_p = dram_out("rSh_p", (2, D))
    rS_s = dram_out("rS_s", (2, 16, 16, 64, 64))
    rSh_s = dram_out("rSh_s", (2, 16, D))
    yout = dram_out("yout", (T, D))
    fC_p = dram_out("fC_p", (4, 2, DFF))
    fC_s = dram_out("fC_s", (4, 16, 2, DFF))

    with ExitStack() as st:
        S = Sched(nc)
        S.alloc_sems(st)
        K = KB(nc, st, S)
        K.init_psum()
        sb = K.sb
        xT = sb("xT", [128, 8, T], F32)
        t_x = [Trk() for _ in TILES]
        ptab = sb("ptab_sb", [128, pt_n], F32); t_pt = Trk()
        ident = sb("ident", [128, 128], F32); t_id = Trk()
        onesb = sb("onesb", [128, 128], BF16); t_ones = Trk()
        cst = sb("cst", [128, 4], F32); t_cst = Trk()
        ARENA = 33000
        arena = sb("arena", [128, ARENA], F32)
        block = st.enter_context(nc.Block())

        def tile_of(c0, c1):
            return [t_x[i] for i, (a, b) in enumerate(TILES) if a < c1 and c0 < b]

        S.dma("sp", ptab[:], ptab_d, writes=[t_pt])
        S.op("pool", lambda h: h.memset(ident[:], 0.0), writes=[t_id])
        S.op("pool", lambda h: h.affine_select(out=ident[:], in_=ident[:], pattern=[[-1, 128]],
                                                 compare_op=ALU.not_equal, fill=1.0, base=0,
                                                 channel_multiplier=1), reads=[t_id], writes=[t_id])
        S.op("pool", lambda h: h.memset(onesb[:], 1.0 / 1024.0), writes=[t_ones])
        S.op("pool", lambda h: h.memset(cst[:, 0:1], 1e-6), writes=[t_cst])
        S.op("pool", lambda h: h.memset(cst[:, 1:2], 64e-5), writes=[t_cst])
        S.op("pool", lambda h: h.memset(cst[:, 2:3], 1.0), writes=[t_cst])
        S.op("pool", lambda h: h.memset(cst[:, 3:4], 0.0), writes=[t_cst])

        tri_t = sb("tri_t", [64, 4, 64], F32); t_tri = Trk()
        rmk = sb("rmk", [64, 16], F32)
        tri_s, tri_i, trib_s, trib_i = tri_t[:, 0, :], tri_t[:, 1, :], tri_t[:, 2, :], tri_t[:, 3, :]
        S.op("pool", lambda h: h.memset(tri_t[:], 1.0), writes=[t_tri])
        S.op("pool", lambda h: h.memset(rmk[:], 1.0), writes=[t_tri])
        for (ix, op_) in ((0, ALU.is_gt), (1, ALU.is_ge), (2, ALU.is_gt), (3, ALU.is_ge)):
            S.op("pool", lambda h, ix=ix, op_=op_: h.affine_select(
                out=tri_t[:, ix, :], in_=tri_t[:, ix, :], pattern=[[1, 64]], compare_op=op_, fill=0.0, base=0,
                channel_multiplier=-1), reads=[t_tri], writes=[t_tri])
        for ix in (2, 3):
            v = tri_t[:, ix, :].rearrange("p (s k) -> p s k", k=4)
            S.op("pool", lambda h, v=v: h.affine_select(out=v, in_=v, pattern=[[-4, 16], [0, 4]], compare_op=ALU.is_ge,
                                                       fill=0.0, base=0, channel_multiplier=1),
                 reads=[t_tri], writes=[t_tri])
            S.op("pool", lambda h, v=v: h.affine_select(out=v, in_=v, pattern=[[4, 16], [0, 4]], compare_op=ALU.is_ge,
                                                       fill=0.0, base=3, channel_multiplier=-1),
                 reads=[t_tri], writes=[t_tri])
        S.op("pool", lambda h: h.affine_select(out=rmk[:], in_=rmk[:], pattern=[[-4, 16]], compare_op=ALU.is_ge,
                                               fill=0.0, base=0, channel_multiplier=1), reads=[t_tri], writes=[t_tri])
        S.op("pool", lambda h: h.affine_select(out=rmk[:], in_=rmk[:], pattern=[[4, 16]], compare_op=ALU.is_ge,
                                               fill=0.0, base=3, channel_multiplier=-1), reads=[t_tri], writes=[t_tri])

        def pcol(name, idx):
            o = pt_off[name] + idx
            return ptab[:, o:o + 1]

        a_f = arena[:, 0:2 * 1024].rearrange("p (i n) -> p i n", i=2)
        t_in = [Trk(), Trk()]
        for it in range(T // 128):
            buf = a_f[:, it % 2, :]
            tb = t_in[it % 2]
            S.dma("sp", buf, xin[it * 128:(it + 1) * 128, :], writes=[tb])
            for half in range(2):
                pst, pt_ = K.ps()
                for cc in range(4):
                    c = half * 4 + cc
                    S.op("pe", lambda h, pst=pst, cc=cc, buf=buf, c=c: h.transpose(
                        out=pst[:, cc * 128:(cc + 1) * 128], in_=buf[:, c * 128:(c + 1) * 128],
                        identity=ident[:]), reads=[tb, t_id], writes=[pt_], inc=(cc == 3))
                tl = tile_of(it * 128, it * 128 + 128)
                S.op("act" if half else "dve",
                     (lambda h, pst=pst, half=half, it=it: h.copy(
                         out=xT[:, half * 4:half * 4 + 4, it * 128:(it + 1) * 128],
                         in_=pst[:].rearrange("p (c n) -> p c n", c=4))) if half else
                     (lambda h, pst=pst, half=half, it=it: h.tensor_copy(
                         out=xT[:, half * 4:half * 4 + 4, it * 128:(it + 1) * 128],
                         in_=pst[:].rearrange("p (c n) -> p c n", c=4))),
                     reads=[pt_], writes=tl)
        S.barrier()

        def rmsnorm(dst, dc0, gname, gidx, c0, c1, sq, t_sq, rs, t_rs, t_dst):
            for (a, b) in TILES:
                a2, b2 = max(a, c0), min(b, c1)
                if a2 >= b2:
                    continue
                n = b2 - a2
                tl = tile_of(a2, b2)
                for c in range(8):
                    S.op("act", lambda h, c=c, a2=a2, b2=b2, n=n: h.activation(
                        out=sq[:, c, 0:n], in_=xT[:, c, a2:b2], func=AF.Square),
                        reads=tl, writes=[t_sq])
                pst, pt_ = K.ps()
                for c in range(8):
                    S.op("pe", lambda h, c=c, n=n, pst=pst: h.matmul(
                        pst[:, 0:n], lhsT=onesb[:], rhs=sq[:, c, 0:n], start=(c == 0), stop=(c == 7)),
                        reads=[t_sq, t_ones], writes=[pt_], inc=(c == 7))
                S.op("act", lambda h, n=n, pst=pst: h.activation(
                    out=rs[:, 0:n], in_=pst[:, 0:n], func=AF.Sqrt, bias=cst[:, 0:1], scale=1.0),
                    reads=[pt_, t_cst], writes=[t_rs])
                S.op("dve", lambda h, n=n: h.reciprocal(out=rs[:, 0:n], in_=rs[:, 0:n]),
                     reads=[t_rs], writes=[t_rs])
                for c in range(8):
                    S.op("dve", lambda h, c=c, a2=a2, b2=b2, n=n: h.scalar_tensor_tensor(
                        out=dst[:, c, dc0 + a2 - c0:dc0 + b2 - c0], in0=xT[:, c, a2:b2],
                        scalar=pcol(gname, gidx * 8 + c), in1=rs[:, 0:n], op0=ALU.mult, op1=ALU.mult),
                        reads=tl + [t_rs, t_pt], writes=[t_dst])

        def ffn_layer(l):
            o = 0
            actT = arena[:, o:o + 8448].bitcast(BF16).rearrange("p (f n) -> p f n", f=NF); o += 8448
            hn2 = arena[:, o:o + 3072].bitcast(BF16).rearrange("p (c n) -> p c n", c=8); o += 3072
            wdb = [arena[:, o + i * 1408:o + (i + 1) * 1408].bitcast(BF16).rearrange(
                "p (f n) -> p f n", f=NF) for i in range(2)]; o += 2816
            wu = [arena[:, o + i * 2048:o + (i + 1) * 2048].bitcast(BF16).rearrange(
                "p (c g n) -> p c g n", c=8, g=2) for i in range(2)]; o += 4096
            sq = arena[:, o:o + 2048].bitcast(BF16).rearrange("p (c n) -> p c n", c=8); o += 2048
            rs = arena[:, o:o + 512]; o += 512
            gpre = [arena[:, o + i * 516:o + (i + 1) * 516] for i in range(2)]; o += 1032
            gps = arena[:, o:o + 96].rearrange("p (s k) -> p s k", s=16); o += 96
            gc = arena[:, o:o + 512]; o += 512
            carry = arena[:, o:o + 44].rearrange("p (f k) -> p f k", f=NF); o += 44
            hist = arena[:, o:o + NF * 32].rearrange("p (f s k) -> p f s k", f=NF, s=16); o += NF * 32
            stg = arena[:, o:o + NF * 34].rearrange("p (f k) -> p f k", f=NF); o += NF * 34
            rows = arena[0:34, 0:DFF]
            hrow = arena[0:32, 0:DFF]
            assert o <= ARENA, o
            t_act, t_hn2, t_sq, t_rs, t_gc = Trk(), Trk(), Trk(), Trk(), Trk()
            t_wdb = [Trk(), Trk()]
            t_wu = [Trk(), Trk()]
            t_gpre = [Trk(), Trk()]
            t_gps, t_carry, t_hist, t_stg = Trk(), Trk(), Trk(), Trk()
            t_rows = t_act
            t_hrow = t_act

            wd_src = w_down[l].rearrange("(f p) n -> p f n", p=128)
            S.dma("sp", hrow, ffn_conv_in[l].rearrange("s k n -> (s k) n"), writes=[t_hrow])
            for f0 in range(0, NF, 4):
                pst, pt_ = K.ps()
                nf = min(4, NF - f0)
                for ff in range(nf):
                    f = f0 + ff
                    S.op("pe", lambda h, pst=pst, ff=ff, f=f: h.transpose(
                        out=pst[:, ff * 32:(ff + 1) * 32], in_=hrow[:, f * 128:(f + 1) * 128],
                        identity=ident[0:32, 0:32]), reads=[t_hrow, t_id], writes=[pt_], inc=(ff == nf - 1))
                S.op("dve", lambda h, pst=pst, f0=f0, nf=nf: h.tensor_copy(
                    out=hist[:, f0:f0 + nf, :, :].rearrange("p f s k -> p f (s k)"),
                    in_=pst[:, 0:nf * 32].rearrange("p (f n) -> p f n", f=nf)),
                    reads=[pt_], writes=[t_hist])
            S.op("dve", lambda h: h.memset(carry, 0.0), writes=[t_carry])

            up_src = w_up[l].rearrange("(c p) n -> p c n", p=128)
            wi = 0
            wdi = 0
            for (g0_, g1_) in FGROUPS:
              def do_group(g0, g1):
                  nonlocal wi, wdi
                  rmsnorm(hn2, 0, "norm_ffn", l, g0, g1, sq, t_sq, rs, t_rs, t_hn2)
                  gtiles = [(max(a, g0), min(b, g1)) for (a, b) in TILES if max(a, g0) < min(b, g1)]
                  for f2 in range(0, NF, 2):
                      wt = wu[wi % 2]; twu = t_wu[wi % 2]; wi += 1
                      S.dma("pool", wt[:, :, 0, :], up_src[:, :, f2 * 128:f2 * 128 + 256], writes=[twu])
                      S.dma("pool", wt[:, :, 1, :], up_src[:, :, DFF + f2 * 128:DFF + f2 * 128 + 256],
                            writes=[twu])
                      for fi in range(2):
                          f = f2 + fi
                          cw = lambda j, f=f: pcol("ffn_cw", (l * 3 + j) * NF + f)
                          for ti, (a, b) in enumerate(gtiles):
                              n = b - a
                              psg, tg = K.ps()
                              psu, tu = K.ps()
                              for c in range(8):
                                  S.op("pe", lambda h, c=c, a=a, b=b, n=n, psg=psg, wt=wt, fi=fi: h.matmul(
                                      psg[:, 0:n], lhsT=wt[:, c, 0, fi * 128:(fi + 1) * 128],
                                      rhs=hn2[:, c, a - g0:b - g0], start=(c == 0), stop=(c == 7)),
                                      reads=[twu, t_hn2], writes=[tg], inc=(c == 7))
                              for c in range(8):
                                  S.op("pe", lambda h, c=c, a=a, b=b, n=n, psu=psu, wt=wt, fi=fi: h.matmul(
                                      psu[:, 0:n], lhsT=wt[:, c, 1, fi * 128:(fi + 1) * 128],
                                      rhs=hn2[:, c, a - g0:b - g0], start=(c == 0), stop=(c == 7)),
                                      reads=[twu, t_hn2], writes=[tu], inc=(c == 7))
                              gp = gpre[ti % 2]; tgp = t_gpre[ti % 2]
                              S.op("dve", lambda h, gp=gp, f=f: h.tensor_copy(out=gp[:, 0:2], in_=carry[:, f, :]),
                                   reads=[t_carry], writes=[tgp])
                              S.op("act", lambda h, gp=gp, n=n, psg=psg: h.copy(out=gp[:, 2:2 + n], in_=psg[:, 0:n]),
                                   reads=[tg], writes=[tgp])
                              S.op("dve", lambda h, gp=gp, n=n, f=f: h.tensor_copy(out=carry[:, f, :],
                                                                                  in_=gp[:, n:n + 2]),
                                   reads=[tgp], writes=[t_carry])
                              npz = n if b <= S0 else S0 - a
                              S.op("dve", lambda h, gp=gp, npz=npz, cw=cw: h.tensor_scalar(
                                  out=gc[:, 0:npz], in0=gp[:, 2:2 + npz], scalar1=cw(2), scalar2=None,
                                  op0=ALU.mult), reads=[tgp, t_pt], writes=[t_gc])
                              S.op("dve", lambda h, gp=gp, npz=npz, cw=cw: h.scalar_tensor_tensor(
                                  out=gc[:, 0:npz], in0=gp[:, 1:1 + npz], scalar=cw(1), in1=gc[:, 0:npz],
                                  op0=ALU.mult, op1=ALU.add), reads=[tgp, t_pt, t_gc], writes=[t_gc])
                              S.op("dve", lambda h, gp=gp, npz=npz, cw=cw: h.scalar_tensor_tensor(
                                  out=gc[:, 0:npz], in0=gp[:, 0:npz], scalar=cw(0), in1=gc[:, 0:npz],
                                  op0=ALU.mult, op1=ALU.add), reads=[tgp, t_pt, t_gc], writes=[t_gc])
                              if b > S0:
                                  so = S0 - a
                                  S.op("dve", lambda h, f=f: h.tensor_copy(out=gps[:, :, 0:2], in_=hist[:, f, :, :]),
                                       reads=[t_hist], writes=[t_gps])
                                  S.op("dve", lambda h, gp=gp, so=so: h.tensor_copy(
                                      out=gps[:, :, 2:6], in_=gp[:, 2 + so:2 + so + 64].rearrange("p (s k) -> p s k", s=16)),
                                      reads=[tgp], writes=[t_gps])
                                  gcs = gc[:, so:so + 64].rearrange("p (s k) -> p s k", s=16)
                                  S.op("dve", lambda h, gcs=gcs, cw=cw: h.tensor_scalar(
                                      out=gcs, in0=gps[:, :, 2:6], scalar1=cw(2), scalar2=None, op0=ALU.mult),
                                      reads=[t_gps, t_pt], writes=[t_gc])
                                  S.op("dve", lambda h, gcs=gcs, cw=cw: h.scalar_tensor_tensor(
                                      out=gcs, in0=gps[:, :, 1:5], scalar=cw(1), in1=gcs, op0=ALU.mult, op1=ALU.add),
                                      reads=[t_gps, t_pt, t_gc], writes=[t_gc])
                                  S.op("dve", lambda h, gcs=gcs, cw=cw: h.scalar_tensor_tensor(
                                      out=gcs, in0=gps[:, :, 0:4], scalar=cw(0), in1=gcs, op0=ALU.mult, op1=ALU.add),
                                      reads=[t_gps, t_pt, t_gc], writes=[t_gc])
                                  S.op("pool", lambda h, gp=gp, so=so, f=f: h.tensor_copy(
                                      out=stg[:, f, 0:2], in_=gp[:, so:so + 2]), reads=[tgp], writes=[t_stg])
                                  S.op("pool", lambda h, f=f: h.tensor_copy(
                                      out=stg[:, f, 2:34].rearrange("p (s k) -> p s k", s=16), in_=gps[:, :, 4:6]),
                                      reads=[t_gps], writes=[t_stg])
                              S.op("act", lambda h, n=n: h.activation(out=gc[:, 0:n], in_=gc[:, 0:n], func=AF.Silu),
                                   reads=[t_gc], writes=[t_gc])
                              S.op("dve", lambda h, n=n, a=a, b=b, f=f, psu=psu: h.tensor_tensor(
                                  out=actT[:, f, a - g0:b - g0], in0=gc[:, 0:n], in1=psu[:, 0:n], op=ALU.mult),
                                  reads=[t_gc, tu], writes=[t_act])
                  for oc in range(8):
                      wd = wdb[wdi % 2]; t_wd = t_wdb[wdi % 2]; wdi += 1
                      S.dma("pool", wd, wd_src[:, :, oc * 128:(oc + 1) * 128], writes=[t_wd])
                      for (a, b) in gtiles:
                          n = b - a
                          pso, to = K.ps()
                          for f in range(NF):
                              S.op("pe", lambda h, f=f, a=a, b=b, n=n, pso=pso, wd=wd: h.matmul(
                                  pso[:, 0:n], lhsT=wd[:, f, :], rhs=actT[:, f, a - g0:b - g0],
                                  start=(f == 0), stop=(f == NF - 1)),
                                  reads=[t_wd, t_act], writes=[to], inc=(f == NF - 1))
                          a2 = max(a, PAD)
                          tl = tile_of(a2, b)
                          S.op("dve", lambda h, a2=a2, a=a, b=b, pso=pso, oc=oc: h.tensor_tensor(
                              out=xT[:, oc, a2:b], in0=xT[:, oc, a2:b], in1=pso[:, a2 - a:b - a], op=ALU.add),
                              reads=[to] + tl, writes=tl)

              do_group(g0_, g1_)
            for f0 in range(0, NF, 4):
                nf = min(4, NF - f0)
                pst, pt_ = K.ps()
                for ff in range(nf):
                    f = f0 + ff
                    S.op("pe", lambda h, pst=pst, ff=ff, f=f: h.transpose(
                        out=pst[0:34, ff * 128:(ff + 1) * 128], in_=stg[:, f, :], identity=ident[:]),
                        reads=[t_stg, t_id], writes=[pt_], inc=(ff == nf - 1))
                S.op("act", lambda h, pst=pst, f0=f0, nf=nf: h.copy(
                    out=rows[:, f0 * 128:(f0 + nf) * 128], in_=pst[0:34, 0:nf * 128]),
                    reads=[pt_], writes=[t_rows])
            S.dma("sp", fC_p[l], rows[0:2, :], reads=[t_rows])
            S.dma("sp", fC_s[l].rearrange("s k n -> (s k) n"), rows[2:34, :], reads=[t_rows])
            S.barrier()

        Y_BANKS = [K.psb[6], K.psb[7]]

        def solve_blocks(nb, B0, Ca, Cb, Ba, Bb, Pa, Pb, t_B0, t_C, t_B, t_P):
            ngr = (nb + 7) // 8
            def grp(g):
                return range(g * 8, min(nb, g * 8 + 8))
            C = [Ca, Cb]; B = [B0, Ba, Bb]
            for g in range(ngr):
                pst, pt_ = K.ps()
                bl = list(grp(g))
                for i, b_ in enumerate(bl):
                    S.op("pe", lambda h, pst=pst, i=i, b_=b_: h.transpose(
                        out=pst[0:64, i * 64:(i + 1) * 64], in_=B0[:, b_, :], identity=ident[0:64, 0:64]),
                        reads=[t_B0, t_id], writes=[pt_], inc=(i == len(bl) - 1))
                S.op("act", lambda h, pst=pst, bl=bl: h.copy(
                    out=Ca[:, bl[0]:bl[-1] + 1, :], in_=pst[0:64, 0:len(bl) * 64].rearrange("p (b n) -> p b n", n=64)),
                    reads=[pt_], writes=[t_C[0]])
            S.op("dve", lambda h: h.tensor_tensor(
                out=Pa[:, 0:nb, :], in0=B0[:, 0:nb, :], in1=ident[0:64, 0:64].unsqueeze(1).to_broadcast([64, nb, 64]),
                op=ALU.add), reads=[t_B0, t_id], writes=[t_P[0]])
            Bcur, tBcur = B0, t_B0
            Ccur, tCcur = Ca, t_C[0]
            Pcur, tPcur, Pn, tPn = Pa, t_P[0], Pb, t_P[1]
            Bn = [Ba, Bb]; tBn = [t_B[0], t_B[1]]
            Cn = [Cb, Ca]; tCn = [t_C[1], t_C[0]]
            for i in range(1, 6):
                Cnew, tCnew = Cn[(i - 1) % 2], tCn[(i - 1) % 2]
                Bnew, tBnew = Bn[(i - 1) % 2], tBn[(i - 1) % 2]
                for g in range(ngr):
                    pst, pt_ = K.ps()
                    bl = list(grp(g))
                    for k_, b_ in enumerate(bl):
                        S.op("pe", lambda h, pst=pst, k_=k_, b_=b_, Bcur=Bcur, Ccur=Ccur: h.matmul(
                            pst[0:64, k_ * 64:(k_ + 1) * 64], lhsT=Bcur[:, b_, :], rhs=Ccur[:, b_, :],
                            start=True, stop=True), reads=[tBcur, tCcur], writes=[pt_], inc=(k_ == len(bl) - 1))
                    S.op("act", lambda h, pst=pst, bl=bl, Cnew=Cnew: h.copy(
                        out=Cnew[:, bl[0]:bl[-1] + 1, :],
                        in_=pst[0:64, 0:len(bl) * 64].rearrange("p (b n) -> p b n", n=64)),
                        reads=[pt_], writes=[tCnew])
                if i < 5:
                    for g in range(ngr):
                        pst, pt_ = K.ps()
                        bl = list(grp(g))
                        for k_, b_ in enumerate(bl):
                            S.op("pe", lambda h, pst=pst, k_=k_, b_=b_, Bcur=Bcur, Ccur=Ccur: h.matmul(
                                pst[0:64, k_ * 64:(k_ + 1) * 64], lhsT=Ccur[:, b_, :], rhs=Bcur[:, b_, :],
                                start=True, stop=True), reads=[tBcur, tCcur], writes=[pt_], inc=(k_ == len(bl) - 1))
                        S.op("dve", lambda h, pst=pst, bl=bl, Bnew=Bnew: h.tensor_copy(
                            out=Bnew[:, bl[0]:bl[-1] + 1, :],
                            in_=pst[0:64, 0:len(bl) * 64].rearrange("p (b n) -> p b n", n=64)),
                            reads=[pt_], writes=[tBnew])
                for g in range(ngr):
                    pst, pt_ = K.ps()
                    bl = list(grp(g))
                    for k_, b_ in enumerate(bl):
                        S.op("pe", lambda h, pst=pst, k_=k_, b_=b_, Cnew=Cnew, Pcur=Pcur: h.matmul(
                            pst[0:64, k_ * 64:(k_ + 1) * 64], lhsT=Cnew[:, b_, :], rhs=Pcur[:, b_, :],
                            start=True, stop=True), reads=[tCnew, tPcur], writes=[pt_], inc=(k_ == len(bl) - 1))
                    S.op("dve", lambda h, pst=pst, bl=bl, Pn=Pn, Pcur=Pcur: h.tensor_tensor(
                        out=Pn[:, bl[0]:bl[-1] + 1, :], in0=Pcur[:, bl[0]:bl[-1] + 1, :],
                        in1=pst[0:64, 0:len(bl) * 64].rearrange("p (b n) -> p b n", n=64), op=ALU.add),
                        reads=[pt_, tPcur], writes=[tPn])
                Bcur, tBcur = Bnew, tBnew
                Ccur, tCcur = Cnew, tCnew
                Pcur, tPcur, Pn, tPn = Pn, tPn, Pcur, tPcur
            return Pcur, tPcur

        def seq_block(blk, dv, pr, fcols, states, ops, ybank, ycols, yparts, tcols=(0, 64)):
            p0, p1 = pr
            c0, c1 = fcols
            (ns, t_ns), (rs_, t_rs_) = ops["ns"], ops["rs"]
            (V, t_V), (Kh, t_Kh), (nPh, t_nPh) = ops["V"], ops["Kh"], ops["nPh"]
            (Ank, t_Ank), (Ark, t_Ark), (nArp, t_nArp), (TT, t_TT) = ops["Ank"], ops["Ark"], ops["nArp"], ops["TT"]
            (Xs, t_Xs), (Es, t_Es) = ops["Xs"], ops["Es"]
            single = len(states) == 1
            psx, tpx = K.ps()
            if single:
                stt = states[0]
                S.op("pe", lambda h: h.matmul(psx[0:64, 0:dv], lhsT=Ank[:, blk, :], rhs=V[:, blk, 0:dv],
                                              start=True, stop=False), reads=[t_Ank, t_V], writes=[tpx], inc=False)
                S.op("pe", lambda h: h.matmul(psx[0:64, 0:dv], lhsT=ns[p0:p1, c0:c1], rhs=stt["Z"],
                                              start=False, stop=True), reads=[t_ns, stt["tZ"]], writes=[tpx])
                S.op("act", lambda h: h.copy(out=Xs[:, 0:dv], in_=psx[0:64, 0:dv]), reads=[tpx], writes=[t_Xs])
            else:
                S.op("pe", lambda h: h.matmul(psx[0:dv, 0:64], lhsT=V[:, blk, 0:dv], rhs=Ank[:, blk, :],
                                              start=True, stop=False), reads=[t_Ank, t_V], writes=[tpx], inc=False)
                for si, stt in enumerate(states):
                    S.op("pe", lambda h, stt=stt, si=si: h.matmul(
                        psx[0:dv, stt["lo"]:stt["hi"]], lhsT=stt["Z"], rhs=ns[p0:p1, c0 + stt["lo"]:c0 + stt["hi"]],
                        start=False, stop=(si == len(states) - 1)), reads=[t_ns, stt["tZ"]], writes=[tpx],
                        inc=(si == len(states) - 1))
                XT, t_XT = ops["XT"]
                S.op("act", lambda h: h.copy(out=XT[0:dv, :], in_=psx[0:dv, 0:64]), reads=[tpx], writes=[t_XT])
                psx2, tpx2 = K.ps()
                S.op("pe", lambda h: h.transpose(out=psx2[0:64, 0:dv], in_=XT[0:dv, :], identity=ident[0:dv, 0:dv]),
                     reads=[t_XT, t_id], writes=[tpx2])
                S.op("act", lambda h: h.copy(out=Xs[:, 0:dv], in_=psx2[0:64, 0:dv]), reads=[tpx2], writes=[t_Xs])
            pse, tpe = K.ps()
            S.op("pe", lambda h: h.matmul(pse[0:64, 0:dv], lhsT=TT[:, blk, :], rhs=Xs[:, 0:dv], start=True, stop=True),
                 reads=[t_TT, t_Xs], writes=[tpe])
            S.op("dve", lambda h: h.tensor_copy(out=Es[:, 0:dv], in_=pse[0:64, 0:dv]), reads=[tpe], writes=[t_Es])
            yb, tyb = ybank
            q0, q1 = yparts
            tl0, tl1 = tcols
            S.op("pe", lambda h: h.matmul(yb[q0:q1, ycols[0] + tl0:ycols[0] + tl1], lhsT=V[:, blk, 0:dv],
                                          rhs=Ark[:, blk, tl0:tl1],
                                          start=True, stop=False), reads=[t_V, t_Ark], writes=[tyb], inc=False)
            for stt in states:
                S.op("pe", lambda h, stt=stt: h.matmul(
                    yb[q0:q1, ycols[0] + stt["lo"]:ycols[0] + stt["hi"]], lhsT=stt["Z"],
                    rhs=rs_[p0:p1, c0 + stt["lo"]:c0 + stt["hi"]], start=False, stop=False),
                    reads=[t_rs_, stt["tZ"]], writes=[tyb], inc=False)
            S.op("pe", lambda h: h.matmul(yb[q0:q1, ycols[0] + tl0:ycols[0] + tl1], lhsT=Es[:, 0:dv],
                                          rhs=nArp[:, blk, tl0:tl1],
                                          start=False, stop=True), reads=[t_Es, t_nArp], writes=[tyb])
            for stt in states:
                psz, tpz = K.ps()
                if stt.get("mask") is not None:
                    Km, t_Km = ops["Km"]; nPm, t_nPm = ops["nPm"]
                    S.op("pool", lambda h, stt=stt: h.tensor_scalar(out=Km, in0=Kh[:, blk, :], scalar1=stt["mask"],
                                                                   scalar2=None, op0=ALU.mult),
                         reads=[t_Kh, ops["t_mask"]], writes=[t_Km])
                    S.op("pool", lambda h, stt=stt: h.tensor_scalar(out=nPm, in0=nPh[:, blk, :], scalar1=stt["mask"],
                                                                   scalar2=None, op0=ALU.mult),
                         reads=[t_nPh, ops["t_mask"]], writes=[t_nPm])
                    lk, tlk, lp, tlp = Km, t_Km, nPm, t_nPm
                else:
                    lk, tlk, lp, tlp = Kh[:, blk, :], t_Kh, nPh[:, blk, :], t_nPh
                S.op("pe", lambda h, psz=psz, lk=lk: h.matmul(psz[p0:p1, 0:dv], lhsT=lk, rhs=V[:, blk, 0:dv],
                                                              start=True, stop=False),
                     reads=[tlk, t_V], writes=[tpz], inc=False)
                S.op("pe", lambda h, psz=psz, lp=lp: h.matmul(psz[p0:p1, 0:dv], lhsT=lp, rhs=Es[:, 0:dv],
                                                              start=False, stop=True),
                     reads=[tlp, t_Es], writes=[tpz])
                S.op("dve", lambda h, stt=stt, psz=psz: h.scalar_tensor_tensor(
                    out=stt["Z"], in0=stt["Z"], scalar=stt["dc"], in1=psz[p0:p1, 0:dv], op0=ALU.mult, op1=ALU.add),
                    reads=[stt["tZ"], stt["t_dc"], tpz], writes=[stt["tZ"]])

        def gdn_layer(l):
            j = l // 2
            o = [0]

            def A(n):
                r = arena[:, o[0]:o[0] + n]; o[0] += n
                assert o[0] <= ARENA, o[0]
                return r
            hn = A(8704).bitcast(BF16).rearrange("p (c n) -> p c n", c=8); t_hn = Trk()
            qt = A(T); qt2 = A(T); t_qt = Trk(); t_qt2 = Trk()
            sel = A(1024).rearrange("p (h m) -> p h m", h=8); t_sel = Trk()
            wq = A(2048).bitcast(BF16).rearrange("p (c g n) -> p c g n", c=8, g=4); t_wq = Trk()
            wo = A(512).bitcast(BF16); t_wo = Trk()
            abw = A(64).bitcast(BF16).rearrange("p (c n) -> p c n", c=8); t_abw = Trk()
            pre = A(515); t_pre = Trk()
            post = A(1536).rearrange("p (g n) -> p g n", g=3); t_post = [Trk(), Trk(), Trk()]
            zs = A(512); t_zs = Trk()
            sqb = A(256).bitcast(BF16); t_sqb = Trk()
            rn = A(512); t_rn = Trk()
            ns = A(512); t_ns = Trk()
            rsq = A(512); t_rsq = Trk()
            og = A(256).bitcast(BF16); t_og = Trk()
            pres = A(112).rearrange("p (s k) -> p s k", s=16); t_pres = Trk()
            dcs = A(32); t_dcs = Trk()
            hist = A(144).rearrange("p (g k) -> p g k", g=3); t_hist = Trk()
            hrow = A(384)[0:48, :]; t_hrow = Trk()
            stg = A(153).rearrange("p (g k) -> p g k", g=3); t_stg = Trk()
            srow = A(384)[0:51, :]; t_srow = Trk()
            ones1 = A(64).bitcast(BF16); t_ones1 = Trk()
            scr_ = A(2048)
            sq_scr = scr_.bitcast(BF16).rearrange("p (c n) -> p c n", c=8)
            Vt = scr_[0:64, 0:1024].rearrange("p (b n) -> p b n", b=8); t_Vt = Trk()
            Kh = scr_[0:64, 1024:2048].rearrange("p (b n) -> p b n", b=8); t_Kh = t_Vt
            nPh = A(1024)[0:64, :].rearrange("p (b n) -> p b n", b=8); t_nPh = Trk()
            v64 = lambda: A(512)[0:64, :].rearrange("p (b n) -> p b n", b=8)
            M1 = v64(); M2 = v64(); t_M = [Trk(), Trk()]
            Ank = v64(); B0 = v64(); Ark = v64(); nArp = v64()
            t_Ank, t_B0, t_Ark, t_nArp = Trk(), Trk(), Trk(), Trk()
            Ca = v64(); Cb = v64(); Ba = v64(); Bb = v64()
            t_C = [Trk(), Trk()]; t_B = [Trk(), Trk()]
            Xs = A(128)[0:64, :]; Es = A(128)[0:64, :]; XT = A(64); t_Xs, t_Es, t_XT = Trk(), Trk(), Trk()
            Z = A(128); t_Z = Trk()
            Zs = A(512).rearrange("p (s n) -> p s n", s=4); t_Zs = [Trk() for _ in range(4)]
            Km = A(128)[0:64, :]; nPm = A(128)[0:64, :]; t_Km, t_nPm = Trk(), Trk()
            tm1 = A(256)[0:64, :].rearrange("p (b q h) -> p b q h", b=8, q=4)
            tm2 = A(256)[0:64, :].rearrange("p (b q h) -> p b q h", b=8, q=4)
            t_tm = Trk()
            kdb = A(16)[0:64, :].rearrange("p (b q) -> p b q", b=8); t_kdb = Trk()

            S.op("pool", lambda h: h.memset(ones1, 1.0), writes=[t_ones1])
            S.op("pool", lambda h: h.memset(sel, 0.0), writes=[t_sel])
            for base in (0, 32):
                S.op("pool", lambda h, base=base: h.affine_select(
                    out=sel[base:base + 32], in_=sel[base:base + 32], pattern=[[-1, 8], [0, 128]],
                    compare_op=ALU.not_equal, fill=1.0, base=0, channel_multiplier=1),
                    reads=[t_sel], writes=[t_sel])
            S.op("pool", lambda h: h.memset(qt2[32:64, :], 1.0), writes=[t_qt2])
            S.op("pool", lambda h: h.memset(qt2[32:64, 0:S0].rearrange("p (c k) -> p c k", k=64)[:, :, 0:1], 0.0),
                 writes=[t_qt2])
            S.op("pool", lambda h: h.memset(qt2[32:64, S0:T].rearrange("p (c k) -> p c k", k=4)[:, :, 0:1], 0.0),
                 writes=[t_qt2])
            S.op("pool", lambda h: h.memset(qt[:], 0.0), writes=[t_qt])
            win = gdn_w_in[j].rearrange("(c p) n -> p c n", p=128)
            S.dma("pool", abw, win[:, :, 4096:4112], writes=[t_abw])
            rmsnorm(hn, 0, "norm_mix", l, 0, T, sq_scr, t_Vt, rn, t_rn, t_hn)
            S.op("act", lambda h: h.activation(out=dcs[64:72, 24:25], in_=pcol("gdn_A", j)[64:72], func=AF.Exp),
                 reads=[t_pt], writes=[t_dcs])
            S.op("dve", lambda h: h.tensor_scalar(out=dcs[64:72, 24:25], in0=dcs[64:72, 24:25], scalar1=-1.0,
                                                  scalar2=None, op0=ALU.mult), reads=[t_dcs], writes=[t_dcs])
            for (a, b) in TILES:
                n = b - a
                pst, pt_ = K.ps()
                for c in range(8):
                    S.op("pe", lambda h, c=c, a=a, b=b, n=n, pst=pst: h.matmul(
                        pst[64:72, 0:n], lhsT=abw[:, c, 0:8], rhs=hn[:, c, a:b], start=(c == 0), stop=(c == 7)),
                        reads=[t_abw, t_hn], writes=[pt_], inc=(c == 7))
                for c in range(8):
                    S.op("pe", lambda h, c=c, a=a, b=b, n=n, pst=pst: h.matmul(
                        pst[0:8, 0:n], lhsT=abw[:, c, 8:16], rhs=hn[:, c, a:b], start=(c == 0), stop=(c == 7)),
                        reads=[t_abw, t_hn], writes=[pt_], inc=(c == 7))
                S.op("act", lambda h, a=a, b=b, n=n, pst=pst: h.activation(
                    out=qt2[64:72, a:b], in_=pst[64:72, 0:n], func=AF.Exp, bias=pcol("gdn_dt", j)[64:72], scale=1.0),
                    reads=[pt_, t_pt], writes=[t_qt2])
                S.op("act", lambda h, a=a, b=b: h.activation(
                    out=qt2[64:72, a:b], in_=qt2[64:72, a:b], func=AF.Ln, bias=cst[64:72, 2:3], scale=1.0),
                    reads=[t_qt2, t_cst], writes=[t_qt2])
                S.op("dve", lambda h, a=a, b=b: h.tensor_scalar(
                    out=qt2[64:72, a:b], in0=qt2[64:72, a:b], scalar1=dcs[64:72, 24:25], scalar2=None, op0=ALU.mult),
                    reads=[t_qt2, t_dcs], writes=[t_qt2])
                S.op("act", lambda h, a=a, b=b, n=n, pst=pst: h.activation(
                    out=qt[0:8, a:b], in_=pst[0:8, 0:n], func=AF.Sigmoid), reads=[pt_], writes=[t_qt])
            S.dma("sp", qt[96:104, :], qt[0:8, :], reads=[t_qt], writes=[t_qt])
            S.dma("sp", qt[32:40, :], qt2[64:72, :], reads=[t_qt2], writes=[t_qt])
            S.dma("sp", qt[0:8, :], qt2[64:72, :], reads=[t_qt2, t_qt], writes=[t_qt])
            S.op("dve", lambda h: h.tensor_tensor_scan(out=qt[32:40, :], data0=qt2[32:40, :], data1=qt[32:40, :],
                                                       initial=0.0, op0=ALU.mult, op1=ALU.add),
                 reads=[t_qt, t_qt2], writes=[t_qt])
            S.dma("sp", qt2[0:8, :], qt[32:40, :], reads=[t_qt], writes=[t_qt2])
            S.op("dve", lambda h: h.tensor_tensor(out=qt[0:8, :], in0=qt2[0:8, :], in1=qt[0:8, :], op=ALU.subtract),
                 reads=[t_qt, t_qt2], writes=[t_qt])
            S.op("dve", lambda h: h.tensor_tensor(
                out=qt2[32:40, 0:S0].rearrange("p (c k) -> p c k", k=64),
                in0=qt[32:40, 0:S0].rearrange("p (c k) -> p c k", k=64)[:, :, 63:64].to_broadcast([8, 33, 64]),
                in1=qt[32:40, 0:S0].rearrange("p (c k) -> p c k", k=64), op=ALU.subtract),
                reads=[t_qt], writes=[t_qt2])
            S.op("dve", lambda h: h.tensor_tensor(
                out=qt2[32:40, S0:T].rearrange("p (c k) -> p c k", k=4),
                in0=qt[32:40, S0:T].rearrange("p (c k) -> p c k", k=4)[:, :, 3:4].to_broadcast([8, 16, 4]),
                in1=qt[32:40, S0:T].rearrange("p (c k) -> p c k", k=4), op=ALU.subtract),
                reads=[t_qt], writes=[t_qt2])
            S.op("act", lambda h: h.activation(out=qt2[32:40, :], in_=qt2[32:40, :], func=AF.Exp),
                 reads=[t_qt2], writes=[t_qt2])
            S.op("act", lambda h: h.activation(out=qt2[64:72, :], in_=qt2[64:72, :], func=AF.Exp),
                 reads=[t_qt2], writes=[t_qt2])

            cwq = lambda g, h_, k_: pcol("gdn_cw", (j * 4 + k_) * 24 + g * 8 + h_)
            wout = gdn_w_out[j]
            ybi = 0
            def do_head(hd):
                nonlocal ybi
                for g in range(4):
                    S.dma("pool", wq[:, :, g, :], win[:, :, g * 1024 + hd * 128:g * 1024 + (hd + 1) * 128],
                          writes=[t_wq])
                S.dma("pool", wo, wout[hd * 128:(hd + 1) * 128, :], writes=[t_wo])
                for g in range(3):
                    S.dma("sp", hrow[:, g * 128:(g + 1) * 128],
                          gdn_conv_in[j].rearrange("s k n -> (s k) n")[:, g * 1024 + hd * 128:g * 1024 + (hd + 1) * 128],
                          writes=[t_hrow])
                pst, pt_ = K.ps()
                for g in range(3):
                    S.op("pe", lambda h, g=g, pst=pst: h.transpose(
                        out=pst[:, g * 48:(g + 1) * 48], in_=hrow[:, g * 128:(g + 1) * 128], identity=ident[0:48, 0:48]),
                        reads=[t_hrow, t_id], writes=[pt_], inc=(g == 2))
                S.op("dve", lambda h, pst=pst: h.tensor_copy(out=hist, in_=pst[:, 0:144].rearrange("p (g k) -> p g k", g=3)),
                     reads=[pt_], writes=[t_hist])
                S.op("dve", lambda h: h.memset(Z, 0.0), writes=[t_Z])
                def do_tile(ti, a, b):
                    nonlocal ybi
                    n = b - a
                    nb = n // 64
                    last = (ti == len(TILES) - 1)
                    for g in range(4):
                        psp, tpp = K.ps()
                        for c in range(8):
                            S.op("pe", lambda h, c=c, g=g, a=a, b=b, n=n, psp=psp: h.matmul(
                                psp[:, 0:n], lhsT=wq[:, c, g, :], rhs=hn[:, c, a:b], start=(c == 0), stop=(c == 7)),
                                reads=[t_wq, t_hn], writes=[tpp], inc=(c == 7))
                        if g == 3:
                            pass
                            S.op("act", lambda h, n=n, psp=psp: h.activation(out=zs[:, 0:n], in_=psp[:, 0:n], func=AF.Silu),
                                 reads=[tpp], writes=[t_zs])
                            continue
                        if ti == 0:
                            S.op("dve", lambda h: h.memset(pre[:, 0:3], 0.0), writes=[t_pre])
                        else:
                            S.op("dve", lambda h, g=g: h.tensor_copy(out=pre[:, 0:3], in_=stg[:, g, 48:51]),
                                 reads=[t_stg], writes=[t_pre])
                        S.op("act", lambda h, n=n, psp=psp: h.copy(out=pre[:, 3:3 + n], in_=psp[:, 0:n]),
                             reads=[tpp], writes=[t_pre])
                        npz = n if not last else S0 - a
                        S.op("pool", lambda h, g=g, npz=npz: h.tensor_copy(out=stg[:, g, 48:51], in_=pre[:, npz:npz + 3]),
                             reads=[t_pre], writes=[t_stg])
                        tp = t_post[g]
                        S.op("dve", lambda h, g=g, npz=npz: h.tensor_scalar(
                            out=post[:, g, 0:npz], in0=pre[:, 3:3 + npz], scalar1=cwq(g, hd, 3), scalar2=None,
                            op0=ALU.mult), reads=[t_pre, t_pt], writes=[tp])
                        for k_ in range(3):
                            S.op("dve", lambda h, g=g, npz=npz, k_=k_: h.scalar_tensor_tensor(
                                out=post[:, g, 0:npz], in0=pre[:, k_:k_ + npz], scalar=cwq(g, hd, k_),
                                in1=post[:, g, 0:npz], op0=ALU.mult, op1=ALU.add), reads=[t_pre, t_pt, tp], writes=[tp])
                        if last:
                            so = S0 - a
                            S.op("dve", lambda h, g=g: h.tensor_copy(
                                out=pres[:, :, 0:3], in_=hist[:, g, :].rearrange("p (s k) -> p s k", s=16)),
                                reads=[t_hist], writes=[t_pres])
                            S.op("dve", lambda h, so=so: h.tensor_copy(
                                out=pres[:, :, 3:7], in_=pre[:, 3 + so:3 + so + 64].rearrange("p (s k) -> p s k", s=16)),
                                reads=[t_pre], writes=[t_pres])
                            ps_ = post[:, g, so:so + 64].rearrange("p (s k) -> p s k", s=16)
                            S.op("dve", lambda h, g=g, ps_=ps_: h.tensor_scalar(
                                out=ps_, in0=pres[:, :, 3:7], scalar1=cwq(g, hd, 3), scalar2=None, op0=ALU.mult),
                                reads=[t_pres, t_pt], writes=[tp])
                            for k_ in range(3):
                                S.op("dve", lambda h, g=g, ps_=ps_, k_=k_: h.scalar_tensor_tensor(
                                    out=ps_, in0=pres[:, :, k_:k_ + 4], scalar=cwq(g, hd, k_), in1=ps_,
                                    op0=ALU.mult, op1=ALU.add), reads=[t_pres, t_pt, tp], writes=[tp])
                            S.op("pool", lambda h, g=g: h.tensor_copy(
                                out=stg[:, g, 0:48].rearrange("p (s k) -> p s k", s=16), in_=pres[:, :, 4:7]),
                                reads=[t_pres], writes=[t_stg])
                        S.op("act", lambda h, g=g, n=n: h.activation(out=post[:, g, 0:n], in_=post[:, g, 0:n], func=AF.Silu),
                             reads=[tp], writes=[tp])
                    for g in range(2):
                        tp = t_post[g]
                        S.op("act", lambda h, g=g, n=n: h.activation(out=sqb[:, 0:n], in_=post[:, g, 0:n], func=AF.Square),
                             reads=[tp], writes=[t_sqb])
                        pss, tps = K.ps()
                        S.op("pe", lambda h, n=n, pss=pss: h.matmul(pss[:, 0:n], lhsT=ones1, rhs=sqb[:, 0:n],
                                                                    start=True, stop=True),
                             reads=[t_sqb, t_ones1], writes=[tps])
                        S.op("act", lambda h, n=n, pss=pss: h.activation(out=rn[:, 0:n], in_=pss[:, 0:n], func=AF.Sqrt,
                                                                         bias=cst[:, 0:1], scale=1.0),
                             reads=[tps, t_cst], writes=[t_rn])
                        S.op("dve", lambda h, n=n: h.reciprocal(out=rn[:, 0:n], in_=rn[:, 0:n]), reads=[t_rn], writes=[t_rn])
                        if g == 0:
                            S.op("dve", lambda h, n=n: h.scalar_tensor_tensor(
                                out=post[:, 0, 0:n], in0=post[:, 0, 0:n], scalar=float(128 ** -0.5), in1=rn[:, 0:n],
                                op0=ALU.mult, op1=ALU.mult), reads=[tp, t_rn], writes=[tp])
                        else:
                            S.op("dve", lambda h, n=n: h.tensor_tensor(out=post[:, 1, 0:n], in0=post[:, 1, 0:n],
                                                                       in1=rn[:, 0:n], op=ALU.mult),
                                 reads=[tp, t_rn], writes=[tp])
                    qn, kn, vv = post[:, 0, :], post[:, 1, :], post[:, 2, :]
                    for (src, dst, tsrc) in ((qt, tm1, t_qt), (qt2, tm2, t_qt2)):
                        for b0_ in range(0, nb, 4):
                            nn = min(4, nb - b0_)
                            pst, pt_ = K.ps()
                            for bi in range(nn):
                                S.op("pe", lambda h, bi=bi, b0_=b0_, src=src, pst=pst, a=a: h.transpose(
                                    out=pst[0:64, bi * 128:(bi + 1) * 128],
                                    in_=src[:, a + (b0_ + bi) * 64:a + (b0_ + bi + 1) * 64], identity=ident[:]),
                                    reads=[tsrc, t_id], writes=[pt_], inc=(bi == nn - 1))
                            S.op("dve", lambda h, pst=pst, dst=dst, b0_=b0_, nn=nn: h.tensor_copy(
                                out=dst[:, b0_:b0_ + nn, :, :],
                                in_=pst[0:64, 0:nn * 128].rearrange("p (b q r) -> p b q r", b=nn, q=4)[:, :, :, 0:8]),
                                reads=[pt_], writes=[t_tm])
                    S.op("dve", lambda h, nb=nb: h.tensor_tensor(out=kdb[:, 0:nb, 0], in0=tm2[:, 0:nb, 1, hd],
                                                                 in1=tm1[:, 0:nb, 3, hd], op=ALU.mult),
                         reads=[t_tm], writes=[t_kdb])
                    S.op("dve", lambda h, nb=nb: h.scalar_tensor_tensor(
                        out=kdb[:, 0:nb, 1], in0=kdb[:, 0:nb, 0], scalar=-1.0, in1=tm2[:, 0:nb, 2, hd],
                        op0=ALU.mult, op1=ALU.mult), reads=[t_tm, t_kdb], writes=[t_kdb])
                    psg1, tg1 = K.ps()
                    S.op("pe", lambda h, a=a, b=b, n=n, psg1=psg1: h.matmul(
                        psg1[:, 0:n], lhsT=sel[0:8, hd, :], rhs=qt[0:8, a:b], start=True, stop=True),
                        reads=[t_sel, t_qt], writes=[tg1])
                    psg2, tg2 = K.ps()
                    S.op("pe", lambda h, a=a, b=b, n=n, psg2=psg2: h.matmul(
                        psg2[:, 0:n], lhsT=sel[32:40, hd, :], rhs=qt[32:40, a:b], start=True, stop=True),
                        reads=[t_sel, t_qt], writes=[tg2])
                    for (Mx, tMx, psg, tg, tri_p, tri_sm) in ((M1, t_M[0], psg1, tg1, tri_s, trib_s),
                                                              (M2, t_M[1], psg2, tg2, tri_i, trib_i)):
                        S.op("dve", lambda h, Mx=Mx, psg=psg, nb=nb: h.tensor_tensor(
                            out=Mx[:, 0:nb, :], in0=psg[0:64, 0:nb * 64].rearrange("p (b n) -> p b n", n=64),
                            in1=tm1[:, 0:nb, 1, hd:hd + 1].to_broadcast([64, nb, 64]), op=ALU.subtract),
                            reads=[tg, t_tm], writes=[tMx])
                        npb = nb - 1 if last else nb
                        if npb > 0:
                            S.op("dve", lambda h, Mx=Mx, npb=npb, tri_p=tri_p: h.tensor_tensor(
                                out=Mx[:, 0:npb, :], in0=Mx[:, 0:npb, :],
                                in1=tri_p.unsqueeze(1).to_broadcast([64, npb, 64]), op=ALU.mult),
                                reads=[tMx, t_tri], writes=[tMx])
                        if last:
                            S.op("dve", lambda h, Mx=Mx, nb=nb, tri_sm=tri_sm: h.tensor_tensor(
                                out=Mx[:, nb - 1, :], in0=Mx[:, nb - 1, :], in1=tri_sm, op=ALU.mult),
                                reads=[tMx, t_tri], writes=[tMx])
                        S.op("act", lambda h, Mx=Mx, nb=nb: h.activation(out=Mx[:, 0:nb, :], in_=Mx[:, 0:nb, :], func=AF.Exp),
                             reads=[tMx], writes=[tMx])
                        if npb > 0:
                            S.op("dve", lambda h, Mx=Mx, npb=npb, tri_p=tri_p: h.tensor_tensor(
                                out=Mx[:, 0:npb, :], in0=Mx[:, 0:npb, :],
                                in1=tri_p.unsqueeze(1).to_broadcast([64, npb, 64]), op=ALU.mult),
                                reads=[tMx, t_tri], writes=[tMx])
                        if last:
                            S.op("dve", lambda h, Mx=Mx, nb=nb, tri_sm=tri_sm: h.tensor_tensor(
                                out=Mx[:, nb - 1, :], in0=Mx[:, nb - 1, :], in1=tri_sm, op=ALU.mult),
                                reads=[tMx, t_tri], writes=[tMx])
                        S.op("dve", lambda h, Mx=Mx, nb=nb: h.tensor_tensor(
                            out=Mx[:, 0:nb, :], in0=Mx[:, 0:nb, :],
                            in1=tm1[:, 0:nb, 3, hd:hd + 1].to_broadcast([64, nb, 64]), op=ALU.mult),
                            reads=[tMx, t_tm], writes=[tMx])
                    S.op("act", lambda h, n=n, psg1=psg1: h.activation(out=ns[:, 0:n], in_=psg1[:, 0:n], func=AF.Exp),
                         reads=[tg1], writes=[t_ns])
                    S.op("dve", lambda h, n=n: h.tensor_tensor(out=ns[:, 0:n], in0=ns[:, 0:n], in1=kn[:, 0:n], op=ALU.mult),
                         reads=[t_ns, t_post[1]], writes=[t_ns])
                    S.op("act", lambda h, n=n, psg2=psg2: h.activation(out=rsq[:, 0:n], in_=psg2[:, 0:n], func=AF.Exp),
                         reads=[tg2], writes=[t_rsq])
                    npb = nb - 1 if last else nb
                    S.op("pool", lambda h, npb=npb: h.tensor_copy(
                        out=dcs[:, 0:npb], in_=rsq[:, 0:npb * 64].rearrange("p (b k) -> p b k", k=64)[:, :, 63]),
                        reads=[t_rsq], writes=[t_dcs])
                    if last:
                        so = S0 - a
                        S.op("pool", lambda h, so=so: h.tensor_copy(
                            out=dcs[:, 8:24], in_=rsq[:, so:so + 64].rearrange("p (s k) -> p s k", k=4)[:, :, 3]),
                            reads=[t_rsq], writes=[t_dcs])
                    S.op("dve", lambda h, n=n: h.tensor_tensor(out=rsq[:, 0:n], in0=rsq[:, 0:n], in1=qn[:, 0:n], op=ALU.mult),
                         reads=[t_rsq, t_post[0], t_dcs], writes=[t_rsq])
                    for (srcg, dst, tdst, sc) in ((2, Vt, t_Vt, None), (1, Kh, t_Kh, 0)):
                        for b0_ in range(0, nb, 4):
                            nn = min(4, nb - b0_)
                            pst, pt_ = K.ps()
                            for bi in range(nn):
                                S.op("pe", lambda h, bi=bi, b0_=b0_, srcg=srcg, pst=pst: h.transpose(
                                    out=pst[0:64, bi * 128:(bi + 1) * 128],
                                    in_=post[:, srcg, (b0_ + bi) * 64:(b0_ + bi + 1) * 64], identity=ident[:]),
                                    reads=[t_post[srcg], t_id], writes=[pt_], inc=(bi == nn - 1))
                            pv = pst[0:64, 0:nn * 128].rearrange("p (b n) -> p b n", n=128)
                            if sc is None:
                                S.op("act", lambda h, pv=pv, b0_=b0_, nn=nn: h.copy(out=Vt[:, b0_:b0_ + nn, :], in_=pv),
                                     reads=[pt_], writes=[t_Vt])
                            else:
                                S.op("dve", lambda h, pv=pv, b0_=b0_, nn=nn: h.tensor_tensor(
                                    out=Kh[:, b0_:b0_ + nn, :], in0=pv,
                                    in1=kdb[:, b0_:b0_ + nn, 0:1].to_broadcast([64, nn, 128]), op=ALU.mult),
                                    reads=[pt_, t_kdb], writes=[t_Kh])
                                S.op("dve", lambda h, pv=pv, b0_=b0_, nn=nn: h.tensor_tensor(
                                    out=nPh[:, b0_:b0_ + nn, :], in0=pv,
                                    in1=kdb[:, b0_:b0_ + nn, 1:2].to_broadcast([64, nn, 128]), op=ALU.mult),
                                    reads=[pt_, t_kdb], writes=[t_nPh])
                    pkk, tkk = K.ps()
                    pqk, tqk = K.ps()
                    for bi in range(nb):
                        S.op("pe", lambda h, bi=bi, pkk=pkk: h.matmul(
                            pkk[0:64, bi * 64:(bi + 1) * 64], lhsT=kn[:, bi * 64:(bi + 1) * 64],
                            rhs=kn[:, bi * 64:(bi + 1) * 64], start=True, stop=True),
                            reads=[t_post[1]], writes=[tkk], inc=(bi == nb - 1))
                    for bi in range(nb):
                        S.op("pe", lambda h, bi=bi, pqk=pqk: h.matmul(
                            pqk[0:64, bi * 64:(bi + 1) * 64], lhsT=kn[:, bi * 64:(bi + 1) * 64],
                            rhs=qn[:, bi * 64:(bi + 1) * 64], start=True, stop=True),
                            reads=[t_post[1], t_post[0]], writes=[tqk], inc=(bi == nb - 1))
                    v3 = lambda p_, nb=nb: p_[0:64, 0:nb * 64].rearrange("p (b n) -> p b n", n=64)
                    al = tm2[:, 0:nb, 2, hd:hd + 1].to_broadcast([64, nb, 64])
                    S.op("dve", lambda h, pkk=pkk, nb=nb: h.tensor_tensor(out=Ank[:, 0:nb, :], in0=v3(pkk), in1=M1[:, 0:nb, :],
                                                                          op=ALU.mult),
                         reads=[tkk, t_M[0]], writes=[t_Ank])
                    S.op("dve", lambda h, nb=nb, al=al: h.scalar_tensor_tensor(
                        out=B0[:, 0:nb, :], in0=Ank[:, 0:nb, :], scalar=-1.0, in1=al, op0=ALU.mult, op1=ALU.mult),
                        reads=[t_Ank, t_tm], writes=[t_B0])
                    S.op("dve", lambda h, pqk=pqk, nb=nb: h.tensor_tensor(out=Ark[:, 0:nb, :], in0=v3(pqk), in1=M2[:, 0:nb, :],
                                                                          op=ALU.mult),
                         reads=[tqk, t_M[1]], writes=[t_Ark])
                    S.op("dve", lambda h, nb=nb, al=al: h.scalar_tensor_tensor(
                        out=nArp[:, 0:nb, :], in0=Ark[:, 0:nb, :], scalar=-1.0, in1=al, op0=ALU.mult, op1=ALU.mult),
                        reads=[t_Ark, t_tm], writes=[t_nArp])
                    TT, t_TT = solve_blocks(nb, B0, Ca, Cb, Ba, Bb, M1, M2, t_B0, t_C, t_B, t_M)
                    ybank = Y_BANKS[ybi % 2]; ybi += 1
                    ops = dict(ns=(ns, t_ns), rs=(rsq, t_rsq), V=(Vt, t_Vt), Kh=(Kh, t_Kh), nPh=(nPh, t_nPh),
                               Ank=(Ank, t_Ank), Ark=(Ark, t_Ark), nArp=(nArp, t_nArp), TT=(TT, t_TT),
                               Xs=(Xs, t_Xs), Es=(Es, t_Es), XT=(XT, t_XT), Km=(Km, t_Km), nPm=(nPm, t_nPm),
                               t_mask=t_tri)
                    for bi in range(npb):
                        seq_block(bi, 128, (0, 128), (bi * 64, bi * 64 + 64),
                                  [dict(Z=Z, tZ=t_Z, lo=0, hi=64, dc=dcs[:, bi:bi + 1], t_dc=t_dcs)],
                                  ops, ybank, (bi * 64, bi * 64 + 64), (0, 128))
                    if last:
                        S.dma("sp", gS_p[j, hd], Z, reads=[t_Z])
                        so = S0 - a
                        bi = nb - 1
                        for sg in range(4):
                            S.dma("sp", Zs, gdn_S_in[j, sg * 4:sg * 4 + 4, hd].rearrange("s k v -> k s v"),
                                  writes=t_Zs)
                            sts = [dict(Z=Zs[:, s_, :], tZ=t_Zs[s_], lo=(sg * 4 + s_) * 4, hi=(sg * 4 + s_) * 4 + 4,
                                        dc=dcs[:, 8 + sg * 4 + s_:9 + sg * 4 + s_], t_dc=t_dcs,
                                        mask=rmk[:, sg * 4 + s_:sg * 4 + s_ + 1]) for s_ in range(4)]
                            seq_block(bi, 128, (0, 128), (so, so + 64), sts, ops, ybank, (so, so + 64), (0, 128),
                                      tcols=(sg * 16, sg * 16 + 16))
                            S.dma("sp", gS_s[j, sg * 4:sg * 4 + 4, hd].rearrange("s k v -> k s v"), Zs, reads=t_Zs)
                    yb, tyb = ybank
                    S.op("act", lambda h, n=n, yb=yb: h.activation(out=sqb[:, 0:n], in_=yb[:, 0:n], func=AF.Square),
                         reads=[tyb], writes=[t_sqb])
                    pss, tps = K.ps()
                    S.op("pe", lambda h, n=n, pss=pss: h.matmul(pss[:, 0:n], lhsT=ones1, rhs=sqb[:, 0:n], start=True, stop=True),
                         reads=[t_sqb, t_ones1], writes=[tps])
                    S.op("act", lambda h, n=n, pss=pss: h.activation(out=rn[:, 0:n], in_=pss[:, 0:n], func=AF.Sqrt,
                                                                     bias=cst[:, 0:1], scale=1.0 / 128.0),
                         reads=[tps, t_cst], writes=[t_rn])
                    S.op("dve", lambda h, n=n: h.reciprocal(out=rn[:, 0:n], in_=rn[:, 0:n]), reads=[t_rn], writes=[t_rn])
                    S.op("dve", lambda h, n=n, yb=yb: h.scalar_tensor_tensor(
                        out=ns[:, 0:n], in0=yb[:, 0:n], scalar=pcol("gdn_nw", j), in1=rn[:, 0:n], op0=ALU.mult, op1=ALU.mult),
                        reads=[tyb, t_rn, t_pt], writes=[t_ns])
                    S.op("dve", lambda h, n=n: h.tensor_tensor(out=og[:, 0:n], in0=ns[:, 0:n], in1=zs[:, 0:n], op=ALU.mult),
                         reads=[t_ns, t_zs], writes=[t_og])
                    a2 = max(a, PAD)
                    tl = tile_of(a2, b)
                    for oc in range(8):
                        pso, to = K.ps()
                        S.op("pe", lambda h, oc=oc, n=n, pso=pso: h.matmul(
                            pso[:, 0:n], lhsT=wo[:, oc * 128:(oc + 1) * 128], rhs=og[:, 0:n], start=True, stop=True),
                            reads=[t_wo, t_og], writes=[to])
                        S.op("dve", lambda h, oc=oc, a2=a2, a=a, b=b, pso=pso: h.tensor_tensor(
                            out=xT[:, oc, a2:b], in0=xT[:, oc, a2:b], in1=pso[:, a2 - a:b - a], op=ALU.add),
                            reads=[to] + tl, writes=tl)
                for ti_, (a_, b_) in enumerate(TILES):
                    do_tile(ti_, a_, b_)
                pst, pt_ = K.ps()
                for g in range(3):
                    S.op("pe", lambda h, g=g, pst=pst: h.transpose(
                        out=pst[0:51, g * 128:(g + 1) * 128], in_=stg[:, g, :], identity=ident[:]),
                        reads=[t_stg, t_id], writes=[pt_], inc=(g == 2))
                S.op("act", lambda h, pst=pst: h.copy(out=srow, in_=pst[0:51, 0:384]), reads=[pt_], writes=[t_srow])
                for g in range(3):
                    cs = slice(g * 1024 + hd * 128, g * 1024 + (hd + 1) * 128)
                    S.dma("sp", gC_s[j].rearrange("s k n -> (s k) n")[:, cs], srow[0:48, g * 128:(g + 1) * 128],
                          reads=[t_srow])
                    S.dma("sp", gC_p[j][:, cs], srow[48:51, g * 128:(g + 1) * 128], reads=[t_srow])
            for hd_ in range(cfg.get('gdn_heads', 8)):
                do_head(hd_)
            S.barrier()

        def seq_block_pair(ci, c0, states, ops, ybank, ycol0, tcols=(0, 64)):
            (ns, t_ns), (rs_, t_rs_) = ops["ns"], ops["rs"]
            (V, t_V), (Kh, t_Kh), (nPh, t_nPh) = ops["V"], ops["Kh"], ops["nPh"]
            (Ank, t_Ank), (Ark, t_Ark), (nArp, t_nArp), (TT, t_TT) = ops["Ank"], ops["Ark"], ops["nArp"], ops["TT"]
            (Xs, t_Xs), (Es, t_Es) = ops["Xs"], ops["Es"]
            H = lambda hs: slice(hs * 64, hs * 64 + 64)
            single = len(states) == 1
            psx, tpx = K.ps()
            if single:
                stt = states[0]
                S.op("pe", lambda h: h.matmul(psx[0:64, 0:128], lhsT=ns[:, c0:c0 + 64], rhs=stt["Z"], start=True, stop=False),
                     reads=[t_ns] + stt["tZ"], writes=[tpx], inc=False)
                for hs in range(2):
                    S.op("pe", lambda h, hs=hs: h.matmul(psx[0:64, H(hs)], lhsT=Ank[:, 2 * ci + hs, :], rhs=V[:, ci, H(hs)],
                                                         start=False, stop=(hs == 1)), reads=[t_Ank, t_V], writes=[tpx], inc=(hs == 1))
                S.op("act", lambda h: h.copy(out=Xs, in_=psx[0:64, 0:128]), reads=[tpx], writes=[t_Xs])
            else:
                for hs in range(2):
                    S.op("pe", lambda h, hs=hs: h.matmul(psx[H(hs), 0:64], lhsT=V[:, ci, H(hs)], rhs=Ank[:, 2 * ci + hs, :],
                                                         start=True, stop=False), reads=[t_Ank, t_V], writes=[tpx], inc=False)
                for si, stt in enumerate(states):
                    S.op("pe", lambda h, stt=stt, si=si: h.matmul(
                        psx[:, stt["lo"]:stt["hi"]], lhsT=stt["Z"], rhs=ns[:, c0 + stt["lo"]:c0 + stt["hi"]],
                        start=False, stop=(si == len(states) - 1)), reads=[t_ns] + stt["tZ"], writes=[tpx],
                        inc=(si == len(states) - 1))
                XT, t_XT = ops["XT"]
                S.op("act", lambda h: h.copy(out=XT, in_=psx[:, 0:64]), reads=[tpx], writes=[t_XT])
                psx2, tpx2 = K.ps()
                S.op("pe", lambda h: h.transpose(out=psx2[0:64, 0:128], in_=XT, identity=ident[:]),
                     reads=[t_XT, t_id], writes=[tpx2])
                S.op("act", lambda h: h.copy(out=Xs, in_=psx2[0:64, 0:128]), reads=[tpx2], writes=[t_Xs])
            pse, tpe = K.ps()
            for hs in range(2):
                S.op("pe", lambda h, hs=hs: h.matmul(pse[0:64, H(hs)], lhsT=TT[:, 2 * ci + hs, :], rhs=Xs[:, H(hs)],
                                                     start=True, stop=True), reads=[t_TT, t_Xs], writes=[tpe], inc=(hs == 1))
            S.op("dve", lambda h: h.tensor_copy(out=Es, in_=pse[0:64, 0:128]), reads=[tpe], writes=[t_Es])
            yb, tyb = ybank
            tl0, tl1 = tcols
            for hs in range(2):
                S.op("pe", lambda h, hs=hs: h.matmul(yb[H(hs), ycol0 + tl0:ycol0 + tl1], lhsT=V[:, ci, H(hs)],
                                                     rhs=Ark[:, 2 * ci + hs, tl0:tl1], start=True, stop=False),
                     reads=[t_V, t_Ark], writes=[tyb], inc=False)
            for stt in states:
                lo, hi = max(stt["lo"], tl0), min(stt["hi"], tl1)
                S.op("pe", lambda h, stt=stt, lo=lo, hi=hi: h.matmul(
                    yb[:, ycol0 + lo:ycol0 + hi], lhsT=stt["Z"], rhs=rs_[:, c0 + lo:c0 + hi], start=False, stop=False),
                    reads=[t_rs_] + stt["tZ"], writes=[tyb], inc=False)
            for hs in range(2):
                S.op("pe", lambda h, hs=hs: h.matmul(yb[H(hs), ycol0 + tl0:ycol0 + tl1], lhsT=Es[:, H(hs)],
                                                     rhs=nArp[:, 2 * ci + hs, tl0:tl1], start=False, stop=(hs == 1)),
                     reads=[t_Es, t_nArp], writes=[tyb], inc=(hs == 1))
            for stt in states:
                psz, tpz = K.ps()
                if stt.get("mask") is not None:
                    Km, t_Km = ops["Km"]; nPm, t_nPm = ops["nPm"]
                    S.op("pool", lambda h, stt=stt: h.tensor_scalar(out=Km, in0=Kh[:, ci, :], scalar1=stt["mask"], scalar2=None,
                                                                   op0=ALU.mult), reads=[t_Kh, ops["t_mask"]], writes=[t_Km])
                    S.op("pool", lambda h, stt=stt: h.tensor_scalar(out=nPm, in0=nPh[:, ci, :], scalar1=stt["mask"], scalar2=None,
                                                                   op0=ALU.mult), reads=[t_nPh, ops["t_mask"]], writes=[t_nPm])
                    lk, tlk, lp, tlp = Km, t_Km, nPm, t_nPm
                else:
                    lk, tlk, lp, tlp = Kh[:, ci, :], t_Kh, nPh[:, ci, :], t_nPh
                for hs in range(2):
                    S.op("pe", lambda h, psz=psz, lk=lk, hs=hs: h.matmul(psz[H(hs), H(hs)], lhsT=lk[:, H(hs)], rhs=V[:, ci, H(hs)],
                                                                         start=True, stop=False),
                         reads=[tlk, t_V], writes=[tpz], inc=False)
                    S.op("pe", lambda h, psz=psz, lp=lp, hs=hs: h.matmul(psz[H(hs), H(hs)], lhsT=lp[:, H(hs)], rhs=Es[:, H(hs)],
                                                                         start=False, stop=True),
                         reads=[tlp, t_Es], writes=[tpz], inc=(hs == 1))
                for hs in range(2):
                    S.op("dve", lambda h, stt=stt, psz=psz, hs=hs: h.scalar_tensor_tensor(
                        out=stt["Z"][H(hs), H(hs)], in0=stt["Z"][H(hs), H(hs)], scalar=stt["dc"][H(hs), :],
                        in1=psz[H(hs), H(hs)], op0=ALU.mult, op1=ALU.add),
                        reads=stt["tZ"] + [stt["t_dc"], tpz], writes=stt["tZ"])

        RT = [(a_, min(a_ + 256, T)) for a_ in range(0, T, 256)]

        def rwkv_layer(l):
            j = l // 2
            o = [0]

            def A(n):
                r = arena[:, o[0]:o[0] + n]; o[0] += n
                assert o[0] <= ARENA, o[0]
                return r
            TX = 2178
            hnx = A(8 * TX // 2).bitcast(BF16).rearrange("p (c n) -> p c n", c=8); t_hn = Trk()
            prv = A(256).bitcast(BF16).rearrange("p (c n) -> p c n", c=8); t_prv = Trk()
            L1 = A(1088).bitcast(BF16); L2 = A(1088).bitcast(BF16); L3 = A(1088).bitcast(BF16)
            t_L = [Trk(), Trk(), Trk()]
            rmq = A(256); rmq_l = A(128); t_rmq = Trk()
            blkb = A(64).bitcast(BF16); blkf = A(128); t_blk = Trk()
            omm = A(112); t_omm = Trk()
            Wab = A(3072).bitcast(BF16).rearrange("p (s c g n) -> p s c g n", s=2, c=8, g=3); t_Wab = Trk()
            wst = [A(1024).rearrange("p (c n) -> p c n", c=8) for _ in range(2)]; t_wst = [Trk(), Trk()]
            W2s = A(192).bitcast(BF16).rearrange("p (g n) -> p g n", g=3); t_W2s = Trk()
            wo = A(512).bitcast(BF16); t_wo = Trk()
            names = ["r", "k", "v", "lw", "G", "a", "g", "eG", "enG", "kd", "kk", "kf", "p", "rk", "ph", "t", "t2"]
            ball = A(256 * len(names))
            Bf = {nm: ball[:, i * 256:(i + 1) * 256] for i, nm in enumerate(names)}
            tb = {nm: Trk() for nm in names}
            tb["t2"] = tb["t"]
            Sst = ball[0:64, 15 * 256:17 * 256].rearrange("p (s n) -> p s n", s=4); t_Sst = tb["t"]
            SstO = ball[:, 15 * 256:17 * 256].rearrange("p (s n) -> p s n", s=4)
            rn_ = ball[:, 0:512]
            yg = A(128).bitcast(BF16); t_yg = Trk()
            sqb = A(128).bitcast(BF16); t_sqb = Trk()
            scr_ = A(2304)
            sq_scr = scr_[:, 0:2048].bitcast(BF16).rearrange("p (c n) -> p c n", c=8)
            W1s = scr_[:, 0:2048].bitcast(BF16).rearrange("p (s c n) -> p s c n", s=2, c=8)
            t_scr = Trk()
            shr = scr_[0:17, 0:1024]
            xs17 = scr_[:, 1024:1024 + 136].rearrange("p (c n) -> p c n", c=8)
            sq17 = scr_[:, 1200:1200 + 68].bitcast(BF16).rearrange("p (c n) -> p c n", c=8)
            Vt = scr_[0:64, 0:512].rearrange("p (b n) -> p b n", n=128); t_Vt = t_scr
            Kh = scr_[0:64, 512:1024].rearrange("p (b n) -> p b n", n=128); t_Kh = t_scr
            nPh = scr_[0:64, 1024:1536].rearrange("p (b n) -> p b n", n=128); t_nPh = t_scr
            v64 = lambda: A(512)[0:64, :].rearrange("p (b n) -> p b n", b=8)
            Pa = v64(); Pb = v64(); t_P = [Trk(), Trk()]
            Ank = v64(); B0 = v64(); Ark = v64(); nArp = v64()
            t_Ank, t_B0, t_Ark, t_nArp = Trk(), Trk(), Trk(), Trk()
            Ca = v64(); Cb = v64(); Ba = v64(); Bb = v64()
            t_C = [Trk(), Trk()]; t_B = [Trk(), Trk()]
            Xs = A(128)[0:64, :]; Es = A(128)[0:64, :]; XT = A(64); t_Xs, t_Es, t_XT = Trk(), Trk(), Trk()
            Zp = A(128); t_Zp = [Trk()]
            Zs = A(512).rearrange("p (s n) -> p s n", s=4); t_Zs = [[Trk()] for _ in range(4)]
            Km = A(128)[0:64, :]; nPm = A(128)[0:64, :]; t_Km, t_nPm = Trk(), Trk()
            dcs = A(24); t_dcs = Trk()
            MIX = {"r": 0, "w": 1, "k": 2, "v": 3, "a": 4, "g": 5}
            mixc = lambda m, c: pcol("rw_mix", (j * 6 + MIX[m]) * 8 + c)
            ommc = lambda m, c: omm[:, MIX[m] * 8 + c:MIX[m] * 8 + c + 1]
            PV = lambda name, c: pcol(name, j * 8 + c)

            S.op("pool", lambda h: h.memset(blkb, 0.0), writes=[t_blk])
            S.op("pool", lambda h: h.memset(blkf, 0.0), writes=[t_blk])
            for hs in range(2):
                sl = slice(hs * 64, hs * 64 + 64)
                S.op("pool", lambda h, sl=sl: h.memset(blkb[sl, sl], 1.0), writes=[t_blk])
                S.op("pool", lambda h, sl=sl: h.memset(blkf[sl, sl], 1.0), writes=[t_blk])
            S.op("pool", lambda h: h.memset(rmq, 1.0), writes=[t_rmq])
            S.op("pool", lambda h: h.memset(rmq.rearrange("p (c k) -> p c k", k=64)[:, :, 0:1], 0.0), writes=[t_rmq])
            S.op("pool", lambda h: h.memset(rmq_l, 1.0), writes=[t_rmq])
            S.op("pool", lambda h: h.memset(rmq_l[:, 0:1], 0.0), writes=[t_rmq])
            S.op("pool", lambda h: h.memset(rmq_l[:, 64:128].rearrange("p (c k) -> p c k", k=4)[:, :, 0:1], 0.0),
                 writes=[t_rmq])
            S.op("pool", lambda h: h.memset(hnx[:, :, 0:2], 0.0), writes=[t_hn])
            mo = pt_off["rw_mix"] + j * 48
            S.op("dve", lambda h: h.tensor_scalar(out=omm[:, 0:48], in0=ptab[:, mo:mo + 48], scalar1=-1.0, scalar2=1.0,
                                                  op0=ALU.mult, op1=ALU.add), reads=[t_pt], writes=[t_omm])
            ko = pt_off["rw_ka"] + j * 8
            S.op("dve", lambda h: h.tensor_scalar(out=omm[:, 48:56], in0=ptab[:, ko:ko + 8], scalar1=-1.0, scalar2=1.0,
                                                  op0=ALU.mult, op1=ALU.add), reads=[t_pt], writes=[t_omm])
            if cfg.get('rw_stop') == 1:
                S.barrier(); return
            tlast = tile_of(S0 - 1, T)
            S.op("dve", lambda h: h.tensor_copy(out=xs17[:, :, 0:1], in_=xT[:, :, S0 - 1:S0]), reads=tlast, writes=[t_scr])
            S.op("dve", lambda h: h.tensor_copy(out=xs17[:, :, 1:17],
                                                in_=xT[:, :, S0:T].rearrange("p c (s k) -> p c s k", k=4)[:, :, :, 3]),
                 reads=tlast, writes=[t_scr])
            S.op("act", lambda h: h.activation(out=sq17, in_=xs17, func=AF.Square), reads=[t_scr], writes=[t_scr])
            pst, pt_ = K.ps()
            for c in range(8):
                S.op("pe", lambda h, c=c, pst=pst: h.matmul(pst[:, 0:17], lhsT=onesb[:], rhs=sq17[:, c, :],
                                                            start=(c == 0), stop=(c == 7)),
                     reads=[t_scr, t_ones], writes=[pt_], inc=(c == 7))
            S.op("act", lambda h, pst=pst: h.activation(out=rn_[:, 0:17], in_=pst[:, 0:17], func=AF.Sqrt, bias=cst[:, 0:1],
                                                        scale=1.0), reads=[pt_, t_cst], writes=[tb["r"]])
            S.op("dve", lambda h: h.reciprocal(out=rn_[:, 0:17], in_=rn_[:, 0:17]), reads=[tb["r"]], writes=[tb["r"]])
            for c in range(8):
                S.op("dve", lambda h, c=c: h.scalar_tensor_tensor(
                    out=xs17[:, c, :], in0=xs17[:, c, :], scalar=pcol("norm_mix", l * 8 + c), in1=rn_[:, 0:17],
                    op0=ALU.mult, op1=ALU.mult), reads=[t_scr, tb["r"], t_pt], writes=[t_scr])
            for half in range(2):
                pst, pt_ = K.ps()
                for cc in range(4):
                    S.op("pe", lambda h, cc=cc, half=half, pst=pst: h.transpose(
                        out=pst[0:17, cc * 128:(cc + 1) * 128], in_=xs17[:, half * 4 + cc, :], identity=ident[:]),
                        reads=[t_scr, t_id], writes=[pt_], inc=(cc == 3))
                S.op("act", lambda h, half=half, pst=pst: h.copy(out=shr[:, half * 512:(half + 1) * 512], in_=pst[0:17, :]),
                     reads=[pt_], writes=[t_scr])
            S.dma("sp", rSh_p[j:j + 1, :], shr[0:1, :], reads=[t_scr])
            S.dma("sp", rSh_s[j], shr[1:17, :], reads=[t_scr])
            if cfg.get('rw_stop') == 2:
                S.barrier(); return
            S.dma("sp", shr[0:16, :], rwkv_shift_in[j], reads=[t_scr], writes=[t_scr])
            pst, pt_ = K.ps()
            for c in range(8):
                S.op("pe", lambda h, c=c, pst=pst: h.transpose(out=pst[:, c * 16:(c + 1) * 16], in_=shr[0:16, c * 128:(c + 1) * 128],
                                                               identity=ident[0:16, 0:16]),
                     reads=[t_scr, t_id], writes=[pt_], inc=(c == 7))
            S.op("dve", lambda h, pst=pst: h.tensor_copy(
                out=prv.rearrange("p c (s k) -> p c s k", k=4)[:, :, :, 0],
                in_=pst[:, 0:128].rearrange("p (c s) -> p c s", c=8)), reads=[pt_], writes=[t_prv])
            rmsnorm(hnx, 1, "norm_mix", l, 0, T, sq_scr, t_scr, rn_, tb["r"], t_hn)
            S.op("dve", lambda h: h.tensor_copy(
                out=prv.rearrange("p c (s k) -> p c s k", k=4)[:, :, :, 1:4],
                in_=hnx[:, :, 1 + S0:1 + T].rearrange("p c (s k) -> p c s k", k=4)[:, :, :, 0:3]),
                reads=[t_hn], writes=[t_prv])

            if cfg.get('rw_stop') == 3:
                S.barrier(); return
            def mm_pair(ps_ap, n, la, lb, a, b, rd, wr, last_stop=True):
                for c in range(8):
                    S.op("pe", lambda h, c=c: h.matmul(ps_ap[:, 0:n], lhsT=la(c), rhs=hnx[:, c, 1 + a:1 + b],
                                                       start=(c == 0), stop=False), reads=rd + [t_hn], writes=wr, inc=False)
                if b <= S0:
                    for c in range(8):
                        S.op("pe", lambda h, c=c: h.matmul(ps_ap[:, 0:n], lhsT=lb(c), rhs=hnx[:, c, a:b],
                                                           start=False, stop=(c == 7)), reads=rd + [t_hn], writes=wr,
                             inc=(c == 7))
                else:
                    npz = S0 - a
                    for c in range(8):
                        S.op("pe", lambda h, c=c: h.matmul(ps_ap[:, 0:npz], lhsT=lb(c), rhs=hnx[:, c, a:a + npz],
                                                           start=False, stop=False), reads=rd + [t_hn], writes=wr, inc=False)
                    for c in range(8):
                        S.op("pe", lambda h, c=c: h.matmul(ps_ap[:, npz:n], lhsT=lb(c), rhs=prv[:, c, :],
                                                           start=False, stop=(c == 7)), reads=rd + [t_prv], writes=wr,
                             inc=(c == 7))

            def load_scaled(src_ap, ncols, dst_a, dst_b, m, wi):
                ws = wst[wi % 2]; tws = t_wst[wi % 2]
                S.dma("sp", ws[:, :, 0:ncols], src_ap.rearrange("(c p) n -> p c n", p=128), writes=[tws])
                for c in range(8):
                    S.op("act", lambda h, c=c: h.activation(out=dst_a(c), in_=ws[:, c, 0:ncols], func=AF.Copy,
                                                            scale=ommc(m, c)), reads=[tws, t_omm], writes=[t_scr])
                    S.op("pool", lambda h, c=c: h.tensor_scalar(out=dst_b(c), in0=ws[:, c, 0:ncols], scalar1=mixc(m, c),
                                                                scalar2=None, op0=ALU.mult), reads=[tws, t_pt], writes=[t_scr])
            stage1 = [
                ([(rwkv_w1[j], 64, 0, "w"), (rwkv_a1[j], 64, 64, "a")], 128, [(0, 64, AF.Tanh), (64, 128, AF.Copy)], 0),
                ([(rwkv_g1[j][:, 0:128], 128, 0, "g")], 128, [(0, 128, AF.Sigmoid)], 1),
                ([(rwkv_g1[j][:, 128:160], 32, 0, "g")] + ([(rwkv_v1[0], 32, 32, "v")] if j == 1 else []),
                 64 if j == 1 else 32, [(0, 32, AF.Sigmoid)] + ([(32, 64, AF.Copy)] if j == 1 else []), 2),
            ]
            wi = 0
            for (srcs, M, acts, li) in stage1:
                for (src, nc_, co, m) in srcs:
                    load_scaled(src, nc_, lambda c, co=co, nc_=nc_: W1s[:, 0, c, co:co + nc_],
                                lambda c, co=co, nc_=nc_: W1s[:, 1, c, co:co + nc_], m, wi)
                    wi += 1
                Lx = (L1, L2, L3)[li]
                for (a, b) in TILES:
                    n = b - a
                    pst, pt_ = K.ps()
                    mm_pair(pst[0:M], n, lambda c, M=M: W1s[:, 0, c, 0:M], lambda c, M=M: W1s[:, 1, c, 0:M], a, b,
                            [t_scr], [pt_])
                    for (lo, hi, fn) in acts:
                        S.op("act", lambda h, lo=lo, hi=hi, fn=fn, pst=pst, a=a, b=b, n=n, Lx=Lx: h.activation(
                            out=Lx[lo:hi, a:b], in_=pst[lo:hi, 0:n], func=fn), reads=[pt_], writes=[t_L[li]])
            if cfg.get('rw_stop') == 4:
                S.barrier(); return
            ybi = [0]

            def do_pair(jp):
                cs = slice(jp * 128, (jp + 1) * 128)
                for gi, (W, m) in enumerate(((rwkv_wr, "r"), (rwkv_wk, "k"), (rwkv_wv, "v"))):
                    ws = wst[gi % 2]; tws = t_wst[gi % 2]
                    S.dma("sp", ws, W[j].rearrange("(c p) n -> p c n", p=128)[:, :, cs], writes=[tws])
                    for c in range(8):
                        S.op("act", lambda h, c=c, gi=gi, m=m, ws=ws: h.activation(
                            out=Wab[:, 0, c, gi, :], in_=ws[:, c, :], func=AF.Copy, scale=ommc(m, c)),
                            reads=[tws, t_omm], writes=[t_Wab])
                        S.op("pool", lambda h, c=c, gi=gi, m=m, ws=ws: h.tensor_scalar(
                            out=Wab[:, 1, c, gi, :], in0=ws[:, c, :], scalar1=mixc(m, c), scalar2=None, op0=ALU.mult),
                            reads=[tws, t_pt], writes=[t_Wab])
                S.dma("pool", W2s[0:64, 0, :], rwkv_w2[j][:, cs], writes=[t_W2s])
                S.dma("pool", W2s[64:128, 0, :], rwkv_a2[j][:, cs], writes=[t_W2s])
                S.dma("pool", W2s[:, 1, :], rwkv_g2[j][0:128, cs], writes=[t_W2s])
                S.dma("pool", W2s[0:32, 2, :], rwkv_g2[j][128:160, cs], writes=[t_W2s])
                if j == 1:
                    S.dma("pool", W2s[32:64, 2, :], rwkv_v2[0][:, cs], writes=[t_W2s])
                S.dma("pool", wo, rwkv_wo[j][cs, :], writes=[t_wo])
                S.op("dve", lambda h: h.memset(Zp, 0.0), writes=t_Zp)

                def do_tile(ti, a, b):
                    n = b - a
                    nch = n // 64
                    nblk = 2 * nch
                    last = b > S0
                    npc = nch - 1 if last else nch
                    so = S0 - a
                    B_ = lambda nm: Bf[nm][:, 0:n]
                    for gi, nm in enumerate(("r", "k", "v")):
                        pst, pt_ = K.ps()
                        mm_pair(pst, n, lambda c, gi=gi: Wab[:, 0, c, gi, :], lambda c, gi=gi: Wab[:, 1, c, gi, :], a, b,
                                [t_Wab], [pt_])
                        S.op("act", lambda h, nm=nm, pst=pst: h.copy(out=B_(nm), in_=pst[:, 0:n]), reads=[pt_], writes=[tb[nm]])
                    if cfg.get('rp_stop') == 1:
                        return
                    pst, pt_ = K.ps()
                    S.op("pe", lambda h, pst=pst: h.matmul(pst[:, 0:n], lhsT=W2s[0:64, 0, :], rhs=L1[0:64, a:b], start=True, stop=True),
                         reads=[t_W2s, t_L[0]], writes=[pt_])
                    S.op("act", lambda h, pst=pst: h.activation(out=B_("lw"), in_=pst[:, 0:n], func=AF.Sigmoid,
                                                                bias=PV("rw_w0", jp), scale=1.0), reads=[pt_, t_pt], writes=[tb["lw"]])
                    S.op("dve", lambda h: h.tensor_scalar(out=B_("lw"), in0=B_("lw"), scalar1=-0.6065306597126334,
                                                          scalar2=None, op0=ALU.mult), reads=[tb["lw"]], writes=[tb["lw"]])
                    pst, pt_ = K.ps()
                    S.op("pe", lambda h, pst=pst: h.matmul(pst[:, 0:n], lhsT=W2s[64:128, 0, :], rhs=L1[64:128, a:b], start=True, stop=True),
                         reads=[t_W2s, t_L[0]], writes=[pt_])
                    S.op("act", lambda h, pst=pst: h.activation(out=B_("a"), in_=pst[:, 0:n], func=AF.Sigmoid,
                                                                bias=PV("rw_a0", jp), scale=1.0), reads=[pt_, t_pt], writes=[tb["a"]])
                    pst, pt_ = K.ps()
                    S.op("pe", lambda h, pst=pst: h.matmul(pst[:, 0:n], lhsT=W2s[:, 1, :], rhs=L2[:, a:b], start=True, stop=False),
                         reads=[t_W2s, t_L[1]], writes=[pt_], inc=False)
                    S.op("pe", lambda h, pst=pst: h.matmul(pst[:, 0:n], lhsT=W2s[0:32, 2, :], rhs=L3[0:32, a:b], start=False, stop=True),
                         reads=[t_W2s, t_L[2]], writes=[pt_])
                    S.op("act", lambda h, pst=pst: h.copy(out=B_("g"), in_=pst[:, 0:n]), reads=[pt_], writes=[tb["g"]])
                    if j == 0:
                        S.dma("sp", vf_d[jp, :, a:b], B_("v"), reads=[tb["v"]])
                    else:
                        pst, pt_ = K.ps()
                        S.op("pe", lambda h, pst=pst: h.matmul(pst[:, 0:n], lhsT=W2s[32:64, 2, :], rhs=L3[32:64, a:b],
                                                               start=True, stop=True), reads=[t_W2s, t_L[2]], writes=[pt_])
                        S.op("act", lambda h, pst=pst: h.activation(out=B_("t"), in_=pst[:, 0:n], func=AF.Sigmoid,
                                                                    bias=pcol("rw_v0", jp), scale=1.0),
                             reads=[pt_, t_pt], writes=[tb["t"]])
                        S.dma("sp", B_("t2"), vf_d[jp, :, a:b], writes=[tb["t"]])
                        S.op("dve", lambda h: h.tensor_tensor(out=B_("t2"), in0=B_("t2"), in1=B_("v"), op=ALU.subtract),
                             reads=[tb["t"], tb["v"]], writes=[tb["t"]])
                        S.op("dve", lambda h: h.tensor_tensor(out=B_("t2"), in0=B_("t2"), in1=B_("t"), op=ALU.mult),
                             reads=[tb["t"]], writes=[tb["t"]])
                        S.op("dve", lambda h: h.tensor_tensor(out=B_("v"), in0=B_("v"), in1=B_("t2"), op=ALU.add),
                             reads=[tb["t"], tb["v"]], writes=[tb["v"]])
                    if cfg.get('rp_stop') == 2:
                        return
                    rm_ap = rmq_l[:, 0:n] if last else rmq[:, 0:n]
                    S.op("dve", lambda h: h.tensor_tensor_scan(out=B_("G"), data0=rm_ap, data1=B_("lw"), initial=0.0,
                                                               op0=ALU.mult, op1=ALU.add),
                         reads=[t_rmq, tb["lw"]], writes=[tb["G"]])
                    S.op("dve", lambda h: h.tensor_tensor(out=B_("lw"), in0=B_("G"), in1=B_("lw"), op=ALU.subtract),
                         reads=[tb["G"], tb["lw"]], writes=[tb["lw"]])
                    S.op("act", lambda h: h.activation(out=B_("eG"), in_=B_("G"), func=AF.Exp), reads=[tb["G"]], writes=[tb["eG"]])
                    S.op("act", lambda h: h.activation(out=B_("enG"), in_=B_("G"), func=AF.Exp, scale=-1.0),
                         reads=[tb["G"]], writes=[tb["enG"]])
                    if npc > 0:
                        S.op("pool", lambda h: h.tensor_copy(
                            out=dcs[:, 0:npc], in_=Bf["eG"][:, 0:npc * 64].rearrange("p (b k) -> p b k", k=64)[:, :, 63]),
                            reads=[tb["eG"]], writes=[t_dcs])
                        S.op("dve", lambda h: h.tensor_tensor(
                            out=Bf["kd"][:, 0:npc * 64].rearrange("p (b k) -> p b k", k=64),
                            in0=Bf["G"][:, 0:npc * 64].rearrange("p (b k) -> p b k", k=64)[:, :, 63:64].to_broadcast([128, npc, 64]),
                            in1=Bf["G"][:, 0:npc * 64].rearrange("p (b k) -> p b k", k=64), op=ALU.subtract),
                            reads=[tb["G"]], writes=[tb["kd"]])
                    if last:
                        S.op("pool", lambda h: h.tensor_copy(
                            out=dcs[:, 8:24], in_=Bf["eG"][:, so:so + 64].rearrange("p (s k) -> p s k", k=4)[:, :, 3]),
                            reads=[tb["eG"]], writes=[t_dcs])
                        S.op("dve", lambda h: h.tensor_tensor(
                            out=Bf["kd"][:, so:so + 64].rearrange("p (s k) -> p s k", k=4),
                            in0=Bf["G"][:, so:so + 64].rearrange("p (s k) -> p s k", k=4)[:, :, 3:4].to_broadcast([128, 16, 4]),
                            in1=Bf["G"][:, so:so + 64].rearrange("p (s k) -> p s k", k=4), op=ALU.subtract),
                            reads=[tb["G"]], writes=[tb["kd"]])
                    S.op("act", lambda h: h.activation(out=B_("kd"), in_=B_("kd"), func=AF.Exp), reads=[tb["kd"]], writes=[tb["kd"]])
                    S.op("dve", lambda h: h.tensor_scalar(out=B_("kk"), in0=B_("k"), scalar1=PV("rw_kk", jp), scalar2=None,
                                                          op0=ALU.mult), reads=[tb["k"], t_pt], writes=[tb["kk"]])
                    S.op("act", lambda h: h.activation(out=sqb[:, 0:n], in_=B_("kk"), func=AF.Square), reads=[tb["kk"]], writes=[t_sqb])
                    pst, pt_ = K.ps()
                    S.op("pe", lambda h, pst=pst: h.matmul(pst[:, 0:n], lhsT=blkb, rhs=sqb[:, 0:n], start=True, stop=True),
                         reads=[t_blk, t_sqb], writes=[pt_])
                    S.op("act", lambda h, pst=pst: h.activation(out=B_("t"), in_=pst[:, 0:n], func=AF.Sqrt, bias=cst[:, 0:1], scale=1.0),
                         reads=[pt_, t_cst], writes=[tb["t"]])
                    S.op("dve", lambda h: h.reciprocal(out=B_("t"), in_=B_("t")), reads=[tb["t"]], writes=[tb["t"]])
                    S.op("dve", lambda h: h.tensor_tensor(out=B_("kk"), in0=B_("kk"), in1=B_("t"), op=ALU.mult),
                         reads=[tb["kk"], tb["t"]], writes=[tb["kk"]])
                    S.op("dve", lambda h: h.tensor_scalar(out=B_("t"), in0=B_("a"), scalar1=PV("rw_ka", jp),
                                                          scalar2=omm[:, 48 + jp:49 + jp], op0=ALU.mult, op1=ALU.add),
                         reads=[tb["a"], t_pt, t_omm], writes=[tb["t"]])
                    S.op("dve", lambda h: h.tensor_tensor(out=B_("kf"), in0=B_("k"), in1=B_("t"), op=ALU.mult),
                         reads=[tb["k"], tb["t"]], writes=[tb["kf"]])
                    S.op("dve", lambda h: h.tensor_tensor(out=B_("p"), in0=B_("kk"), in1=B_("a"), op=ALU.mult),
                         reads=[tb["kk"], tb["a"]], writes=[tb["p"]])
                    S.op("dve", lambda h: h.scalar_tensor_tensor(out=B_("rk"), in0=B_("r"), scalar=PV("rw_rk", jp), in1=B_("kf"),
                                                                 op0=ALU.mult, op1=ALU.mult),
                         reads=[tb["r"], tb["kf"], t_pt], writes=[tb["rk"]])
                    S.op("act", lambda h: h.activation(out=B_("t"), in_=B_("lw"), func=AF.Exp), reads=[tb["lw"]], writes=[tb["t"]])
                    S.op("dve", lambda h: h.tensor_tensor(out=B_("kk"), in0=B_("kk"), in1=B_("t"), op=ALU.mult),
                         reads=[tb["kk"], tb["t"]], writes=[tb["kk"]])
                    S.op("dve", lambda h: h.scalar_tensor_tensor(out=B_("ph"), in0=B_("p"), scalar=-1.0, in1=B_("kd"),
                                                                 op0=ALU.mult, op1=ALU.mult),
                         reads=[tb["p"], tb["kd"]], writes=[tb["ph"]])
                    S.op("dve", lambda h: h.tensor_tensor(out=B_("p"), in0=B_("p"), in1=B_("enG"), op=ALU.mult),
                         reads=[tb["p"], tb["enG"]], writes=[tb["p"]])
                    S.op("dve", lambda h: h.tensor_tensor(out=B_("kd"), in0=B_("kf"), in1=B_("kd"), op=ALU.mult),
                         reads=[tb["kf"], tb["kd"]], writes=[tb["kd"]])
                    S.op("dve", lambda h: h.tensor_tensor(out=B_("enG"), in0=B_("kf"), in1=B_("enG"), op=ALU.mult),
                         reads=[tb["kf"], tb["enG"]], writes=[tb["enG"]])
                    S.op("dve", lambda h: h.tensor_tensor(out=B_("eG"), in0=B_("r"), in1=B_("eG"), op=ALU.mult),
                         reads=[tb["r"], tb["eG"], t_dcs], writes=[tb["eG"]])
                    nT, pT, kT, rT = Bf["kk"], Bf["p"], Bf["enG"], Bf["eG"]
                    t_nT, t_pT, t_kT, t_rT = tb["kk"], tb["p"], tb["enG"], tb["eG"]
                    if cfg.get('rp_stop') == 3:
                        return
                    for (nm, dst) in (("v", Vt), ("kd", Kh), ("ph", nPh)):
                        pst, pt_ = K.ps()
                        for ci in range(nch):
                            S.op("pe", lambda h, ci=ci, nm=nm, pst=pst: h.transpose(
                                out=pst[0:64, ci * 128:(ci + 1) * 128], in_=Bf[nm][:, ci * 64:(ci + 1) * 64], identity=ident[:]),
                                reads=[tb[nm], t_id], writes=[pt_], inc=(ci == nch - 1))
                        S.op("act", lambda h, pst=pst, dst=dst: h.copy(
                            out=dst[:, 0:nch, :], in_=pst[0:64, 0:nch * 128].rearrange("p (b n) -> p b n", n=128)),
                            reads=[pt_], writes=[t_scr])
                    if cfg.get('rp_stop') == 4:
                        return
                    combos = ((pT, t_pT, nT, t_nT), (kT, t_kT, nT, t_nT), (kT, t_kT, rT, t_rT), (pT, t_pT, rT, t_rT))
                    dsts = ((B0, t_B0, -1.0, tri_s, trib_s), (Ank, t_Ank, 1.0, tri_s, trib_s),
                            (Ark, t_Ark, 1.0, tri_i, trib_i), (nArp, t_nArp, -1.0, tri_i, trib_i))
                    for rnd in range(2):
                        banks = {}
                        for qi in (2 * rnd, 2 * rnd + 1):
                            lt, tlt, rt, trt = combos[qi]
                            banks[qi] = [K.ps(), K.ps()]
                            for ci in range(nch):
                                for hs in range(2):
                                    pst, pt_ = banks[qi][hs]
                                    Ls = slice(hs * 64, hs * 64 + 64)
                                    S.op("pe", lambda h, ci=ci, Ls=Ls, lt=lt, rt=rt, pst=pst: h.matmul(
                                        pst[0:64, ci * 64:(ci + 1) * 64], lhsT=lt[Ls, ci * 64:(ci + 1) * 64],
                                        rhs=rt[Ls, ci * 64:(ci + 1) * 64], start=True, stop=True),
                                        reads=[tlt, trt], writes=[pt_], inc=(ci == nch - 1))
                        for qi in (2 * rnd, 2 * rnd + 1):
                            dst, tdst, sgn, trp, trs = dsts[qi]
                            dv_ = dst[:, 0:nblk, :].rearrange("p (c h) n -> p c h n", h=2)
                            for hs in range(2):
                                pst, pt_ = banks[qi][hs]
                                pv = pst[0:64, 0:nch * 64].rearrange("p (b n) -> p b n", n=64)
                                if npc > 0:
                                    S.op("dve", lambda h, dv_=dv_, pv=pv, sgn=sgn, trp=trp, hs=hs: h.scalar_tensor_tensor(
                                        out=dv_[:, 0:npc, hs, :], in0=pv[:, 0:npc, :], scalar=sgn,
                                        in1=trp.unsqueeze(1).to_broadcast([64, npc, 64]), op0=ALU.mult, op1=ALU.mult),
                                        reads=[pt_, t_tri], writes=[tdst])
                                if last:
                                    S.op("dve", lambda h, dv_=dv_, pv=pv, sgn=sgn, trs=trs, hs=hs: h.scalar_tensor_tensor(
                                        out=dv_[:, npc, hs, :], in0=pv[:, npc, :], scalar=sgn, in1=trs,
                                        op0=ALU.mult, op1=ALU.mult), reads=[pt_, t_tri], writes=[tdst])
                    if cfg.get('rp_stop') == 5:
                        return
                    TT, t_TT = solve_blocks(nblk, B0, Ca, Cb, Ba, Bb, Pa, Pb, t_B0, t_C, t_B, t_P)
                    if cfg.get('rp_stop') == 6:
                        return
                    ybank = Y_BANKS[ybi[0] % 2]; ybi[0] += 1
                    ops = dict(ns=(nT, t_nT), rs=(rT, t_rT), V=(Vt, t_scr), Kh=(Kh, t_scr), nPh=(nPh, t_scr),
                               Ank=(Ank, t_Ank), Ark=(Ark, t_Ark), nArp=(nArp, t_nArp), TT=(TT, t_TT),
                               Xs=(Xs, t_Xs), Es=(Es, t_Es), XT=(XT, t_XT), Km=(Km, t_Km), nPm=(nPm, t_nPm), t_mask=t_tri)
                    for ci in range(npc):
                        seq_block_pair(ci, ci * 64, [dict(Z=Zp, tZ=t_Zp, lo=0, hi=64, dc=dcs[:, ci:ci + 1], t_dc=t_dcs)],
                                       ops, ybank, ci * 64)
                    if last:
                        pst, pt_ = K.ps()
                        S.op("pe", lambda h, pst=pst: h.transpose(out=pst[:, 0:128], in_=Zp, identity=ident[:]),
                             reads=t_Zp + [t_id], writes=[pt_])
                        S.op("act", lambda h, pst=pst: h.copy(out=SstO[:, 0, :], in_=pst[:, 0:128]), reads=[pt_], writes=[t_Sst])
                        for hh in range(2):
                            Hs = slice(hh * 64, hh * 64 + 64)
                            S.dma("sp", rS_p[j, 2 * jp + hh], SstO[Hs, 0, Hs], reads=[t_Sst])
                        tZall = [t for tt in t_Zs for t in tt]
                        S.op("dve", lambda h: h.memset(Zs, 0.0), writes=tZall)
                        for sg in range(4):
                            for s_ in range(4):
                                S.dma("sp", Sst[:, s_, :].rearrange("p (h k) -> p h k", h=2),
                                      rwkv_S_in[j, sg * 4 + s_, 2 * jp:2 * jp + 2].rearrange("h v k -> v h k"),
                                      writes=[t_Sst])
                            pst, pt_ = K.ps()
                            for s_ in range(4):
                                S.op("pe", lambda h, s_=s_, pst=pst: h.transpose(out=pst[:, s_ * 64:(s_ + 1) * 64], in_=Sst[:, s_, :],
                                                                                  identity=ident[0:64, 0:64]),
                                     reads=[t_Sst, t_id], writes=[pt_], inc=(s_ == 3))
                            for hh in range(2):
                                Hs = slice(hh * 64, hh * 64 + 64)
                                S.op("dve", lambda h, pst=pst, Hs=Hs: h.tensor_copy(
                                    out=Zs[Hs, :, Hs], in_=pst[Hs, 0:256].rearrange("p (s n) -> p s n", s=4)),
                                    reads=[pt_], writes=tZall)
                            sts = [dict(Z=Zs[:, s_, :], tZ=t_Zs[s_], lo=(sg * 4 + s_) * 4, hi=(sg * 4 + s_) * 4 + 4,
                                        dc=dcs[:, 8 + sg * 4 + s_:9 + sg * 4 + s_], t_dc=t_dcs,
                                        mask=rmk[:, sg * 4 + s_:sg * 4 + s_ + 1]) for s_ in range(4)]
                            seq_block_pair(npc, so, sts, ops, ybank, so, tcols=(sg * 16, sg * 16 + 16))
                            pst, pt_ = K.ps()
                            for s_ in range(4):
                                S.op("pe", lambda h, s_=s_, pst=pst: h.transpose(out=pst[:, s_ * 128:(s_ + 1) * 128], in_=Zs[:, s_, :],
                                                                                  identity=ident[:]),
                                     reads=tZall + [t_id], writes=[pt_], inc=(s_ == 3))
                            S.op("act", lambda h, pst=pst: h.copy(out=SstO, in_=pst[:, :].rearrange("p (s n) -> p s n", s=4)),
                                 reads=[pt_], writes=[t_Sst])
                            for s_ in range(4):
                                for hh in range(2):
                                    Hs = slice(hh * 64, hh * 64 + 64)
                                    S.dma("sp", rS_s[j, sg * 4 + s_, 2 * jp + hh], SstO[Hs, s_, Hs], reads=[t_Sst])
                    if cfg.get('rp_stop') == 7:
                        return
                    yb, tyb = ybank
                    S.op("act", lambda h: h.copy(out=B_("r"), in_=yb[:, 0:n]), reads=[tyb, tb["rk"]], writes=[tb["r"]])
                    pst, pt_ = K.ps()
                    S.op("pe", lambda h, pst=pst: h.matmul(pst[:, 0:n], lhsT=blkf, rhs=B_("r"), start=True, stop=True),
                         reads=[t_blk, tb["r"]], writes=[pt_])
                    S.op("dve", lambda h, pst=pst: h.scalar_tensor_tensor(out=B_("r"), in0=pst[:, 0:n], scalar=-1.0 / 64.0, in1=B_("r"),
                                                                         op0=ALU.mult, op1=ALU.add), reads=[pt_, tb["r"]], writes=[tb["r"]])
                    S.op("act", lambda h: h.activation(out=sqb[:, 0:n], in_=B_("r"), func=AF.Square), reads=[tb["r"]], writes=[t_sqb])
                    pst, pt_ = K.ps()
                    S.op("pe", lambda h, pst=pst: h.matmul(pst[:, 0:n], lhsT=blkb, rhs=sqb[:, 0:n], start=True, stop=True),
                         reads=[t_blk, t_sqb], writes=[pt_])
                    S.op("act", lambda h, pst=pst: h.activation(out=B_("t"), in_=pst[:, 0:n], func=AF.Sqrt, bias=cst[:, 1:2],
                                                                scale=1.0 / 64.0), reads=[pt_, t_cst], writes=[tb["t"]])
                    S.op("dve", lambda h: h.reciprocal(out=B_("t"), in_=B_("t")), reads=[tb["t"]], writes=[tb["t"]])
                    S.op("dve", lambda h: h.tensor_tensor(out=B_("r"), in0=B_("r"), in1=B_("t"), op=ALU.mult),
                         reads=[tb["r"], tb["t"]], writes=[tb["r"]])
                    S.op("dve", lambda h: h.tensor_scalar(out=B_("r"), in0=B_("r"), scalar1=PV("rw_lnw", jp), scalar2=PV("rw_lnb", jp),
                                                          op0=ALU.mult, op1=ALU.add), reads=[tb["r"], t_pt], writes=[tb["r"]])
                    pst, pt_ = K.ps()
                    S.op("pe", lambda h, pst=pst: h.matmul(pst[:, 0:n], lhsT=blkf, rhs=B_("rk"), start=True, stop=True),
                         reads=[t_blk, tb["rk"]], writes=[pt_])
                    S.op("dve", lambda h, pst=pst: h.tensor_tensor(out=B_("t"), in0=pst[:, 0:n], in1=B_("v"), op=ALU.mult),
                         reads=[pt_, tb["v"]], writes=[tb["t"]])
                    S.op("dve", lambda h: h.tensor_tensor(out=B_("r"), in0=B_("r"), in1=B_("t"), op=ALU.add),
                         reads=[tb["r"], tb["t"]], writes=[tb["r"]])
                    S.op("dve", lambda h: h.tensor_tensor(out=yg[:, 0:n], in0=B_("r"), in1=B_("g"), op=ALU.mult),
                         reads=[tb["r"], tb["g"]], writes=[t_yg])
                    a2 = max(a, PAD)
                    if a2 < b:
                        tl = tile_of(a2, b)
                        for oc in range(8):
                            pso, to = K.ps()
                            S.op("pe", lambda h, oc=oc, pso=pso: h.matmul(pso[:, 0:n], lhsT=wo[:, oc * 128:(oc + 1) * 128], rhs=yg[:, 0:n],
                                                                          start=True, stop=True), reads=[t_wo, t_yg], writes=[to])
                            S.op("dve", lambda h, oc=oc, pso=pso: h.tensor_tensor(
                                out=xT[:, oc, a2:b], in0=xT[:, oc, a2:b], in1=pso[:, a2 - a:b - a], op=ALU.add),
                                reads=[to] + tl, writes=tl)
                for ti_, (a_, b_) in enumerate(RT):
                    if ti_ in cfg.get('rw_tiles', range(100)):
                        do_tile(ti_, a_, b_)
            for jp_ in range(cfg.get("rwkv_pairs", 8)):
                do_pair(jp_)
            S.barrier()

        for l in range(nL):
            if l % 2 == 0 and cfg.get("gdn", False):
                gdn_layer(l)
            if l % 2 == 1 and cfg.get("rwkv", False):
                rwkv_layer(l)
            if cfg.get("ffn", True):
                ffn_layer(l)

        o = 0
        hnf = arena[:, o:o + 8 * 512].rearrange("p (c n) -> p c n", c=8); o += 4096
        sq = arena[:, o:o + 2048].bitcast(BF16).rearrange("p (c n) -> p c n", c=8); o += 2048
        rs = arena[:, o:o + 512]; o += 512
        orow = [arena[:, o + i * 1024:o + (i + 1) * 1024] for i in range(2)]; o += 2048
        t_hnf, t_sq, t_rs = Trk(), Trk(), Trk()
        t_orow = [Trk(), Trk()]
        oi = 0
        for (a, b) in TILES:
            rmsnorm(hnf, 0, "norm_final", 0, a, b, sq, t_sq, rs, t_rs, t_hnf)
            for k in range((b - a) // 128):
                ob = orow[oi % 2]; tob = t_orow[oi % 2]; oi += 1
                for half in range(2):
                    pst, pt_ = K.ps()
                    for cc in range(4):
                        c = half * 4 + cc
                        S.op("pe", lambda h, pst=pst, cc=cc, c=c, k=k: h.transpose(
                            out=pst[:, cc * 128:(cc + 1) * 128], in_=hnf[:, c, k * 128:(k + 1) * 128],
                            identity=ident[:]), reads=[t_hnf, t_id], writes=[pt_], inc=(cc == 3))
                    if half:
                        S.op("act", lambda h, pst=pst, ob=ob: h.copy(out=ob[:, 512:1024], in_=pst[:]),
                             reads=[pt_], writes=[tob])
                    else:
                        S.op("dve", lambda h, pst=pst, ob=ob: h.tensor_copy(out=ob[:, 0:512], in_=pst[:]),
                             reads=[pt_], writes=[tob])
                S.dma("sp", yout[a + k * 128:a + (k + 1) * 128, :], ob, reads=[tob])
        S.barrier()
        S.emit(block)
    return nc


_CACHE = {}


def kernel(**inp):
    inp = {k: np.asarray(v) for k, v in inp.items()}
    pt = make_ptab(inp)
    ptab = pt.build()
    cfg = dict(_CFG)
    cfg["pt_off"] = pt.off
    cfg["pt_n"] = pt.n
    key = tuple(sorted((k, str(v)) for k, v in cfg.items() if k not in ("pt_off",)))
    if key not in _CACHE:
        _CACHE[key] = build(cfg)
    nc = _CACHE[key]

    xp = inp["x_prompt"].astype(np.float32)
    xs = inp["x_sample"].astype(np.float32)
    meta = inp["meta_tokens"].astype(np.float32)
    in_maps = []
    for b in range(NCORE):
        xin = np.zeros((T, D), np.float32)
        xin[PAD:P0] = meta
        xin[P0:S0] = xp[b]
        xin[S0:] = xs[16 * b:16 * b + 16].reshape(64, D)
        m = {
            "xin": xin,
            "ptab": ptab,
            "ffn_conv_in": np.ascontiguousarray(inp["state_ffn_conv"][:, 16 * b:16 * b + 16]),
            "ffn_w_up": inp["ffn_w_up"],
            "gdn_w_in": inp["gdn_w_in"], "gdn_w_out": inp["gdn_w_out"],
            "rwkv_S_in": np.ascontiguousarray(inp["state_rwkv_S"][:, 16 * b:16 * b + 16]),
            "rwkv_shift_in": np.ascontiguousarray(inp["state_rwkv_shift"][:, 16 * b:16 * b + 16]),
            **{kk_: inp[kk_] for kk_ in ("rwkv_wr", "rwkv_wk", "rwkv_wv", "rwkv_wo", "rwkv_w1", "rwkv_w2", "rwkv_a1",
                                         "rwkv_a2", "rwkv_g1", "rwkv_g2", "rwkv_v1", "rwkv_v2")},
            "gdn_S_in": np.ascontiguousarray(inp["state_gdn_S"][:, 16 * b:16 * b + 16]),
            "gdn_conv_in": np.ascontiguousarray(inp["state_gdn_conv"][:, 16 * b:16 * b + 16]),
            "ffn_w_down": inp["ffn_w_down"],
        }
        in_maps.append(m)
    ncr = cfg.get('ncores', NCORE)
    res = run_bass_kernel_spmd(nc, in_maps[:ncr], core_ids=list(range(ncr)))
    R = list(res.results)
    while len(R) < NCORE:
        R.append(R[0])
    y_prompt = np.stack([R[b]["yout"][P0:S0] for b in range(NCORE)])
    y_sample = np.concatenate([R[b]["yout"][S0:].reshape(16, 4, D) for b in range(NCORE)])
    fC_p = np.stack([R[b]["fC_p"] for b in range(NCORE)], axis=1)
    fC_s = np.concatenate([R[b]["fC_s"] for b in range(NCORE)], axis=1)
    z = lambda *s: np.zeros(s, np.float32)
    rS_p = np.stack([R[b]["rS_p"] for b in range(NCORE)], axis=1)
    rSh_p = np.stack([R[b]["rSh_p"] for b in range(NCORE)], axis=1)
    gS_p = np.stack([R[b]["gS_p"] for b in range(NCORE)], axis=1)
    gC_p = np.stack([R[b]["gC_p"] for b in range(NCORE)], axis=1)
    gS_s = np.concatenate([R[b]["gS_s"] for b in range(NCORE)], axis=1)
    gC_s = np.concatenate([R[b]["gC_s"] for b in range(NCORE)], axis=1)
    rS_s = np.concatenate([R[b]["rS_s"] for b in range(NCORE)], axis=1)
    rSh_s = np.concatenate([R[b]["rSh_s"] for b in range(NCORE)], axis=1)
    return (y_prompt, y_sample, gS_p, gC_p, rS_p, rSh_p, fC_p, gS_s, gC_s, rS_s, rSh_s, fC_s)


_CFG = {"nlayers": 4, "ffn": True, "gdn": True, "rwkv": True}
```

```python
import numpy as np
from contextlib import ExitStack
import concourse.bass as bass
import concourse.mybir as mybir
from concourse.bass_utils import run_bass_kernel_spmd

F32 = mybir.dt.float32
BF16 = mybir.dt.bfloat16
F32R = mybir.dt.float32r
R = lambda ap: ap.bitcast(F32R)
AF = mybir.ActivationFunctionType
ALU = mybir.AluOpType

NCORE = 8
D = 1024
T = 2176
PAD = 48
P0 = 64
S0 = 2112
DFF = 2816
NF = 22
TILES = [(0, 512), (512, 1024), (1024, 1536), (1536, 2048), (2048, 2176)]
FGROUPS = [(0, 768), (768, 1536), (1536, 2176)]


class Trk:
    __slots__ = ("w", "r")

    def __init__(self):
        self.w = None
        self.r = {}


class Sched:
    ENGS = ("pe", "act", "dve", "pool", "sp")

    def __init__(self, nc):
        self.nc = nc
        self.q = {e: [] for e in self.ENGS}
        self.cnt = {}
        self.sem = {}
        self.seen = {e: {} for e in self.ENGS}

    NDS = 24

    def alloc_sems(self, stack):
        self.dslot = {}
        for e in self.ENGS:
            self.sem[e] = stack.enter_context(self.nc.semaphore("s_" + e))
            self.cnt[e] = 0
        for e in ("sp", "pool"):
            self.dslot[e] = 0
            for i in range(self.NDS):
                d = "%s.d%d" % (e, i)
                self.sem[d] = stack.enter_context(self.nc.semaphore("d_%s_%d" % (e, i)))
                self.cnt[d] = 0

    def _waits(self, eng, reads, writes):
        need = {}
        for t in reads:
            if t.w is not None:
                p, c = t.w
                if p == eng and eng == "pe":
                    continue
                if need.get(p, 0) < c:
                    need[p] = c
        for t in writes:
            if t.w is not None:
                p, c = t.w
                if p != eng and need.get(p, 0) < c:
                    need[p] = c
            for p, c in t.r.items():
                if p != eng and need.get(p, 0) < c:
                    need[p] = c
        out = []
        for p, c in need.items():
            if self.seen[eng].get(p, 0) >= c:
                continue
            self.seen[eng][p] = c
            out.append((p, c))
        return out

    @staticmethod
    def _mark(prod, c, reads, writes):
        for t in reads:
            if t.r.get(prod, 0) < c:
                t.r[prod] = c
        for t in writes:
            t.w = (prod, c)
            t.r = {}

    def op(self, eng, fn, reads=(), writes=(), inc=True):
        waits = self._waits(eng, reads, writes)
        c = self.cnt[eng] + 1
        if inc:
            self.cnt[eng] = c
        sem = self.sem

        def run(h):
            for p, cc in waits:
                h.wait_ge(sem[p], cc * 16 if ".d" in p else cc)
            ins = fn(h)
            if inc:
                ins.then_inc(sem[eng], 1)
        self.q[eng].append(run)
        self._mark(eng, c, reads, writes)

    def dma(self, eng, out, in_, reads=(), writes=(), **kw):
        waits = self._waits(eng, reads, writes)
        d = "%s.d%d" % (eng, self.dslot[eng] % self.NDS)
        self.dslot[eng] += 1
        if self.cnt[d] > 0 and self.seen[eng].get(d, 0) < self.cnt[d]:
            self.seen[eng][d] = self.cnt[d]
            waits = waits + [(d, self.cnt[d])]
        self.cnt[d] += 1
        c = self.cnt[d]
        sem = self.sem

        def run(h):
            for p, cc in waits:
                h.wait_ge(sem[p], cc * 16 if ".d" in p else cc)
            h.dma_start(out=out, in_=in_, **kw).then_inc(sem[d], 16)
        self.q[eng].append(run)
        self._mark(d, c, reads, writes)

    def barrier(self):
        sem = self.sem
        finals = [(p, c) for p, c in self.cnt.items() if c > 0]
        for e in self.ENGS:
            mine = []
            for p, c in finals:
                if self.seen[e].get(p, 0) < c:
                    self.seen[e][p] = c
                    mine.append((p, c))

            def run(h, mine=mine):
                for p, cc in mine:
                    h.wait_ge(sem[p], cc * 16 if ".d" in p else cc)
            self.q[e].append(run)

    def emit(self, block):
        for e, deco in (("pe", block.tensor), ("act", block.scalar), ("dve", block.vector),
                        ("pool", block.gpsimd), ("sp", block.sync)):
            lst = self.q[e]

            def body(h, lst=lst):
                for r in lst:
                    r(h)
            deco(body)


def _feat(v):
    v = np.asarray(v, np.float32)
    n = v.shape[-1] // 128
    v = v.reshape(v.shape[:-1] + (n, 128))
    return np.ascontiguousarray(np.moveaxis(v, -1, 0))


class PTab:
    def __init__(self):
        self.cols = []
        self.off = {}
        self.n = 0

    def add(self, name, arr):
        arr = np.asarray(arr, np.float32).reshape(128, -1)
        self.off[name] = self.n
        self.cols.append(arr)
        self.n += arr.shape[1]

    def build(self):
        return np.ascontiguousarray(np.concatenate(self.cols, axis=1))


def make_ptab(inp):
    pt = PTab()
    pt.add("norm_mix", _feat(inp["norm_mix"]))
    pt.add("norm_ffn", _feat(inp["norm_ffn"]))
    pt.add("norm_final", _feat(inp["norm_final"]))
    pt.add("ffn_cw", _feat(inp["ffn_conv_w"]))
    pt.add("gdn_cw", _feat(inp["gdn_conv_w"]))
    pt.add("gdn_nw", np.asarray(inp["gdn_norm_w"], np.float32).T)
    z = np.zeros((128, 2), np.float32)
    ga = z.copy(); ga[64:72] = np.asarray(inp["gdn_A_log"], np.float32).T
    gd = z.copy(); gd[64:72] = np.asarray(inp["gdn_dt_bias"], np.float32).T
    pt.add("rw_mix", _feat(inp["rwkv_mix"]))
    for nm_, key_ in (("rw_w0", "rwkv_w0"), ("rw_a0", "rwkv_a0"), ("rw_kk", "rwkv_k_k"), ("rw_ka", "rwkv_k_a"),
                      ("rw_lnw", "rwkv_ln_w"), ("rw_lnb", "rwkv_ln_b"), ("rw_v0", "rwkv_v0")):
        pt.add(nm_, _feat(inp[key_]))
    pt.add("rw_rk", _feat(np.asarray(inp["rwkv_r_k"], np.float32).reshape(2, D)))
    pt.add("gdn_A", ga)
    pt.add("gdn_dt", gd)
    return pt


PT_OFF = None


class KB:
    def __init__(self, nc, st, S):
        self.nc = nc
        self.st = st
        self.S = S
        self.psb = []
        self.psi = 0

    def sb(self, name, shape, dt):
        return self.st.enter_context(self.nc.sbuf_tensor(name, shape, dt))

    def init_psum(self):
        for i in range(8):
            t = self.st.enter_context(self.nc.psum_tensor("ps%d" % i, [128, 512], F32))
            self.psb.append((t, Trk()))

    def ps(self):
        r = self.psb[self.psi]
        self.psi = (self.psi + 1) % 6
        return r


def build(cfg):
    nc = bass.Bass("TRN2", target_bir_lowering=False)
    dram_in = lambda n, s: nc.dram_tensor(n, list(s), F32, kind="ExternalInput").ap()
    dram_out = lambda n, s: nc.dram_tensor(n, list(s), F32, kind="ExternalOutput").ap()
    nL = cfg.get("nlayers", 4)
    pt_off = cfg["pt_off"]
    pt_n = cfg["pt_n"]

    xin = dram_in("xin", (T, D))
    ptab_d = dram_in("ptab", (128, pt_n))
    ffn_conv_in = dram_in("ffn_conv_in", (4, 16, 2, DFF))
    w_up = dram_in("ffn_w_up", (4, D, 2 * DFF))
    w_down = dram_in("ffn_w_down", (4, DFF, D))
    gdn_w_in = dram_in("gdn_w_in", (2, D, 4112))
    gdn_w_out = dram_in("gdn_w_out", (2, D, D))
    gdn_S_in = dram_in("gdn_S_in", (2, 16, 8, 128, 128))
    gdn_conv_in = dram_in("gdn_conv_in", (2, 16, 3, 3072))
    gS_p = dram_out("gS_p", (2, 8, 128, 128))
    gC_p = dram_out("gC_p", (2, 3, 3072))
    gS_s = dram_out("gS_s", (2, 16, 8, 128, 128))
    gC_s = dram_out("gC_s", (2, 16, 3, 3072))
    rwkv_S_in = dram_in("rwkv_S_in", (2, 16, 16, 64, 64))
    rwkv_shift_in = dram_in("rwkv_shift_in", (2, 16, D))
    rwkv_wr = dram_in("rwkv_wr", (2, D, D)); rwkv_wk = dram_in("rwkv_wk", (2, D, D))
    rwkv_wv = dram_in("rwkv_wv", (2, D, D)); rwkv_wo = dram_in("rwkv_wo", (2, D, D))
    rwkv_w1 = dram_in("rwkv_w1", (2, D, 64)); rwkv_w2 = dram_in("rwkv_w2", (2, 64, D))
    rwkv_a1 = dram_in("rwkv_a1", (2, D, 64)); rwkv_a2 = dram_in("rwkv_a2", (2, 64, D))
    rwkv_g1 = dram_in("rwkv_g1", (2, D, 160)); rwkv_g2 = dram_in("rwkv_g2", (2, 160, D))
    rwkv_v1 = dram_in("rwkv_v1", (1, D, 32)); rwkv_v2 = dram_in("rwkv_v2", (1, 32, D))
    vf_d = nc.dram_tensor("vfirst_scratch", [8, 128, T], F32).ap()
    rS_p = dram_out("rS_p", (2, 16, 64, 64))
    rSh_p = dram_out("rSh_p", (2, D))
    rS_s = dram_out("rS_s", (2, 16, 16, 64, 64))
    rSh_s = dram_out("rSh_s", (2, 16, D))
    yout = dram_out("yout", (T, D))
    fC_p = dram_out("fC_p", (4, 2, DFF))
    fC_s = dram_out("fC_s", (4, 16, 2, DFF))

    with ExitStack() as st:
        S = Sched(nc)
        S.alloc_sems(st)
        K = KB(nc, st, S)
        K.init_psum()
        sb = K.sb
        xT = sb("xT", [128, 8, T], F32)
        t_x = [Trk() for _ in TILES]
        ptab = sb("ptab_sb", [128, pt_n], F32); t_pt = Trk()
        ident = sb("ident", [128, 128], F32); t_id = Trk()
        onesb = sb("onesb", [128, 128], BF16); t_ones = Trk()
        cst = sb("cst", [128, 4], F32); t_cst = Trk()
        ARENA = 30500
        arena = sb("arena", [128, ARENA], F32)
        ARENAR = 3584
        arenaR_ = sb("arenaR", [128, ARENAR], F32R)
        arenaR = arenaR_[:].bitcast(F32)
        block = st.enter_context(nc.Block())

        def tile_of(c0, c1):
            return [t_x[i] for i, (a, b) in enumerate(TILES) if a < c1 and c0 < b]

        S.dma("sp", ptab[:], ptab_d, writes=[t_pt])
        S.op("pool", lambda h: h.memset(ident[:], 0.0), writes=[t_id])
        S.op("pool", lambda h: h.affine_select(out=ident[:], in_=ident[:], pattern=[[-1, 128]],
                                                 compare_op=ALU.not_equal, fill=1.0, base=0,
                                                 channel_multiplier=1), reads=[t_id], writes=[t_id])
        S.op("pool", lambda h: h.memset(onesb[:], 1.0 / 1024.0), writes=[t_ones])
        S.op("pool", lambda h: h.memset(cst[:, 0:1], 1e-6), writes=[t_cst])
        S.op("pool", lambda h: h.memset(cst[:, 1:2], 64e-5), writes=[t_cst])
        S.op("pool", lambda h: h.memset(cst[:, 2:3], 1.0), writes=[t_cst])
        S.op("pool", lambda h: h.memset(cst[:, 3:4], 0.0), writes=[t_cst])

        tri_t = sb("tri_t", [64, 4, 64], F32); t_tri = Trk()
        rmk = sb("rmk", [64, 16], F32)
        tri_s, tri_i, trib_s, trib_i = tri_t[:, 0, :], tri_t[:, 1, :], tri_t[:, 2, :], tri_t[:, 3, :]
        S.op("pool", lambda h: h.memset(tri_t[:], 1.0), writes=[t_tri])
        S.op("pool", lambda h: h.memset(rmk[:], 1.0), writes=[t_tri])
        for (ix, op_) in ((0, ALU.is_gt), (1, ALU.is_ge), (2, ALU.is_gt), (3, ALU.is_ge)):
            S.op("pool", lambda h, ix=ix, op_=op_: h.affine_select(
                out=tri_t[:, ix, :], in_=tri_t[:, ix, :], pattern=[[1, 64]], compare_op=op_, fill=0.0, base=0,
                channel_multiplier=-1), reads=[t_tri], writes=[t_tri])
        for ix in (2, 3):
            v = tri_t[:, ix, :].rearrange("p (s k) -> p s k", k=4)
            S.op("pool", lambda h, v=v: h.affine_select(out=v, in_=v, pattern=[[-4, 16], [0, 4]], compare_op=ALU.is_ge,
                                                       fill=0.0, base=0, channel_multiplier=1),
                 reads=[t_tri], writes=[t_tri])
            S.op("pool", lambda h, v=v: h.affine_select(out=v, in_=v, pattern=[[4, 16], [0, 4]], compare_op=ALU.is_ge,
                                                       fill=0.0, base=3, channel_multiplier=-1),
                 reads=[t_tri], writes=[t_tri])
        S.op("pool", lambda h: h.affine_select(out=rmk[:], in_=rmk[:], pattern=[[-4, 16]], compare_op=ALU.is_ge,
                                               fill=0.0, base=0, channel_multiplier=1), reads=[t_tri], writes=[t_tri])
        S.op("pool", lambda h: h.affine_select(out=rmk[:], in_=rmk[:], pattern=[[4, 16]], compare_op=ALU.is_ge,
                                               fill=0.0, base=3, channel_multiplier=-1), reads=[t_tri], writes=[t_tri])

        def pcol(name, idx):
            o = pt_off[name] + idx
            return ptab[:, o:o + 1]

        a_f = arena[:, 0:2 * 1024].rearrange("p (i n) -> p i n", i=2)
        t_in = [Trk(), Trk()]
        for it in range(T // 128):
            buf = a_f[:, it % 2, :]
            tb = t_in[it % 2]
            S.dma("sp", buf, xin[it * 128:(it + 1) * 128, :], writes=[tb])
            for half in range(2):
                pst, pt_ = K.ps()
                for cc in range(4):
                    c = half * 4 + cc
                    S.op("pe", lambda h, pst=pst, cc=cc, buf=buf, c=c: h.transpose(
                        out=pst[:, cc * 128:(cc + 1) * 128], in_=buf[:, c * 128:(c + 1) * 128],
                        identity=ident[:]), reads=[tb, t_id], writes=[pt_], inc=(cc == 3))
                tl = tile_of(it * 128, it * 128 + 128)
                S.op("act" if half else "dve",
                     (lambda h, pst=pst, half=half, it=it: h.copy(
                         out=xT[:, half * 4:half * 4 + 4, it * 128:(it + 1) * 128],
                         in_=pst[:].rearrange("p (c n) -> p c n", c=4))) if half else
                     (lambda h, pst=pst, half=half, it=it: h.tensor_copy(
                         out=xT[:, half * 4:half * 4 + 4, it * 128:(it + 1) * 128],
                         in_=pst[:].rearrange("p (c n) -> p c n", c=4))),
                     reads=[pt_], writes=tl)
        S.barrier()

        def rmsnorm(dst, dc0, gname, gidx, c0, c1, sq, t_sq, rs, t_rs, t_dst):
            for (a, b) in TILES:
                a2, b2 = max(a, c0), min(b, c1)
                if a2 >= b2:
                    continue
                n = b2 - a2
                tl = tile_of(a2, b2)
                for c in range(8):
                    S.op("act", lambda h, c=c, a2=a2, b2=b2, n=n: h.activation(
                        out=sq[:, c, 0:n], in_=xT[:, c, a2:b2], func=AF.Square),
                        reads=tl, writes=[t_sq])
                pst, pt_ = K.ps()
                for c in range(8):
                    S.op("pe", lambda h, c=c, n=n, pst=pst: h.matmul(
                        pst[:, 0:n], lhsT=onesb[:], rhs=sq[:, c, 0:n], start=(c == 0), stop=(c == 7)),
                        reads=[t_sq, t_ones], writes=[pt_], inc=(c == 7))
                S.op("act", lambda h, n=n, pst=pst: h.activation(
                    out=rs[:, 0:n], in_=pst[:, 0:n], func=AF.Sqrt, bias=cst[:, 0:1], scale=1.0),
                    reads=[pt_, t_cst], writes=[t_rs])
                S.op("dve", lambda h, n=n: h.reciprocal(out=rs[:, 0:n], in_=rs[:, 0:n]),
                     reads=[t_rs], writes=[t_rs])
                for c in range(8):
                    S.op("dve", lambda h, c=c, a2=a2, b2=b2, n=n: h.scalar_tensor_tensor(
                        out=dst[:, c, dc0 + a2 - c0:dc0 + b2 - c0], in0=xT[:, c, a2:b2],
                        scalar=pcol(gname, gidx * 8 + c), in1=rs[:, 0:n], op0=ALU.mult, op1=ALU.mult),
                        reads=tl + [t_rs, t_pt], writes=[t_dst])

        def ffn_layer(l):
            o = 0
            actT = arena[:, o:o + 8448].bitcast(BF16).rearrange("p (f n) -> p f n", f=NF); o += 8448
            hn2 = arena[:, o:o + 3072].bitcast(BF16).rearrange("p (c n) -> p c n", c=8); o += 3072
            wdb = [arena[:, o + i * 1408:o + (i + 1) * 1408].bitcast(BF16).rearrange(
                "p (f n) -> p f n", f=NF) for i in range(2)]; o += 2816
            wu = [arena[:, o + i * 2048:o + (i + 1) * 2048].bitcast(BF16).rearrange(
                "p (c g n) -> p c g n", c=8, g=2) for i in range(2)]; o += 4096
            sq = arena[:, o:o + 2048].bitcast(BF16).rearrange("p (c n) -> p c n", c=8); o += 2048
            rs = arena[:, o:o + 512]; o += 512
            gpre = [arena[:, o + i * 516:o + (i + 1) * 516] for i in range(2)]; o += 1032
            gps = arena[:, o:o + 96].rearrange("p (s k) -> p s k", s=16); o += 96
            gc = arena[:, o:o + 512]; o += 512
            carry = arena[:, o:o + 44].rearrange("p (f k) -> p f k", f=NF); o += 44
            hist = arena[:, o:o + NF * 32].rearrange("p (f s k) -> p f s k", f=NF, s=16); o += NF * 32
            stg = arena[:, o:o + NF * 34].rearrange("p (f k) -> p f k", f=NF); o += NF * 34
            rows = arena[0:34, 0:DFF]
            hrow = arena[0:32, 0:DFF]
            assert o <= ARENA, o
            t_act, t_hn2, t_sq, t_rs, t_gc = Trk(), Trk(), Trk(), Trk(), Trk()
            t_wdb = [Trk(), Trk()]
            t_wu = [Trk(), Trk()]
            t_gpre = [Trk(), Trk()]
            t_gps, t_carry, t_hist, t_stg = Trk(), Trk(), Trk(), Trk()
            t_rows = t_act
            t_hrow = t_act

            wd_src = w_down[l].rearrange("(f p) n -> p f n", p=128)
            S.dma("sp", hrow, ffn_conv_in[l].rearrange("s k n -> (s k) n"), writes=[t_hrow])
            for f0 in range(0, NF, 4):
                pst, pt_ = K.ps()
                nf = min(4, NF - f0)
                for ff in range(nf):
                    f = f0 + ff
                    S.op("pe", lambda h, pst=pst, ff=ff, f=f: h.transpose(
                        out=pst[:, ff * 32:(ff + 1) * 32], in_=hrow[:, f * 128:(f + 1) * 128],
                        identity=ident[0:32, 0:32]), reads=[t_hrow, t_id], writes=[pt_], inc=(ff == nf - 1))
                S.op("dve", lambda h, pst=pst, f0=f0, nf=nf: h.tensor_copy(
                    out=hist[:, f0:f0 + nf, :, :].rearrange("p f s k -> p f (s k)"),
                    in_=pst[:, 0:nf * 32].rearrange("p (f n) -> p f n", f=nf)),
                    reads=[pt_], writes=[t_hist])
            S.op("dve", lambda h: h.memset(carry, 0.0), writes=[t_carry])

            up_src = w_up[l].rearrange("(c p) n -> p c n", p=128)
            wi = 0
            wdi = 0
            for (g0_, g1_) in FGROUPS:
              def do_group(g0, g1):
                  nonlocal wi, wdi
                  rmsnorm(hn2, 0, "norm_ffn", l, g0, g1, sq, t_sq, rs, t_rs, t_hn2)
                  gtiles = [(max(a, g0), min(b, g1)) for (a, b) in TILES if max(a, g0) < min(b, g1)]
                  for f2 in range(0, NF, 2):
                      wt = wu[wi % 2]; twu = t_wu[wi % 2]; wi += 1
                      S.dma("pool", wt[:, :, 0, :], up_src[:, :, f2 * 128:f2 * 128 + 256], writes=[twu])
                      S.dma("pool", wt[:, :, 1, :], up_src[:, :, DFF + f2 * 128:DFF + f2 * 128 + 256],
                            writes=[twu])
                      for fi in range(2):
                          f = f2 + fi
                          cw = lambda j, f=f: pcol("ffn_cw", (l * 3 + j) * NF + f)
                          for ti, (a, b) in enumerate(gtiles):
                              n = b - a
                              psg, tg = K.ps()
                              psu, tu = K.ps()
                              for c in range(8):
                                  S.op("pe", lambda h, c=c, a=a, b=b, n=n, psg=psg, wt=wt, fi=fi: h.matmul(
                                      psg[:, 0:n], lhsT=wt[:, c, 0, fi * 128:(fi + 1) * 128],
                                      rhs=hn2[:, c, a - g0:b - g0], start=(c == 0), stop=(c == 7)),
                                      reads=[twu, t_hn2], writes=[tg], inc=(c == 7))
                              for c in range(8):
                                  S.op("pe", lambda h, c=c, a=a, b=b, n=n, psu=psu, wt=wt, fi=fi: h.matmul(
                                      psu[:, 0:n], lhsT=wt[:, c, 1, fi * 128:(fi + 1) * 128],
                                      rhs=hn2[:, c, a - g0:b - g0], start=(c == 0), stop=(c == 7)),
                                      reads=[twu, t_hn2], writes=[tu], inc=(c == 7))
                              gp = gpre[ti % 2]; tgp = t_gpre[ti % 2]
                              S.op("dve", lambda h, gp=gp, f=f: h.tensor_copy(out=gp[:, 0:2], in_=carry[:, f, :]),
                                   reads=[t_carry], writes=[tgp])
                              S.op("act", lambda h, gp=gp, n=n, psg=psg: h.copy(out=gp[:, 2:2 + n], in_=psg[:, 0:n]),
                                   reads=[tg], writes=[tgp])
                              S.op("dve", lambda h, gp=gp, n=n, f=f: h.tensor_copy(out=carry[:, f, :],
                                                                                  in_=gp[:, n:n + 2]),
                                   reads=[tgp], writes=[t_carry])
                              npz = n if b <= S0 else S0 - a
                              S.op("dve", lambda h, gp=gp, npz=npz, cw=cw: h.tensor_scalar(
                                  out=gc[:, 0:npz], in0=gp[:, 2:2 + npz], scalar1=cw(2), scalar2=None,
                                  op0=ALU.mult), reads=[tgp, t_pt], writes=[t_gc])
                              S.op("dve", lambda h, gp=gp, npz=npz, cw=cw: h.scalar_tensor_tensor(
                                  out=gc[:, 0:npz], in0=gp[:, 1:1 + npz], scalar=cw(1), in1=gc[:, 0:npz],
                                  op0=ALU.mult, op1=ALU.add), reads=[tgp, t_pt, t_gc], writes=[t_gc])
                              S.op("dve", lambda h, gp=gp, npz=npz, cw=cw: h.scalar_tensor_tensor(
                                  out=gc[:, 0:npz], in0=gp[:, 0:npz], scalar=cw(0), in1=gc[:, 0:npz],
                                  op0=ALU.mult, op1=ALU.add), reads=[tgp, t_pt, t_gc], writes=[t_gc])
                              if b > S0:
                                  so = S0 - a
                                  S.op("dve", lambda h, f=f: h.tensor_copy(out=gps[:, :, 0:2], in_=hist[:, f, :, :]),
                                       reads=[t_hist], writes=[t_gps])
                                  S.op("dve", lambda h, gp=gp, so=so: h.tensor_copy(
                                      out=gps[:, :, 2:6], in_=gp[:, 2 + so:2 + so + 64].rearrange("p (s k) -> p s k", s=16)),
                                      reads=[tgp], writes=[t_gps])
                                  gcs = gc[:, so:so + 64].rearrange("p (s k) -> p s k", s=16)
                                  S.op("dve", lambda h, gcs=gcs, cw=cw: h.tensor_scalar(
                                      out=gcs, in0=gps[:, :, 2:6], scalar1=cw(2), scalar2=None, op0=ALU.mult),
                                      reads=[t_gps, t_pt], writes=[t_gc])
                                  S.op("dve", lambda h, gcs=gcs, cw=cw: h.scalar_tensor_tensor(
                                      out=gcs, in0=gps[:, :, 1:5], scalar=cw(1), in1=gcs, op0=ALU.mult, op1=ALU.add),
                                      reads=[t_gps, t_pt, t_gc], writes=[t_gc])
                                  S.op("dve", lambda h, gcs=gcs, cw=cw: h.scalar_tensor_tensor(
                                      out=gcs, in0=gps[:, :, 0:4], scalar=cw(0), in1=gcs, op0=ALU.mult, op1=ALU.add),
                                      reads=[t_gps, t_pt, t_gc], writes=[t_gc])
                                  S.op("pool", lambda h, gp=gp, so=so, f=f: h.tensor_copy(
                                      out=stg[:, f, 0:2], in_=gp[:, so:so + 2]), reads=[tgp], writes=[t_stg])
                                  S.op("pool", lambda h, f=f: h.tensor_copy(
                                      out=stg[:, f, 2:34].rearrange("p (s k) -> p s k", s=16), in_=gps[:, :, 4:6]),
                                      reads=[t_gps], writes=[t_stg])
                              S.op("act", lambda h, n=n: h.activation(out=gc[:, 0:n], in_=gc[:, 0:n], func=AF.Silu),
                                   reads=[t_gc], writes=[t_gc])
                              S.op("dve", lambda h, n=n, a=a, b=b, f=f, psu=psu: h.tensor_tensor(
                                  out=actT[:, f, a - g0:b - g0], in0=gc[:, 0:n], in1=psu[:, 0:n], op=ALU.mult),
                                  reads=[t_gc, tu], writes=[t_act])
                  for oc in range(8):
                      wd = wdb[wdi % 2]; t_wd = t_wdb[wdi % 2]; wdi += 1
                      S.dma("pool", wd, wd_src[:, :, oc * 128:(oc + 1) * 128], writes=[t_wd])
                      for (a, b) in gtiles:
                          n = b - a
                          pso, to = K.ps()
                          for f in range(NF):
                              S.op("pe", lambda h, f=f, a=a, b=b, n=n, pso=pso, wd=wd: h.matmul(
                                  pso[:, 0:n], lhsT=wd[:, f, :], rhs=actT[:, f, a - g0:b - g0],
                                  start=(f == 0), stop=(f == NF - 1)),
                                  reads=[t_wd, t_act], writes=[to], inc=(f == NF - 1))
                          a2 = max(a, PAD)
                          tl = tile_of(a2, b)
                          S.op("dve", lambda h, a2=a2, a=a, b=b, pso=pso, oc=oc: h.tensor_tensor(
                              out=xT[:, oc, a2:b], in0=xT[:, oc, a2:b], in1=pso[:, a2 - a:b - a], op=ALU.add),
                              reads=[to] + tl, writes=tl)

              do_group(g0_, g1_)
            for f0 in range(0, NF, 4):
                nf = min(4, NF - f0)
                pst, pt_ = K.ps()
                for ff in range(nf):
                    f = f0 + ff
                    S.op("pe", lambda h, pst=pst, ff=ff, f=f: h.transpose(
                        out=pst[0:34, ff * 128:(ff + 1) * 128], in_=stg[:, f, :], identity=ident[:]),
                        reads=[t_stg, t_id], writes=[pt_], inc=(ff == nf - 1))
                S.op("act", lambda h, pst=pst, f0=f0, nf=nf: h.copy(
                    out=rows[:, f0 * 128:(f0 + nf) * 128], in_=pst[0:34, 0:nf * 128]),
                    reads=[pt_], writes=[t_rows])
            S.dma("sp", fC_p[l], rows[0:2, :], reads=[t_rows])
            S.dma("sp", fC_s[l].rearrange("s k n -> (s k) n"), rows[2:34, :], reads=[t_rows])
            S.barrier()

        Y_BANKS = [K.psb[6], K.psb[7]]

        def solve_blocks(nb, B0, Ca, Cb, Ba, Bb, Pa, Pb, t_B0, t_C, t_B, t_P):
            ngr = (nb + 7) // 8
            def grp(g):
                return range(g * 8, min(nb, g * 8 + 8))
            C = [Ca, Cb]; B = [B0, Ba, Bb]
            for g in range(ngr):
                pst, pt_ = K.ps()
                bl = list(grp(g))
                for i, b_ in enumerate(bl):
                    S.op("pe", lambda h, pst=pst, i=i, b_=b_: h.transpose(
                        out=pst[0:64, i * 64:(i + 1) * 64], in_=B0[:, b_, :], identity=ident[0:64, 0:64]),
                        reads=[t_B0, t_id], writes=[pt_], inc=(i == len(bl) - 1))
                S.op("act", lambda h, pst=pst, bl=bl: h.copy(
                    out=R(Ca[:, bl[0]:bl[-1] + 1, :]), in_=pst[0:64, 0:len(bl) * 64].rearrange("p (b n) -> p b n", n=64)),
                    reads=[pt_], writes=[t_C[0]])
            S.op("dve", lambda h: h.tensor_tensor(
                out=R(Pa[:, 0:nb, :]), in0=B0[:, 0:nb, :], in1=ident[0:64, 0:64].unsqueeze(1).to_broadcast([64, nb, 64]),
                op=ALU.add), reads=[t_B0, t_id], writes=[t_P[0]])
            Bcur, tBcur = B0, t_B0
            Ccur, tCcur = Ca, t_C[0]
            Pcur, tPcur, Pn, tPn = Pa, t_P[0], Pb, t_P[1]
            Bn = [Ba, Bb]; tBn = [t_B[0], t_B[1]]
            Cn = [Cb, Ca]; tCn = [t_C[1], t_C[0]]
            for i in range(1, 6):
                Cnew, tCnew = Cn[(i - 1) % 2], tCn[(i - 1) % 2]
                Bnew, tBnew = Bn[(i - 1) % 2], tBn[(i - 1) % 2]
                for g in range(ngr):
                    pst, pt_ = K.ps()
                    bl = list(grp(g))
                    for k_, b_ in enumerate(bl):
                        S.op("pe", lambda h, pst=pst, k_=k_, b_=b_, Bcur=Bcur, Ccur=Ccur: h.matmul(
                            pst[0:64, k_ * 64:(k_ + 1) * 64], lhsT=R(Bcur[:, b_, :]), rhs=R(Ccur[:, b_, :]),
                            start=True, stop=True), reads=[tBcur, tCcur], writes=[pt_], inc=(k_ == len(bl) - 1))
                    S.op("act", lambda h, pst=pst, bl=bl, Cnew=Cnew: h.copy(
                        out=R(Cnew[:, bl[0]:bl[-1] + 1, :]),
                        in_=pst[0:64, 0:len(bl) * 64].rearrange("p (b n) -> p b n", n=64)),
                        reads=[pt_], writes=[tCnew])
                if i < 5:
                    for g in range(ngr):
                        pst, pt_ = K.ps()
                        bl = list(grp(g))
                        for k_, b_ in enumerate(bl):
                            S.op("pe", lambda h, pst=pst, k_=k_, b_=b_, Bcur=Bcur, Ccur=Ccur: h.matmul(
                                pst[0:64, k_ * 64:(k_ + 1) * 64], lhsT=R(Ccur[:, b_, :]), rhs=R(Bcur[:, b_, :]),
                                start=True, stop=True), reads=[tBcur, tCcur], writes=[pt_], inc=(k_ == len(bl) - 1))
                        S.op("dve", lambda h, pst=pst, bl=bl, Bnew=Bnew: h.tensor_copy(
                            out=R(Bnew[:, bl[0]:bl[-1] + 1, :]),
                            in_=pst[0:64, 0:len(bl) * 64].rearrange("p (b n) -> p b n", n=64)),
                            reads=[pt_], writes=[tBnew])
                for g in range(ngr):
                    pst, pt_ = K.ps()
                    bl = list(grp(g))
                    for k_, b_ in enumerate(bl):
                        S.op("pe", lambda h, pst=pst, k_=k_, b_=b_, Cnew=Cnew, Pcur=Pcur: h.matmul(
                            pst[0:64, k_ * 64:(k_ + 1) * 64], lhsT=R(Cnew[:, b_, :]), rhs=R(Pcur[:, b_, :]),
                            start=True, stop=True), reads=[tCnew, tPcur], writes=[pt_], inc=(k_ == len(bl) - 1))
                    S.op("dve", lambda h, pst=pst, bl=bl, Pn=Pn, Pcur=Pcur: h.tensor_tensor(
                        out=R(Pn[:, bl[0]:bl[-1] + 1, :]), in0=Pcur[:, bl[0]:bl[-1] + 1, :],
                        in1=pst[0:64, 0:len(bl) * 64].rearrange("p (b n) -> p b n", n=64), op=ALU.add),
                        reads=[pt_, tPcur], writes=[tPn])
                Bcur, tBcur = Bnew, tBnew
                Ccur, tCcur = Cnew, tCnew
                Pcur, tPcur, Pn, tPn = Pn, tPn, Pcur, tPcur
            return Pcur, tPcur

        def seq_block(blk, dv, pr, fcols, states, ops, ybank, ycols, yparts, tcols=(0, 64)):
            p0, p1 = pr
            c0, c1 = fcols
            (ns, t_ns), (rs_, t_rs_) = ops["ns"], ops["rs"]
            (V, t_V), (Kh, t_Kh), (nPh, t_nPh) = ops["V"], ops["Kh"], ops["nPh"]
            (Ank, t_Ank), (Ark, t_Ark), (nArp, t_nArp), (TT, t_TT) = ops["Ank"], ops["Ark"], ops["nArp"], ops["TT"]
            (Xs, t_Xs), (Es, t_Es) = ops["Xs"], ops["Es"]
            single = len(states) == 1
            psx, tpx = K.ps()
            if single:
                stt = states[0]
                S.op("pe", lambda h: h.matmul(psx[0:64, 0:dv], lhsT=Ank[:, blk, :], rhs=V[:, blk, 0:dv],
                                              start=True, stop=False), reads=[t_Ank, t_V], writes=[tpx], inc=False)
                S.op("pe", lambda h: h.matmul(psx[0:64, 0:dv], lhsT=ns[p0:p1, c0:c1], rhs=stt["Z"],
                                              start=False, stop=True), reads=[t_ns, stt["tZ"]], writes=[tpx])
                S.op("act", lambda h: h.copy(out=Xs[:, 0:dv], in_=psx[0:64, 0:dv]), reads=[tpx], writes=[t_Xs])
            else:
                S.op("pe", lambda h: h.matmul(psx[0:dv, 0:64], lhsT=V[:, blk, 0:dv], rhs=Ank[:, blk, :],
                                              start=True, stop=False), reads=[t_Ank, t_V], writes=[tpx], inc=False)
                for si, stt in enumerate(states):
                    S.op("pe", lambda h, stt=stt, si=si: h.matmul(
                        psx[0:dv, stt["lo"]:stt["hi"]], lhsT=stt["Z"], rhs=ns[p0:p1, c0 + stt["lo"]:c0 + stt["hi"]],
                        start=False, stop=(si == len(states) - 1)), reads=[t_ns, stt["tZ"]], writes=[tpx],
                        inc=(si == len(states) - 1))
                XT, t_XT = ops["XT"]
                S.op("act", lambda h: h.copy(out=XT[0:dv, :], in_=psx[0:dv, 0:64]), reads=[tpx], writes=[t_XT])
                psx2, tpx2 = K.ps()
                S.op("pe", lambda h: h.transpose(out=psx2[0:64, 0:dv], in_=XT[0:dv, :], identity=ident[0:dv, 0:dv]),
                     reads=[t_XT, t_id], writes=[tpx2])
                S.op("act", lambda h: h.copy(out=Xs[:, 0:dv], in_=psx2[0:64, 0:dv]), reads=[tpx2], writes=[t_Xs])
            pse, tpe = K.ps()
            S.op("pe", lambda h: h.matmul(pse[0:64, 0:dv], lhsT=TT[:, blk, :], rhs=Xs[:, 0:dv], start=True, stop=True),
                 reads=[t_TT, t_Xs], writes=[tpe])
            S.op("dve", lambda h: h.tensor_copy(out=Es[:, 0:dv], in_=pse[0:64, 0:dv]), reads=[tpe], writes=[t_Es])
            yb, tyb = ybank
            q0, q1 = yparts
            tl0, tl1 = tcols
            S.op("pe", lambda h: h.matmul(yb[q0:q1, ycols[0] + tl0:ycols[0] + tl1], lhsT=V[:, blk, 0:dv],
                                          rhs=Ark[:, blk, tl0:tl1],
                                          start=True, stop=False), reads=[t_V, t_Ark], writes=[tyb], inc=False)
            for stt in states:
                S.op("pe", lambda h, stt=stt: h.matmul(
                    yb[q0:q1, ycols[0] + stt["lo"]:ycols[0] + stt["hi"]], lhsT=stt["Z"],
                    rhs=rs_[p0:p1, c0 + stt["lo"]:c0 + stt["hi"]], start=False, stop=False),
                    reads=[t_rs_, stt["tZ"]], writes=[tyb], inc=False)
            S.op("pe", lambda h: h.matmul(yb[q0:q1, ycols[0] + tl0:ycols[0] + tl1], lhsT=Es[:, 0:dv],
                                          rhs=nArp[:, blk, tl0:tl1],
                                          start=False, stop=True), reads=[t_Es, t_nArp], writes=[tyb])
            for stt in states:
                psz, tpz = K.ps()
                if stt.get("mask") is not None:
                    Km, t_Km = ops["Km"]; nPm, t_nPm = ops["nPm"]
                    S.op("pool", lambda h, stt=stt: h.tensor_scalar(out=Km, in0=Kh[:, blk, :], scalar1=stt["mask"],
                                                                   scalar2=None, op0=ALU.mult),
                         reads=[t_Kh, ops["t_mask"]], writes=[t_Km])
                    S.op("pool", lambda h, stt=stt: h.tensor_scalar(out=nPm, in0=nPh[:, blk, :], scalar1=stt["mask"],
                                                                   scalar2=None, op0=ALU.mult),
                         reads=[t_nPh, ops["t_mask"]], writes=[t_nPm])
                    lk, tlk, lp, tlp = Km, t_Km, nPm, t_nPm
                else:
                    lk, tlk, lp, tlp = Kh[:, blk, :], t_Kh, nPh[:, blk, :], t_nPh
                S.op("pe", lambda h, psz=psz, lk=lk: h.matmul(psz[p0:p1, 0:dv], lhsT=lk, rhs=V[:, blk, 0:dv],
                                                              start=True, stop=False),
                     reads=[tlk, t_V], writes=[tpz], inc=False)
                S.op("pe", lambda h, psz=psz, lp=lp: h.matmul(psz[p0:p1, 0:dv], lhsT=lp, rhs=Es[:, 0:dv],
                                                              start=False, stop=True),
                     reads=[tlp, t_Es], writes=[tpz])
                S.op("dve", lambda h, stt=stt, psz=psz: h.scalar_tensor_tensor(
                    out=stt["Z"], in0=stt["Z"], scalar=stt["dc"], in1=psz[p0:p1, 0:dv], op0=ALU.mult, op1=ALU.add),
                    reads=[stt["tZ"], stt["t_dc"], tpz], writes=[stt["tZ"]])

        def gdn_layer(l):
            j = l // 2
            o = [0]

            def A(n):
                r = arena[:, o[0]:o[0] + n]; o[0] += n
                assert o[0] <= ARENA, o[0]
                return r
            hn = A(8704).bitcast(BF16).rearrange("p (c n) -> p c n", c=8); t_hn = Trk()
            qt = A(T); qt2 = A(T); t_qt = Trk(); t_qt2 = Trk()
            sel = A(1024).rearrange("p (h m) -> p h m", h=8); t_sel = Trk()
            wq = A(2048).bitcast(BF16).rearrange("p (c g n) -> p c g n", c=8, g=4); t_wq = Trk()
            wo = A(512).bitcast(BF16); t_wo = Trk()
            abw = A(64).bitcast(BF16).rearrange("p (c n) -> p c n", c=8); t_abw = Trk()
            pre = A(515); t_pre = Trk()
            post = A(1536).rearrange("p (g n) -> p g n", g=3); t_post = [Trk(), Trk(), Trk()]
            zs = A(512); t_zs = Trk()
            sqb = A(256).bitcast(BF16); t_sqb = Trk()
            rn = A(512); t_rn = Trk()
            ns = A(512); t_ns = Trk()
            rsq = A(512); t_rsq = Trk()
            og = A(256).bitcast(BF16); t_og = Trk()
            pres = A(112).rearrange("p (s k) -> p s k", s=16); t_pres = Trk()
            dcs = A(32); t_dcs = Trk()
            hist = A(144).rearrange("p (g k) -> p g k", g=3); t_hist = Trk()
            hrow = A(384)[0:48, :]; t_hrow = Trk()
            stg = A(153).rearrange("p (g k) -> p g k", g=3); t_stg = Trk()
            srow = A(384)[0:51, :]; t_srow = Trk()
            ones1 = A(64).bitcast(BF16); t_ones1 = Trk()
            scr_ = A(2048)
            sq_scr = scr_.bitcast(BF16).rearrange("p (c n) -> p c n", c=8)
            Vt = scr_[0:64, 0:1024].rearrange("p (b n) -> p b n", b=8); t_Vt = Trk()
            Kh = scr_[0:64, 1024:2048].rearrange("p (b n) -> p b n", b=8); t_Kh = t_Vt
            nPh = A(1024)[0:64, :].rearrange("p (b n) -> p b n", b=8); t_nPh = Trk()
            v64 = lambda: A(512)[0:64, :].rearrange("p (b n) -> p b n", b=8)
            M1 = v64(); M2 = v64(); t_M = [Trk(), Trk()]
            orr = [0]

            def v64r():
                r_ = arenaR[0:64, orr[0]:orr[0] + 512].rearrange("p (b n) -> p b n", b=8); orr[0] += 512
                assert orr[0] <= ARENAR
                return r_
            Ank = v64(); B0 = v64r(); Ark = v64(); nArp = v64()
            t_Ank, t_B0, t_Ark, t_nArp = Trk(), Trk(), Trk(), Trk()
            Ca = v64r(); Cb = v64r(); Ba = v64r(); Bb = v64r()
            Pa_ = v64r(); Pb_ = v64r(); t_Pab = [Trk(), Trk()]
            t_C = [Trk(), Trk()]; t_B = [Trk(), Trk()]
            Xs = A(128)[0:64, :]; Es = A(128)[0:64, :]; XT = A(64); t_Xs, t_Es, t_XT = Trk(), Trk(), Trk()
            Z = A(128); t_Z = Trk()
            Zs = A(512).rearrange("p (s n) -> p s n", s=4); t_Zs = [Trk() for _ in range(4)]
            Km = A(128)[0:64, :]; nPm = A(128)[0:64, :]; t_Km, t_nPm = Trk(), Trk()
            tm1 = A(256)[0:64, :].rearrange("p (b q h) -> p b q h", b=8, q=4)
            tm2 = A(256)[0:64, :].rearrange("p (b q h) -> p b q h", b=8, q=4)
            t_tm = Trk()
            kdb = A(16)[0:64, :].rearrange("p (b q) -> p b q", b=8); t_kdb = Trk()

            S.op("pool", lambda h: h.memset(ones1, 1.0), writes=[t_ones1])
            S.op("pool", lambda h: h.memset(sel, 0.0), writes=[t_sel])
            for base in (0, 32):
                S.op("pool", lambda h, base=base: h.affine_select(
                    out=sel[base:base + 32], in_=sel[base:base + 32], pattern=[[-1, 8], [0, 128]],
                    compare_op=ALU.not_equal, fill=1.0, base=0, channel_multiplier=1),
                    reads=[t_sel], writes=[t_sel])
            S.op("pool", lambda h: h.memset(qt2[32:64, :], 1.0), writes=[t_qt2])
            S.op("pool", lambda h: h.memset(qt2[32:64, 0:S0].rearrange("p (c k) -> p c k", k=64)[:, :, 0:1], 0.0),
                 writes=[t_qt2])
            S.op("pool", lambda h: h.memset(qt2[32:64, S0:T].rearrange("p (c k) -> p c k", k=4)[:, :, 0:1], 0.0),
                 writes=[t_qt2])
            S.op("pool", lambda h: h.memset(qt[:], 0.0), writes=[t_qt])
            win = gdn_w_in[j].rearrange("(c p) n -> p c n", p=128)
            S.dma("pool", abw, win[:, :, 4096:4112], writes=[t_abw])
            rmsnorm(hn, 0, "norm_mix", l, 0, T, sq_scr, t_Vt, rn, t_rn, t_hn)
            S.op("act", lambda h: h.activation(out=dcs[64:72, 24:25], in_=pcol("gdn_A", j)[64:72], func=AF.Exp),
                 reads=[t_pt], writes=[t_dcs])
            S.op("dve", lambda h: h.tensor_scalar(out=dcs[64:72, 24:25], in0=dcs[64:72, 24:25], scalar1=-1.0,
                                                  scalar2=None, op0=ALU.mult), reads=[t_dcs], writes=[t_dcs])
            for (a, b) in TILES:
                n = b - a
                pst, pt_ = K.ps()
                for c in range(8):
                    S.op("pe", lambda h, c=c, a=a, b=b, n=n, pst=pst: h.matmul(
                        pst[64:72, 0:n], lhsT=abw[:, c, 0:8], rhs=hn[:, c, a:b], start=(c == 0), stop=(c == 7)),
                        reads=[t_abw, t_hn], writes=[pt_], inc=(c == 7))
                for c in range(8):
                    S.op("pe", lambda h, c=c, a=a, b=b, n=n, pst=pst: h.matmul(
                        pst[0:8, 0:n], lhsT=abw[:, c, 8:16], rhs=hn[:, c, a:b], start=(c == 0), stop=(c == 7)),
                        reads=[t_abw, t_hn], writes=[pt_], inc=(c == 7))
                S.op("act", lambda h, a=a, b=b, n=n, pst=pst: h.activation(
                    out=qt2[64:72, a:b], in_=pst[64:72, 0:n], func=AF.Exp, bias=pcol("gdn_dt", j)[64:72], scale=1.0),
                    reads=[pt_, t_pt], writes=[t_qt2])
                S.op("act", lambda h, a=a, b=b: h.activation(
                    out=qt2[64:72, a:b], in_=qt2[64:72, a:b], func=AF.Ln, bias=cst[64:72, 2:3], scale=1.0),
                    reads=[t_qt2, t_cst], writes=[t_qt2])
                S.op("dve", lambda h, a=a, b=b: h.tensor_scalar(
                    out=qt2[64:72, a:b], in0=qt2[64:72, a:b], scalar1=dcs[64:72, 24:25], scalar2=None, op0=ALU.mult),
                    reads=[t_qt2, t_dcs], writes=[t_qt2])
                S.op("act", lambda h, a=a, b=b, n=n, pst=pst: h.activation(
                    out=qt[0:8, a:b], in_=pst[0:8, 0:n], func=AF.Sigmoid), reads=[pt_], writes=[t_qt])
            S.dma("sp", qt[96:104, :], qt[0:8, :], reads=[t_qt], writes=[t_qt])
            S.dma("sp", qt[32:40, :], qt2[64:72, :], reads=[t_qt2], writes=[t_qt])
            S.dma("sp", qt[0:8, :], qt2[64:72, :], reads=[t_qt2, t_qt], writes=[t_qt])
            S.op("dve", lambda h: h.tensor_tensor_scan(out=qt[32:40, :], data0=qt2[32:40, :], data1=qt[32:40, :],
                                                       initial=0.0, op0=ALU.mult, op1=ALU.add),
                 reads=[t_qt, t_qt2], writes=[t_qt])
            S.dma("sp", qt2[0:8, :], qt[32:40, :], reads=[t_qt], writes=[t_qt2])
            S.op("dve", lambda h: h.tensor_tensor(out=qt[0:8, :], in0=qt2[0:8, :], in1=qt[0:8, :], op=ALU.subtract),
                 reads=[t_qt, t_qt2], writes=[t_qt])
            S.op("dve", lambda h: h.tensor_tensor(
                out=qt2[32:40, 0:S0].rearrange("p (c k) -> p c k", k=64),
                in0=qt[32:40, 0:S0].rearrange("p (c k) -> p c k", k=64)[:, :, 63:64].to_broadcast([8, 33, 64]),
                in1=qt[32:40, 0:S0].rearrange("p (c k) -> p c k", k=64), op=ALU.subtract),
                reads=[t_qt], writes=[t_qt2])
            S.op("dve", lambda h: h.tensor_tensor(
                out=qt2[32:40, S0:T].rearrange("p (c k) -> p c k", k=4),
                in0=qt[32:40, S0:T].rearrange("p (c k) -> p c k", k=4)[:, :, 3:4].to_broadcast([8, 16, 4]),
                in1=qt[32:40, S0:T].rearrange("p (c k) -> p c k", k=4), op=ALU.subtract),
                reads=[t_qt], writes=[t_qt2])
            S.op("act", lambda h: h.activation(out=qt2[32:40, :], in_=qt2[32:40, :], func=AF.Exp),
                 reads=[t_qt2], writes=[t_qt2])
            S.op("act", lambda h: h.activation(out=qt2[64:72, :], in_=qt2[64:72, :], func=AF.Exp),
                 reads=[t_qt2], writes=[t_qt2])

            cwq = lambda g, h_, k_: pcol("gdn_cw", (j * 4 + k_) * 24 + g * 8 + h_)
            wout = gdn_w_out[j]
            ybi = 0
            def do_head(hd):
                nonlocal ybi
                for g in range(4):
                    S.dma("pool", wq[:, :, g, :], win[:, :, g * 1024 + hd * 128:g * 1024 + (hd + 1) * 128],
                          writes=[t_wq])
                S.dma("pool", wo, wout[hd * 128:(hd + 1) * 128, :], writes=[t_wo])
                for g in range(3):
                    S.dma("sp", hrow[:, g * 128:(g + 1) * 128],
                          gdn_conv_in[j].rearrange("s k n -> (s k) n")[:, g * 1024 + hd * 128:g * 1024 + (hd + 1) * 128],
                          writes=[t_hrow])
                pst, pt_ = K.ps()
                for g in range(3):
                    S.op("pe", lambda h, g=g, pst=pst: h.transpose(
                        out=pst[:, g * 48:(g + 1) * 48], in_=hrow[:, g * 128:(g + 1) * 128], identity=ident[0:48, 0:48]),
                        reads=[t_hrow, t_id], writes=[pt_], inc=(g == 2))
                S.op("dve", lambda h, pst=pst: h.tensor_copy(out=hist, in_=pst[:, 0:144].rearrange("p (g k) -> p g k", g=3)),
                     reads=[pt_], writes=[t_hist])
                S.op("dve", lambda h: h.memset(Z, 0.0), writes=[t_Z])
                def do_tile(ti, a, b):
                    nonlocal ybi
                    n = b - a
                    nb = n // 64
                    last = (ti == len(TILES) - 1)
                    for g in range(4):
                        psp, tpp = K.ps()
                        for c in range(8):
                            S.op("pe", lambda h, c=c, g=g, a=a, b=b, n=n, psp=psp: h.matmul(
                                psp[:, 0:n], lhsT=wq[:, c, g, :], rhs=hn[:, c, a:b], start=(c == 0), stop=(c == 7)),
                                reads=[t_wq, t_hn], writes=[tpp], inc=(c == 7))
                        if g == 3:
                            pass
                            S.op("act", lambda h, n=n, psp=psp: h.activation(out=zs[:, 0:n], in_=psp[:, 0:n], func=AF.Silu),
                                 reads=[tpp], writes=[t_zs])
                            continue
                        if ti == 0:
                            S.op("dve", lambda h: h.memset(pre[:, 0:3], 0.0), writes=[t_pre])
                        else:
                            S.op("dve", lambda h, g=g: h.tensor_copy(out=pre[:, 0:3], in_=stg[:, g, 48:51]),
                                 reads=[t_stg], writes=[t_pre])
                        S.op("act", lambda h, n=n, psp=psp: h.copy(out=pre[:, 3:3 + n], in_=psp[:, 0:n]),
                             reads=[tpp], writes=[t_pre])
                        npz = n if not last else S0 - a
                        S.op("pool", lambda h, g=g, npz=npz: h.tensor_copy(out=stg[:, g, 48:51], in_=pre[:, npz:npz + 3]),
                             reads=[t_pre], writes=[t_stg])
                        tp = t_post[g]
                        S.op("dve", lambda h, g=g, npz=npz: h.tensor_scalar(
                            out=post[:, g, 0:npz], in0=pre[:, 3:3 + npz], scalar1=cwq(g, hd, 3), scalar2=None,
                            op0=ALU.mult), reads=[t_pre, t_pt], writes=[tp])
                        for k_ in range(3):
                            S.op("dve", lambda h, g=g, npz=npz, k_=k_: h.scalar_tensor_tensor(
                                out=post[:, g, 0:npz], in0=pre[:, k_:k_ + npz], scalar=cwq(g, hd, k_),
                                in1=post[:, g, 0:npz], op0=ALU.mult, op1=ALU.add), reads=[t_pre, t_pt, tp], writes=[tp])
                        if last:
                            so = S0 - a
                            S.op("dve", lambda h, g=g: h.tensor_copy(
                                out=pres[:, :, 0:3], in_=hist[:, g, :].rearrange("p (s k) -> p s k", s=16)),
                                reads=[t_hist], writes=[t_pres])
                            S.op("dve", lambda h, so=so: h.tensor_copy(
                                out=pres[:, :, 3:7], in_=pre[:, 3 + so:3 + so + 64].rearrange("p (s k) -> p s k", s=16)),
                                reads=[t_pre], writes=[t_pres])
                            ps_ = post[:, g, so:so + 64].rearrange("p (s k) -> p s k", s=16)
                            S.op("dve", lambda h, g=g, ps_=ps_: h.tensor_scalar(
                                out=ps_, in0=pres[:, :, 3:7], scalar1=cwq(g, hd, 3), scalar2=None, op0=ALU.mult),
                                reads=[t_pres, t_pt], writes=[tp])
                            for k_ in range(3):
                                S.op("dve", lambda h, g=g, ps_=ps_, k_=k_: h.scalar_tensor_tensor(
                                    out=ps_, in0=pres[:, :, k_:k_ + 4], scalar=cwq(g, hd, k_), in1=ps_,
                                    op0=ALU.mult, op1=ALU.add), reads=[t_pres, t_pt, tp], writes=[tp])
                            S.op("pool", lambda h, g=g: h.tensor_copy(
                                out=stg[:, g, 0:48].rearrange("p (s k) -> p s k", s=16), in_=pres[:, :, 4:7]),
                                reads=[t_pres], writes=[t_stg])
                        S.op("act", lambda h, g=g, n=n: h.activation(out=post[:, g, 0:n], in_=post[:, g, 0:n], func=AF.Silu),
                             reads=[tp], writes=[tp])
                    for g in range(2):
                        tp = t_post[g]
                        S.op("act", lambda h, g=g, n=n: h.activation(out=sqb[:, 0:n], in_=post[:, g, 0:n], func=AF.Square),
                             reads=[tp], writes=[t_sqb])
                        pss, tps = K.ps()
                        S.op("pe", lambda h, n=n, pss=pss: h.matmul(pss[:, 0:n], lhsT=ones1, rhs=sqb[:, 0:n],
                                                                    start=True, stop=True),
                             reads=[t_sqb, t_ones1], writes=[tps])
                        S.op("act", lambda h, n=n, pss=pss: h.activation(out=rn[:, 0:n], in_=pss[:, 0:n], func=AF.Sqrt,
                                                                         bias=cst[:, 0:1], scale=1.0),
                             reads=[tps, t_cst], writes=[t_rn])
                        S.op("dve", lambda h, n=n: h.reciprocal(out=rn[:, 0:n], in_=rn[:, 0:n]), reads=[t_rn], writes=[t_rn])
                        if g == 0:
                            S.op("dve", lambda h, n=n: h.scalar_tensor_tensor(
                                out=post[:, 0, 0:n], in0=post[:, 0, 0:n], scalar=float(128 ** -0.5), in1=rn[:, 0:n],
                                op0=ALU.mult, op1=ALU.mult), reads=[tp, t_rn], writes=[tp])
                        else:
                            S.op("dve", lambda h, n=n: h.tensor_tensor(out=post[:, 1, 0:n], in0=post[:, 1, 0:n],
                                                                       in1=rn[:, 0:n], op=ALU.mult),
                                 reads=[tp, t_rn], writes=[tp])
                    qn, kn, vv = post[:, 0, :], post[:, 1, :], post[:, 2, :]
                    for (src, dst, tsrc) in ((qt, tm1, t_qt), (qt2, tm2, t_qt2)):
                        for b0_ in range(0, nb, 4):
                            nn = min(4, nb - b0_)
                            pst, pt_ = K.ps()
                            for bi in range(nn):
                                S.op("pe", lambda h, bi=bi, b0_=b0_, src=src, pst=pst, a=a: h.transpose(
                                    out=pst[0:64, bi * 128:(bi + 1) * 128],
                                    in_=src[:, a + (b0_ + bi) * 64:a + (b0_ + bi + 1) * 64], identity=ident[:]),
                                    reads=[tsrc, t_id], writes=[pt_], inc=(bi == nn - 1))
                            S.op("dve", lambda h, pst=pst, dst=dst, b0_=b0_, nn=nn: h.tensor_copy(
                                out=dst[:, b0_:b0_ + nn, :, :],
                                in_=pst[0:64, 0:nn * 128].rearrange("p (b q r) -> p b q r", b=nn, q=4)[:, :, :, 0:8]),
                                reads=[pt_], writes=[t_tm])
                    S.op("dve", lambda h, nb=nb: h.tensor_tensor(out=kdb[:, 0:nb, 0], in0=tm2[:, 0:nb, 1, hd],
                                                                 in1=tm1[:, 0:nb, 3, hd], op=ALU.mult),
                         reads=[t_tm], writes=[t_kdb])
                    S.op("dve", lambda h, nb=nb: h.scalar_tensor_tensor(
                        out=kdb[:, 0:nb, 1], in0=kdb[:, 0:nb, 0], scalar=-1.0, in1=tm2[:, 0:nb, 2, hd],
                        op0=ALU.mult, op1=ALU.mult), reads=[t_tm, t_kdb], writes=[t_kdb])
                    psg1, tg1 = K.ps()
                    S.op("pe", lambda h, a=a, b=b, n=n, psg1=psg1: h.matmul(
                        psg1[:, 0:n], lhsT=sel[0:8, hd, :], rhs=qt[0:8, a:b], start=True, stop=True),
                        reads=[t_sel, t_qt], writes=[tg1])
                    psg2, tg2 = K.ps()
                    S.op("pe", lambda h, a=a, b=b, n=n, psg2=psg2: h.matmul(
                        psg2[:, 0:n], lhsT=sel[32:40, hd, :], rhs=qt[32:40, a:b], start=True, stop=True),
                        reads=[t_sel, t_qt], writes=[tg2])
                    for (Mx, tMx, psg, tg, tri_p, tri_sm) in ((M1, t_M[0], psg1, tg1, tri_s, trib_s),
                                                              (M2, t_M[1], psg2, tg2, tri_i, trib_i)):
                        S.op("dve", lambda h, Mx=Mx, psg=psg, nb=nb: h.tensor_tensor(
                            out=Mx[:, 0:nb, :], in0=psg[0:64, 0:nb * 64].rearrange("p (b n) -> p b n", n=64),
                            in1=tm1[:, 0:nb, 1, hd:hd + 1].to_broadcast([64, nb, 64]), op=ALU.subtract),
                            reads=[tg, t_tm], writes=[tMx])
                        npb = nb - 1 if last else nb
                        if npb > 0:
                            S.op("dve", lambda h, Mx=Mx, npb=npb, tri_p=tri_p: h.tensor_tensor(
                                out=Mx[:, 0:npb, :], in0=Mx[:, 0:npb, :],
                                in1=tri_p.unsqueeze(1).to_broadcast([64, npb, 64]), op=ALU.mult),
                                reads=[tMx, t_tri], writes=[tMx])
                        if last:
                            S.op("dve", lambda h, Mx=Mx, nb=nb, tri_sm=tri_sm: h.tensor_tensor(
                                out=Mx[:, nb - 1, :], in0=Mx[:, nb - 1, :], in1=tri_sm, op=ALU.mult),
                                reads=[tMx, t_tri], writes=[tMx])
                        S.op("act", lambda h, Mx=Mx, nb=nb: h.activation(out=Mx[:, 0:nb, :], in_=Mx[:, 0:nb, :], func=AF.Exp),
                             reads=[tMx], writes=[tMx])
                        if npb > 0:
                            S.op("dve", lambda h, Mx=Mx, npb=npb, tri_p=tri_p: h.tensor_tensor(
                                out=Mx[:, 0:npb, :], in0=Mx[:, 0:npb, :],
                                in1=tri_p.unsqueeze(1).to_broadcast([64, npb, 64]), op=ALU.mult),
                                reads=[tMx, t_tri], writes=[tMx])
                        if last:
                            S.op("dve", lambda h, Mx=Mx, nb=nb, tri_sm=tri_sm: h.tensor_tensor(
                                out=Mx[:, nb - 1, :], in0=Mx[:, nb - 1, :], in1=tri_sm, op=ALU.mult),
                                reads=[tMx, t_tri], writes=[tMx])
                        S.op("dve", lambda h, Mx=Mx, nb=nb: h.tensor_tensor(
                            out=Mx[:, 0:nb, :], in0=Mx[:, 0:nb, :],
                            in1=tm1[:, 0:nb, 3, hd:hd + 1].to_broadcast([64, nb, 64]), op=ALU.mult),
                            reads=[tMx, t_tm], writes=[tMx])
                    S.op("act", lambda h, n=n, psg1=psg1: h.activation(out=ns[:, 0:n], in_=psg1[:, 0:n], func=AF.Exp),
                         reads=[tg1], writes=[t_ns])
                    S.op("dve", lambda h, n=n: h.tensor_tensor(out=ns[:, 0:n], in0=ns[:, 0:n], in1=kn[:, 0:n], op=ALU.mult),
                         reads=[t_ns, t_post[1]], writes=[t_ns])
                    S.op("act", lambda h, n=n, psg2=psg2: h.activation(out=rsq[:, 0:n], in_=psg2[:, 0:n], func=AF.Exp),
                         reads=[tg2], writes=[t_rsq])
                    npb = nb - 1 if last else nb
                    S.op("pool", lambda h, npb=npb: h.tensor_copy(
                        out=dcs[:, 0:npb], in_=rsq[:, 0:npb * 64].rearrange("p (b k) -> p b k", k=64)[:, :, 63]),
                        reads=[t_rsq], writes=[t_dcs])
                    if last:
                        so = S0 - a
                        S.op("pool", lambda h, so=so: h.tensor_copy(
                            out=dcs[:, 8:24], in_=rsq[:, so:so + 64].rearrange("p (s k) -> p s k", k=4)[:, :, 3]),
                            reads=[t_rsq], writes=[t_dcs])
                    S.op("dve", lambda h, n=n: h.tensor_tensor(out=rsq[:, 0:n], in0=rsq[:, 0:n], in1=qn[:, 0:n], op=ALU.mult),
                         reads=[t_rsq, t_post[0], t_dcs], writes=[t_rsq])
                    for (srcg, dst, tdst, sc) in ((2, Vt, t_Vt, None), (1, Kh, t_Kh, 0)):
                        for b0_ in range(0, nb, 4):
                            nn = min(4, nb - b0_)
                            pst, pt_ = K.ps()
                            for bi in range(nn):
                                S.op("pe", lambda h, bi=bi, b0_=b0_, srcg=srcg, pst=pst: h.transpose(
                                    out=pst[0:64, bi * 128:(bi + 1) * 128],
                                    in_=post[:, srcg, (b0_ + bi) * 64:(b0_ + bi + 1) * 64], identity=ident[:]),
                                    reads=[t_post[srcg], t_id], writes=[pt_], inc=(bi == nn - 1))
                            pv = pst[0:64, 0:nn * 128].rearrange("p (b n) -> p b n", n=128)
                            if sc is None:
                                S.op("act", lambda h, pv=pv, b0_=b0_, nn=nn: h.copy(out=Vt[:, b0_:b0_ + nn, :], in_=pv),
                                     reads=[pt_], writes=[t_Vt])
                            else:
                                S.op("dve", lambda h, pv=pv, b0_=b0_, nn=nn: h.tensor_tensor(
                                    out=Kh[:, b0_:b0_ + nn, :], in0=pv,
                                    in1=kdb[:, b0_:b0_ + nn, 0:1].to_broadcast([64, nn, 128]), op=ALU.mult),
                                    reads=[pt_, t_kdb], writes=[t_Kh])
                                S.op("dve", lambda h, pv=pv, b0_=b0_, nn=nn: h.tensor_tensor(
                                    out=nPh[:, b0_:b0_ + nn, :], in0=pv,
                                    in1=kdb[:, b0_:b0_ + nn, 1:2].to_broadcast([64, nn, 128]), op=ALU.mult),
                                    reads=[pt_, t_kdb], writes=[t_nPh])
                    pkk, tkk = K.ps()
                    pqk, tqk = K.ps()
                    for bi in range(nb):
                        S.op("pe", lambda h, bi=bi, pkk=pkk: h.matmul(
                            pkk[0:64, bi * 64:(bi + 1) * 64], lhsT=kn[:, bi * 64:(bi + 1) * 64],
                            rhs=kn[:, bi * 64:(bi + 1) * 64], start=True, stop=True),
                            reads=[t_post[1]], writes=[tkk], inc=(bi == nb - 1))
                    for bi in range(nb):
                        S.op("pe", lambda h, bi=bi, pqk=pqk: h.matmul(
                            pqk[0:64, bi * 64:(bi + 1) * 64], lhsT=kn[:, bi * 64:(bi + 1) * 64],
                            rhs=qn[:, bi * 64:(bi + 1) * 64], start=True, stop=True),
                            reads=[t_post[1], t_post[0]], writes=[tqk], inc=(bi == nb - 1))
                    v3 = lambda p_, nb=nb: p_[0:64, 0:nb * 64].rearrange("p (b n) -> p b n", n=64)
                    al = tm2[:, 0:nb, 2, hd:hd + 1].to_broadcast([64, nb, 64])
                    S.op("dve", lambda h, pkk=pkk, nb=nb: h.tensor_tensor(out=Ank[:, 0:nb, :], in0=v3(pkk), in1=M1[:, 0:nb, :],
                                                                          op=ALU.mult),
                         reads=[tkk, t_M[0]], writes=[t_Ank])
                    S.op("dve", lambda h, nb=nb, al=al: h.scalar_tensor_tensor(
                        out=R(B0[:, 0:nb, :]), in0=Ank[:, 0:nb, :], scalar=-1.0, in1=al, op0=ALU.mult, op1=ALU.mult),
                        reads=[t_Ank, t_tm], writes=[t_B0])
                    S.op("dve", lambda h, pqk=pqk, nb=nb: h.tensor_tensor(out=Ark[:, 0:nb, :], in0=v3(pqk), in1=M2[:, 0:nb, :],
                                                                          op=ALU.mult),
                         reads=[tqk, t_M[1]], writes=[t_Ark])
                    S.op("dve", lambda h, nb=nb, al=al: h.scalar_tensor_tensor(
                        out=nArp[:, 0:nb, :], in0=Ark[:, 0:nb, :], scalar=-1.0, in1=al, op0=ALU.mult, op1=ALU.mult),
                        reads=[t_Ark, t_tm], writes=[t_nArp])
                    TT, t_TT = solve_blocks(nb, B0, Ca, Cb, Ba, Bb, Pa_, Pb_, t_B0, t_C, t_B, t_Pab)
                    ybank = Y_BANKS[ybi % 2]; ybi += 1
                    ops = dict(ns=(ns, t_ns), rs=(rsq, t_rsq), V=(Vt, t_Vt), Kh=(Kh, t_Kh), nPh=(nPh, t_nPh),
                               Ank=(Ank, t_Ank), Ark=(Ark, t_Ark), nArp=(nArp, t_nArp), TT=(TT, t_TT),
                               Xs=(Xs, t_Xs), Es=(Es, t_Es), XT=(XT, t_XT), Km=(Km, t_Km), nPm=(nPm, t_nPm),
                               t_mask=t_tri)
                    for bi in range(npb):
                        seq_block(bi, 128, (0, 128), (bi * 64, bi * 64 + 64),
                                  [dict(Z=Z, tZ=t_Z, lo=0, hi=64, dc=dcs[:, bi:bi + 1], t_dc=t_dcs)],
                                  ops, ybank, (bi * 64, bi * 64 + 64), (0, 128))
                    if last:
                        S.dma("sp", gS_p[j, hd], Z, reads=[t_Z])
                        so = S0 - a
                        bi = nb - 1
                        for sg in range(4):
                            S.dma("sp", Zs, gdn_S_in[j, sg * 4:sg * 4 + 4, hd].rearrange("s k v -> k s v"),
                                  writes=t_Zs)
                            sts = [dict(Z=Zs[:, s_, :], tZ=t_Zs[s_], lo=(sg * 4 + s_) * 4, hi=(sg * 4 + s_) * 4 + 4,
                                        dc=dcs[:, 8 + sg * 4 + s_:9 + sg * 4 + s_], t_dc=t_dcs,
                                        mask=rmk[:, sg * 4 + s_:sg * 4 + s_ + 1]) for s_ in range(4)]
                            seq_block(bi, 128, (0, 128), (so, so + 64), sts, ops, ybank, (so, so + 64), (0, 128),
                                      tcols=(sg * 16, sg * 16 + 16))
                            S.dma("sp", gS_s[j, sg * 4:sg * 4 + 4, hd].rearrange("s k v -> k s v"), Zs, reads=t_Zs)
                    yb, tyb = ybank
                    S.op("act", lambda h, n=n, yb=yb: h.activation(out=sqb[:, 0:n], in_=yb[:, 0:n], func=AF.Square),
                         reads=[tyb], writes=[t_sqb])
                    pss, tps = K.ps()
                    S.op("pe", lambda h, n=n, pss=pss: h.matmul(pss[:, 0:n], lhsT=ones1, rhs=sqb[:, 0:n], start=True, stop=True),
                         reads=[t_sqb, t_ones1], writes=[tps])
                    S.op("act", lambda h, n=n, pss=pss: h.activation(out=rn[:, 0:n], in_=pss[:, 0:n], func=AF.Sqrt,
                                                                     bias=cst[:, 0:1], scale=1.0 / 128.0),
                         reads=[tps, t_cst], writes=[t_rn])
                    S.op("dve", lambda h, n=n: h.reciprocal(out=rn[:, 0:n], in_=rn[:, 0:n]), reads=[t_rn], writes=[t_rn])
                    S.op("dve", lambda h, n=n, yb=yb: h.scalar_tensor_tensor(
                        out=ns[:, 0:n], in0=yb[:, 0:n], scalar=pcol("gdn_nw", j), in1=rn[:, 0:n], op0=ALU.mult, op1=ALU.mult),
                        reads=[tyb, t_rn, t_pt], writes=[t_ns])
                    S.op("dve", lambda h, n=n: h.tensor_tensor(out=og[:, 0:n], in0=ns[:, 0:n], in1=zs[:, 0:n], op=ALU.mult),
                         reads=[t_ns, t_zs], writes=[t_og])
                    a2 = max(a, PAD)
                    tl = tile_of(a2, b)
                    for oc in range(8):
                        pso, to = K.ps()
                        S.op("pe", lambda h, oc=oc, n=n, pso=pso: h.matmul(
                            pso[:, 0:n], lhsT=wo[:, oc * 128:(oc + 1) * 128], rhs=og[:, 0:n], start=True, stop=True),
                            reads=[t_wo, t_og], writes=[to])
                        S.op("dve", lambda h, oc=oc, a2=a2, a=a, b=b, pso=pso: h.tensor_tensor(
                            out=xT[:, oc, a2:b], in0=xT[:, oc, a2:b], in1=pso[:, a2 - a:b - a], op=ALU.add),
                            reads=[to] + tl, writes=tl)
                for ti_, (a_, b_) in enumerate(TILES):
                    do_tile(ti_, a_, b_)
                pst, pt_ = K.ps()
                for g in range(3):
                    S.op("pe", lambda h, g=g, pst=pst: h.transpose(
                        out=pst[0:51, g * 128:(g + 1) * 128], in_=stg[:, g, :], identity=ident[:]),
                        reads=[t_stg, t_id], writes=[pt_], inc=(g == 2))
                S.op("act", lambda h, pst=pst: h.copy(out=srow, in_=pst[0:51, 0:384]), reads=[pt_], writes=[t_srow])
                for g in range(3):
                    cs = slice(g * 1024 + hd * 128, g * 1024 + (hd + 1) * 128)
                    S.dma("sp", gC_s[j].rearrange("s k n -> (s k) n")[:, cs], srow[0:48, g * 128:(g + 1) * 128],
                          reads=[t_srow])
                    S.dma("sp", gC_p[j][:, cs], srow[48:51, g * 128:(g + 1) * 128], reads=[t_srow])
            for hd_ in range(cfg.get('gdn_heads', 8)):
                do_head(hd_)
            S.barrier()

        def seq_block_pair(ci, c0, states, ops, ybank, ycol0, tcols=(0, 64)):
            (ns, t_ns), (rs_, t_rs_) = ops["ns"], ops["rs"]
            (V, t_V), (Kh, t_Kh), (nPh, t_nPh) = ops["V"], ops["Kh"], ops["nPh"]
            (Ank, t_Ank), (Ark, t_Ark), (nArp, t_nArp), (TT, t_TT) = ops["Ank"], ops["Ark"], ops["nArp"], ops["TT"]
            (Xs, t_Xs), (Es, t_Es) = ops["Xs"], ops["Es"]
            H = lambda hs: slice(hs * 64, hs * 64 + 64)
            single = len(states) == 1
            psx, tpx = K.ps()
            if single:
                stt = states[0]
                S.op("pe", lambda h: h.matmul(psx[0:64, 0:128], lhsT=ns[:, c0:c0 + 64], rhs=stt["Z"], start=True, stop=False),
                     reads=[t_ns] + stt["tZ"], writes=[tpx], inc=False)
                for hs in range(2):
                    S.op("pe", lambda h, hs=hs: h.matmul(psx[0:64, H(hs)], lhsT=Ank[:, 2 * ci + hs, :], rhs=V[:, ci, H(hs)],
                                                         start=False, stop=(hs == 1)), reads=[t_Ank, t_V], writes=[tpx], inc=(hs == 1))
                S.op("act", lambda h: h.copy(out=Xs, in_=psx[0:64, 0:128]), reads=[tpx], writes=[t_Xs])
            else:
                for hs in range(2):
                    S.op("pe", lambda h, hs=hs: h.matmul(psx[H(hs), 0:64], lhsT=V[:, ci, H(hs)], rhs=Ank[:, 2 * ci + hs, :],
                                                         start=True, stop=False), reads=[t_Ank, t_V], writes=[tpx], inc=False)
                for si, stt in enumerate(states):
                    S.op("pe", lambda h, stt=stt, si=si: h.matmul(
                        psx[:, stt["lo"]:stt["hi"]], lhsT=stt["Z"], rhs=ns[:, c0 + stt["lo"]:c0 + stt["hi"]],
                        start=False, stop=(si == len(states) - 1)), reads=[t_ns] + stt["tZ"], writes=[tpx],
                        inc=(si == len(states) - 1))
                XT, t_XT = ops["XT"]
                S.op("act", lambda h: h.copy(out=XT, in_=psx[:, 0:64]), reads=[tpx], writes=[t_XT])
                psx2, tpx2 = K.ps()
                S.op("pe", lambda h: h.transpose(out=psx2[0:64, 0:128], in_=XT, identity=ident[:]),
                     reads=[t_XT, t_id], writes=[tpx2])
                S.op("act", lambda h: h.copy(out=Xs, in_=psx2[0:64, 0:128]), reads=[tpx2], writes=[t_Xs])
            pse, tpe = K.ps()
            for hs in range(2):
                S.op("pe", lambda h, hs=hs: h.matmul(pse[0:64, H(hs)], lhsT=TT[:, 2 * ci + hs, :], rhs=Xs[:, H(hs)],
                                                     start=True, stop=True), reads=[t_TT, t_Xs], writes=[tpe], inc=(hs == 1))
            S.op("dve", lambda h: h.tensor_copy(out=Es, in_=pse[0:64, 0:128]), reads=[tpe], writes=[t_Es])
            yb, tyb = ybank
            tl0, tl1 = tcols
            for hs in range(2):
                S.op("pe", lambda h, hs=hs: h.matmul(yb[H(hs), ycol0 + tl0:ycol0 + tl1], lhsT=V[:, ci, H(hs)],
                                                     rhs=Ark[:, 2 * ci + hs, tl0:tl1], start=True, stop=False),
                     reads=[t_V, t_Ark], writes=[tyb], inc=False)
            for stt in states:
                lo, hi = max(stt["lo"], tl0), min(stt["hi"], tl1)
                S.op("pe", lambda h, stt=stt, lo=lo, hi=hi: h.matmul(
                    yb[:, ycol0 + lo:ycol0 + hi], lhsT=stt["Z"], rhs=rs_[:, c0 + lo:c0 + hi], start=False, stop=False),
                    reads=[t_rs_] + stt["tZ"], writes=[tyb], inc=False)
            for hs in range(2):
                S.op("pe", lambda h, hs=hs: h.matmul(yb[H(hs), ycol0 + tl0:ycol0 + tl1], lhsT=Es[:, H(hs)],
                                                     rhs=nArp[:, 2 * ci + hs, tl0:tl1], start=False, stop=(hs == 1)),
                     reads=[t_Es, t_nArp], writes=[tyb], inc=(hs == 1))
            for stt in states:
                psz, tpz = K.ps()
                if stt.get("mask") is not None:
                    Km, t_Km = ops["Km"]; nPm, t_nPm = ops["nPm"]
                    S.op("pool", lambda h, stt=stt: h.tensor_scalar(out=Km, in0=Kh[:, ci, :], scalar1=stt["mask"], scalar2=None,
                                                                   op0=ALU.mult), reads=[t_Kh, ops["t_mask"]], writes=[t_Km])
                    S.op("pool", lambda h, stt=stt: h.tensor_scalar(out=nPm, in0=nPh[:, ci, :], scalar1=stt["mask"], scalar2=None,
                                                                   op0=ALU.mult), reads=[t_nPh, ops["t_mask"]], writes=[t_nPm])
                    lk, tlk, lp, tlp = Km, t_Km, nPm, t_nPm
                else:
                    lk, tlk, lp, tlp = Kh[:, ci, :], t_Kh, nPh[:, ci, :], t_nPh
                for hs in range(2):
                    S.op("pe", lambda h, psz=psz, lk=lk, hs=hs: h.matmul(psz[H(hs), H(hs)], lhsT=lk[:, H(hs)], rhs=V[:, ci, H(hs)],
                                                                         start=True, stop=False),
                         reads=[tlk, t_V], writes=[tpz], inc=False)
                    S.op("pe", lambda h, psz=psz, lp=lp, hs=hs: h.matmul(psz[H(hs), H(hs)], lhsT=lp[:, H(hs)], rhs=Es[:, H(hs)],
                                                                         start=False, stop=True),
                         reads=[tlp, t_Es], writes=[tpz], inc=(hs == 1))
                for hs in range(2):
                    S.op("dve", lambda h, stt=stt, psz=psz, hs=hs: h.scalar_tensor_tensor(
                        out=stt["Z"][H(hs), H(hs)], in0=stt["Z"][H(hs), H(hs)], scalar=stt["dc"][H(hs), :],
                        in1=psz[H(hs), H(hs)], op0=ALU.mult, op1=ALU.add),
                        reads=stt["tZ"] + [stt["t_dc"], tpz], writes=stt["tZ"])

        RT = [(a_, min(a_ + 256, T)) for a_ in range(0, T, 256)]

        def rwkv_layer(l):
            j = l // 2
            o = [0]

            def A(n):
                r = arena[:, o[0]:o[0] + n]; o[0] += n
                assert o[0] <= ARENA, o[0]
                return r
            TX = 2178
            hnx = A(8 * TX // 2).bitcast(BF16).rearrange("p (c n) -> p c n", c=8); t_hn = Trk()
            prv = A(256).bitcast(BF16).rearrange("p (c n) -> p c n", c=8); t_prv = Trk()
            L1 = A(1088).bitcast(BF16); L2 = A(1088).bitcast(BF16); L3 = A(1088).bitcast(BF16)
            t_L = [Trk(), Trk(), Trk()]
            rmq = A(256); rmq_l = A(128); t_rmq = Trk()
            blkb = A(64).bitcast(BF16); blkf = A(128); t_blk = Trk()
            omm = A(112); t_omm = Trk()
            Wab = A(3072).bitcast(BF16).rearrange("p (s c g n) -> p s c g n", s=2, c=8, g=3); t_Wab = Trk()
            wst = [A(1024).rearrange("p (c n) -> p c n", c=8) for _ in range(2)]; t_wst = [Trk(), Trk()]
            W2s = A(192).bitcast(BF16).rearrange("p (g n) -> p g n", g=3); t_W2s = Trk()
            wo = A(512).bitcast(BF16); t_wo = Trk()
            names = ["r", "k", "v", "lw", "G", "a", "g", "eG", "enG", "kd", "kk", "kf", "p", "rk", "ph", "t", "t2"]
            ball = A(256 * len(names))
            Bf = {nm: ball[:, i * 256:(i + 1) * 256] for i, nm in enumerate(names)}
            tb = {nm: Trk() for nm in names}
            tb["t2"] = tb["t"]
            Sst = ball[0:64, 15 * 256:17 * 256].rearrange("p (s n) -> p s n", s=4); t_Sst = tb["t"]
            SstO = ball[:, 15 * 256:17 * 256].rearrange("p (s n) -> p s n", s=4)
            rn_ = ball[:, 0:512]
            yg = A(128).bitcast(BF16); t_yg = Trk()
            sqb = A(128).bitcast(BF16); t_sqb = Trk()
            scr_ = A(2304)
            sq_scr = scr_[:, 0:2048].bitcast(BF16).rearrange("p (c n) -> p c n", c=8)
            W1s = scr_[:, 0:2048].bitcast(BF16).rearrange("p (s c n) -> p s c n", s=2, c=8)
            t_scr = Trk()
            shr = scr_[0:17, 0:1024]
            xs17 = scr_[:, 1024:1024 + 136].rearrange("p (c n) -> p c n", c=8)
            sq17 = scr_[:, 1200:1200 + 68].bitcast(BF16).rearrange("p (c n) -> p c n", c=8)
            Vt = scr_[0:64, 0:512].rearrange("p (b n) -> p b n", n=128); t_Vt = t_scr
            Kh = scr_[0:64, 512:1024].rearrange("p (b n) -> p b n", n=128); t_Kh = t_scr
            nPh = scr_[0:64, 1024:1536].rearrange("p (b n) -> p b n", n=128); t_nPh = t_scr
            v64 = lambda: A(512)[0:64, :].rearrange("p (b n) -> p b n", b=8)
            orr = [0]

            def v64r():
                r_ = arenaR[0:64, orr[0]:orr[0] + 512].rearrange("p (b n) -> p b n", b=8); orr[0] += 512
                assert orr[0] <= ARENAR
                return r_
            Pa = v64r(); Pb = v64r(); t_P = [Trk(), Trk()]
            Ank = v64(); B0 = v64r(); Ark = v64(); nArp = v64()
            t_Ank, t_B0, t_Ark, t_nArp = Trk(), Trk(), Trk(), Trk()
            Ca = v64r(); Cb = v64r(); Ba = v64r(); Bb = v64r()
            t_C = [Trk(), Trk()]; t_B = [Trk(), Trk()]
            Xs = A(128)[0:64, :]; Es = A(128)[0:64, :]; XT = A(64); t_Xs, t_Es, t_XT = Trk(), Trk(), Trk()
            Zp = A(128); t_Zp = [Trk()]
            Zs = A(512).rearrange("p (s n) -> p s n", s=4); t_Zs = [[Trk()] for _ in range(4)]
            Km = A(128)[0:64, :]; nPm = A(128)[0:64, :]; t_Km, t_nPm = Trk(), Trk()
            dcs = A(24); t_dcs = Trk()
            MIX = {"r": 0, "w": 1, "k": 2, "v": 3, "a": 4, "g": 5}
            mixc = lambda m, c: pcol("rw_mix", (j * 6 + MIX[m]) * 8 + c)
            ommc = lambda m, c: omm[:, MIX[m] * 8 + c:MIX[m] * 8 + c + 1]
            PV = lambda name, c: pcol(name, j * 8 + c)

            S.op("pool", lambda h: h.memset(blkb, 0.0), writes=[t_blk])
            S.op("pool", lambda h: h.memset(blkf, 0.0), writes=[t_blk])
            for hs in range(2):
                sl = slice(hs * 64, hs * 64 + 64)
                S.op("pool", lambda h, sl=sl: h.memset(blkb[sl, sl], 1.0), writes=[t_blk])
                S.op("pool", lambda h, sl=sl: h.memset(blkf[sl, sl], 1.0), writes=[t_blk])
            S.op("pool", lambda h: h.memset(rmq, 1.0), writes=[t_rmq])
            S.op("pool", lambda h: h.memset(rmq.rearrange("p (c k) -> p c k", k=64)[:, :, 0:1], 0.0), writes=[t_rmq])
            S.op("pool", lambda h: h.memset(rmq_l, 1.0), writes=[t_rmq])
            S.op("pool", lambda h: h.memset(rmq_l[:, 0:1], 0.0), writes=[t_rmq])
            S.op("pool", lambda h: h.memset(rmq_l[:, 64:128].rearrange("p (c k) -> p c k", k=4)[:, :, 0:1], 0.0),
                 writes=[t_rmq])
            S.op("pool", lambda h: h.memset(hnx[:, :, 0:2], 0.0), writes=[t_hn])
            mo = pt_off["rw_mix"] + j * 48
            S.op("dve", lambda h: h.tensor_scalar(out=omm[:, 0:48], in0=ptab[:, mo:mo + 48], scalar1=-1.0, scalar2=1.0,
                                                  op0=ALU.mult, op1=ALU.add), reads=[t_pt], writes=[t_omm])
            ko = pt_off["rw_ka"] + j * 8
            S.op("dve", lambda h: h.tensor_scalar(out=omm[:, 48:56], in0=ptab[:, ko:ko + 8], scalar1=-1.0, scalar2=1.0,
                                                  op0=ALU.mult, op1=ALU.add), reads=[t_pt], writes=[t_omm])
            if cfg.get('rw_stop') == 1:
                S.barrier(); return
            tlast = tile_of(S0 - 1, T)
            S.op("dve", lambda h: h.tensor_copy(out=xs17[:, :, 0:1], in_=xT[:, :, S0 - 1:S0]), reads=tlast, writes=[t_scr])
            S.op("dve", lambda h: h.tensor_copy(out=xs17[:, :, 1:17],
                                                in_=xT[:, :, S0:T].rearrange("p c (s k) -> p c s k", k=4)[:, :, :, 3]),
                 reads=tlast, writes=[t_scr])
            S.op("act", lambda h: h.activation(out=sq17, in_=xs17, func=AF.Square), reads=[t_scr], writes=[t_scr])
            pst, pt_ = K.ps()
            for c in range(8):
                S.op("pe", lambda h, c=c, pst=pst: h.matmul(pst[:, 0:17], lhsT=onesb[:], rhs=sq17[:, c, :],
                                                            start=(c == 0), stop=(c == 7)),
                     reads=[t_scr, t_ones], writes=[pt_], inc=(c == 7))
            S.op("act", lambda h, pst=pst: h.activation(out=rn_[:, 0:17], in_=pst[:, 0:17], func=AF.Sqrt, bias=cst[:, 0:1],
                                                        scale=1.0), reads=[pt_, t_cst], writes=[tb["r"]])
            S.op("dve", lambda h: h.reciprocal(out=rn_[:, 0:17], in_=rn_[:, 0:17]), reads=[tb["r"]], writes=[tb["r"]])
            for c in range(8):
                S.op("dve", lambda h, c=c: h.scalar_tensor_tensor(
                    out=xs17[:, c, :], in0=xs17[:, c, :], scalar=pcol("norm_mix", l * 8 + c), in1=rn_[:, 0:17],
                    op0=ALU.mult, op1=ALU.mult), reads=[t_scr, tb["r"], t_pt], writes=[t_scr])
            for half in range(2):
                pst, pt_ = K.ps()
                for cc in range(4):
                    S.op("pe", lambda h, cc=cc, half=half, pst=pst: h.transpose(
                        out=pst[0:17, cc * 128:(cc + 1) * 128], in_=xs17[:, half * 4 + cc, :], identity=ident[:]),
                        reads=[t_scr, t_id], writes=[pt_], inc=(cc == 3))
                S.op("act", lambda h, half=half, pst=pst: h.copy(out=shr[:, half * 512:(half + 1) * 512], in_=pst[0:17, :]),
                     reads=[pt_], writes=[t_scr])
            S.dma("sp", rSh_p[j:j + 1, :], shr[0:1, :], reads=[t_scr])
            S.dma("sp", rSh_s[j], shr[1:17, :], reads=[t_scr])
            if cfg.get('rw_stop') == 2:
                S.barrier(); return
            S.dma("sp", shr[0:16, :], rwkv_shift_in[j], reads=[t_scr], writes=[t_scr])
            pst, pt_ = K.ps()
            for c in range(8):
                S.op("pe", lambda h, c=c, pst=pst: h.transpose(out=pst[:, c * 16:(c + 1) * 16], in_=shr[0:16, c * 128:(c + 1) * 128],
                                                               identity=ident[0:16, 0:16]),
                     reads=[t_scr, t_id], writes=[pt_], inc=(c == 7))
            S.op("dve", lambda h, pst=pst: h.tensor_copy(
                out=prv.rearrange("p c (s k) -> p c s k", k=4)[:, :, :, 0],
                in_=pst[:, 0:128].rearrange("p (c s) -> p c s", c=8)), reads=[pt_], writes=[t_prv])
            rmsnorm(hnx, 1, "norm_mix", l, 0, T, sq_scr, t_scr, rn_, tb["r"], t_hn)
            S.op("dve", lambda h: h.tensor_copy(
                out=prv.rearrange("p c (s k) -> p c s k", k=4)[:, :, :, 1:4],
                in_=hnx[:, :, 1 + S0:1 + T].rearrange("p c (s k) -> p c s k", k=4)[:, :, :, 0:3]),
                reads=[t_hn], writes=[t_prv])

            if cfg.get('rw_stop') == 3:
                S.barrier(); return
            def mm_pair(ps_ap, n, la, lb, a, b, rd, wr, last_stop=True):
                for c in range(8):
                    S.op("pe", lambda h, c=c: h.matmul(ps_ap[:, 0:n], lhsT=la(c), rhs=hnx[:, c, 1 + a:1 + b],
                                                       start=(c == 0), stop=False), reads=rd + [t_hn], writes=wr, inc=False)
                if b <= S0:
                    for c in range(8):
                        S.op("pe", lambda h, c=c: h.matmul(ps_ap[:, 0:n], lhsT=lb(c), rhs=hnx[:, c, a:b],
                                                           start=False, stop=(c == 7)), reads=rd + [t_hn], writes=wr,
                             inc=(c == 7))
                else:
                    npz = S0 - a
                    for c in range(8):
                        S.op("pe", lambda h, c=c: h.matmul(ps_ap[:, 0:npz], lhsT=lb(c), rhs=hnx[:, c, a:a + npz],
                                                           start=False, stop=False), reads=rd + [t_hn], writes=wr, inc=False)
                    for c in range(8):
                        S.op("pe", lambda h, c=c: h.matmul(ps_ap[:, npz:n], lhsT=lb(c), rhs=prv[:, c, :],
                                                           start=False, stop=(c == 7)), reads=rd + [t_prv], writes=wr,
                             inc=(c == 7))

            def load_scaled(src_ap, ncols, dst_a, dst_b, m, wi):
                ws = wst[wi % 2]; tws = t_wst[wi % 2]
                S.dma("sp", ws[:, :, 0:ncols], src_ap.rearrange("(c p) n -> p c n", p=128), writes=[tws])
                for c in range(8):
                    S.op("act", lambda h, c=c: h.activation(out=dst_a(c), in_=ws[:, c, 0:ncols], func=AF.Copy,
                                                            scale=ommc(m, c)), reads=[tws, t_omm], writes=[t_scr])
                    S.op("pool", lambda h, c=c: h.tensor_scalar(out=dst_b(c), in0=ws[:, c, 0:ncols], scalar1=mixc(m, c),
                                                                scalar2=None, op0=ALU.mult), reads=[tws, t_pt], writes=[t_scr])
            stage1 = [
                ([(rwkv_w1[j], 64, 0, "w"), (rwkv_a1[j], 64, 64, "a")], 128, [(0, 64, AF.Tanh), (64, 128, AF.Copy)], 0),
                ([(rwkv_g1[j][:, 0:128], 128, 0, "g")], 128, [(0, 128, AF.Sigmoid)], 1),
                ([(rwkv_g1[j][:, 128:160], 32, 0, "g")] + ([(rwkv_v1[0], 32, 32, "v")] if j == 1 else []),
                 64 if j == 1 else 32, [(0, 32, AF.Sigmoid)] + ([(32, 64, AF.Copy)] if j == 1 else []), 2),
            ]
            wi = 0
            for (srcs, M, acts, li) in stage1:
                for (src, nc_, co, m) in srcs:
                    load_scaled(src, nc_, lambda c, co=co, nc_=nc_: W1s[:, 0, c, co:co + nc_],
                                lambda c, co=co, nc_=nc_: W1s[:, 1, c, co:co + nc_], m, wi)
                    wi += 1
                Lx = (L1, L2, L3)[li]
                for (a, b) in TILES:
                    n = b - a
                    pst, pt_ = K.ps()
                    mm_pair(pst[0:M], n, lambda c, M=M: W1s[:, 0, c, 0:M], lambda c, M=M: W1s[:, 1, c, 0:M], a, b,
                            [t_scr], [pt_])
                    for (lo, hi, fn) in acts:
                        S.op("act", lambda h, lo=lo, hi=hi, fn=fn, pst=pst, a=a, b=b, n=n, Lx=Lx: h.activation(
                            out=Lx[lo:hi, a:b], in_=pst[lo:hi, 0:n], func=fn), reads=[pt_], writes=[t_L[li]])
            if cfg.get('rw_stop') == 4:
                S.barrier(); return
            ybi = [0]

            def do_pair(jp):
                cs = slice(jp * 128, (jp + 1) * 128)
                for gi, (W, m) in enumerate(((rwkv_wr, "r"), (rwkv_wk, "k"), (rwkv_wv, "v"))):
                    ws = wst[gi % 2]; tws = t_wst[gi % 2]
                    S.dma("sp", ws, W[j].rearrange("(c p) n -> p c n", p=128)[:, :, cs], writes=[tws])
                    for c in range(8):
                        S.op("act", lambda h, c=c, gi=gi, m=m, ws=ws: h.activation(
                            out=Wab[:, 0, c, gi, :], in_=ws[:, c, :], func=AF.Copy, scale=ommc(m, c)),
                            reads=[tws, t_omm], writes=[t_Wab])
                        S.op("pool", lambda h, c=c, gi=gi, m=m, ws=ws: h.tensor_scalar(
                            out=Wab[:, 1, c, gi, :], in0=ws[:, c, :], scalar1=mixc(m, c), scalar2=None, op0=ALU.mult),
                            reads=[tws, t_pt], writes=[t_Wab])
                S.dma("pool", W2s[0:64, 0, :], rwkv_w2[j][:, cs], writes=[t_W2s])
                S.dma("pool", W2s[64:128, 0, :], rwkv_a2[j][:, cs], writes=[t_W2s])
                S.dma("pool", W2s[:, 1, :], rwkv_g2[j][0:128, cs], writes=[t_W2s])
                S.dma("pool", W2s[0:32, 2, :], rwkv_g2[j][128:160, cs], writes=[t_W2s])
                if j == 1:
                    S.dma("pool", W2s[32:64, 2, :], rwkv_v2[0][:, cs], writes=[t_W2s])
                S.dma("pool", wo, rwkv_wo[j][cs, :], writes=[t_wo])
                S.op("dve", lambda h: h.memset(Zp, 0.0), writes=t_Zp)

                def do_tile(ti, a, b):
                    n = b - a
                    nch = n // 64
                    nblk = 2 * nch
                    last = b > S0
                    npc = nch - 1 if last else nch
                    so = S0 - a
                    B_ = lambda nm: Bf[nm][:, 0:n]
                    for gi, nm in enumerate(("r", "k", "v")):
                        pst, pt_ = K.ps()
                        mm_pair(pst, n, lambda c, gi=gi: Wab[:, 0, c, gi, :], lambda c, gi=gi: Wab[:, 1, c, gi, :], a, b,
                                [t_Wab], [pt_])
                        S.op("act", lambda h, nm=nm, pst=pst: h.copy(out=B_(nm), in_=pst[:, 0:n]), reads=[pt_], writes=[tb[nm]])
                    if cfg.get('rp_stop') == 1:
                        return
                    pst, pt_ = K.ps()
                    S.op("pe", lambda h, pst=pst: h.matmul(pst[:, 0:n], lhsT=W2s[0:64, 0, :], rhs=L1[0:64, a:b], start=True, stop=True),
                         reads=[t_W2s, t_L[0]], writes=[pt_])
                    S.op("act", lambda h, pst=pst: h.activation(out=B_("lw"), in_=pst[:, 0:n], func=AF.Sigmoid,
                                                                bias=PV("rw_w0", jp), scale=1.0), reads=[pt_, t_pt], writes=[tb["lw"]])
                    S.op("dve", lambda h: h.tensor_scalar(out=B_("lw"), in0=B_("lw"), scalar1=-0.6065306597126334,
                                                          scalar2=None, op0=ALU.mult), reads=[tb["lw"]], writes=[tb["lw"]])
                    pst, pt_ = K.ps()
                    S.op("pe", lambda h, pst=pst: h.matmul(pst[:, 0:n], lhsT=W2s[64:128, 0, :], rhs=L1[64:128, a:b], start=True, stop=True),
                         reads=[t_W2s, t_L[0]], writes=[pt_])
                    S.op("act", lambda h, pst=pst: h.activation(out=B_("a"), in_=pst[:, 0:n], func=AF.Sigmoid,
                                                                bias=PV("rw_a0", jp), scale=1.0), reads=[pt_, t_pt], writes=[tb["a"]])
                    pst, pt_ = K.ps()
                    S.op("pe", lambda h, pst=pst: h.matmul(pst[:, 0:n], lhsT=W2s[:, 1, :], rhs=L2[:, a:b], start=True, stop=False),
                         reads=[t_W2s, t_L[1]], writes=[pt_], inc=False)
                    S.op("pe", lambda h, pst=pst: h.matmul(pst[:, 0:n], lhsT=W2s[0:32, 2, :], rhs=L3[0:32, a:b], start=False, stop=True),
                         reads=[t_W2s, t_L[2]], writes=[pt_])
                    S.op("act", lambda h, pst=pst: h.copy(out=B_("g"), in_=pst[:, 0:n]), reads=[pt_], writes=[tb["g"]])
                    if j == 0:
                        S.dma("sp", vf_d[jp, :, a:b], B_("v"), reads=[tb["v"]])
                    else:
                        pst, pt_ = K.ps()
                        S.op("pe", lambda h, pst=pst: h.matmul(pst[:, 0:n], lhsT=W2s[32:64, 2, :], rhs=L3[32:64, a:b],
                                                               start=True, stop=True), reads=[t_W2s, t_L[2]], writes=[pt_])
                        S.op("act", lambda h, pst=pst: h.activation(out=B_("t"), in_=pst[:, 0:n], func=AF.Sigmoid,
                                                                    bias=pcol("rw_v0", jp), scale=1.0),
                             reads=[pt_, t_pt], writes=[tb["t"]])
                        S.dma("sp", B_("t2"), vf_d[jp, :, a:b], writes=[tb["t"]])
                        S.op("dve", lambda h: h.tensor_tensor(out=B_("t2"), in0=B_("t2"), in1=B_("v"), op=ALU.subtract),
                             reads=[tb["t"], tb["v"]], writes=[tb["t"]])
                        S.op("dve", lambda h: h.tensor_tensor(out=B_("t2"), in0=B_("t2"), in1=B_("t"), op=ALU.mult),
                             reads=[tb["t"]], writes=[tb["t"]])
                        S.op("dve", lambda h: h.tensor_tensor(out=B_("v"), in0=B_("v"), in1=B_("t2"), op=ALU.add),
                             reads=[tb["t"], tb["v"]], writes=[tb["v"]])
                    if cfg.get('rp_stop') == 2:
                        return
                    rm_ap = rmq_l[:, 0:n] if last else rmq[:, 0:n]
                    S.op("dve", lambda h: h.tensor_tensor_scan(out=B_("G"), data0=rm_ap, data1=B_("lw"), initial=0.0,
                                                               op0=ALU.mult, op1=ALU.add),
                         reads=[t_rmq, tb["lw"]], writes=[tb["G"]])
                    S.op("dve", lambda h: h.tensor_tensor(out=B_("lw"), in0=B_("G"), in1=B_("lw"), op=ALU.subtract),
                         reads=[tb["G"], tb["lw"]], writes=[tb["lw"]])
                    S.op("act", lambda h: h.activation(out=B_("eG"), in_=B_("G"), func=AF.Exp), reads=[tb["G"]], writes=[tb["eG"]])
                    S.op("act", lambda h: h.activation(out=B_("enG"), in_=B_("G"), func=AF.Exp, scale=-1.0),
                         reads=[tb["G"]], writes=[tb["enG"]])
                    if npc > 0:
                        S.op("pool", lambda h: h.tensor_copy(
                            out=dcs[:, 0:npc], in_=Bf["eG"][:, 0:npc * 64].rearrange("p (b k) -> p b k", k=64)[:, :, 63]),
                            reads=[tb["eG"]], writes=[t_dcs])
                        S.op("dve", lambda h: h.tensor_tensor(
                            out=Bf["kd"][:, 0:npc * 64].rearrange("p (b k) -> p b k", k=64),
                            in0=Bf["G"][:, 0:npc * 64].rearrange("p (b k) -> p b k", k=64)[:, :, 63:64].to_broadcast([128, npc, 64]),
                            in1=Bf["G"][:, 0:npc * 64].rearrange("p (b k) -> p b k", k=64), op=ALU.subtract),
                            reads=[tb["G"]], writes=[tb["kd"]])
                    if last:
                        S.op("pool", lambda h: h.tensor_copy(
                            out=dcs[:, 8:24], in_=Bf["eG"][:, so:so + 64].rearrange("p (s k) -> p s k", k=4)[:, :, 3]),
                            reads=[tb["eG"]], writes=[t_dcs])
                        S.op("dve", lambda h: h.tensor_tensor(
                            out=Bf["kd"][:, so:so + 64].rearrange("p (s k) -> p s k", k=4),
                            in0=Bf["G"][:, so:so + 64].rearrange("p (s k) -> p s k", k=4)[:, :, 3:4].to_broadcast([128, 16, 4]),
                            in1=Bf["G"][:, so:so + 64].rearrange("p (s k) -> p s k", k=4), op=ALU.subtract),
                            reads=[tb["G"]], writes=[tb["kd"]])
                    S.op("act", lambda h: h.activation(out=B_("kd"), in_=B_("kd"), func=AF.Exp), reads=[tb["kd"]], writes=[tb["kd"]])
                    S.op("dve", lambda h: h.tensor_scalar(out=B_("kk"), in0=B_("k"), scalar1=PV("rw_kk", jp), scalar2=None,
                                                          op0=ALU.mult), reads=[tb["k"], t_pt], writes=[tb["kk"]])
                    S.op("act", lambda h: h.activation(out=sqb[:, 0:n], in_=B_("kk"), func=AF.Square), reads=[tb["kk"]], writes=[t_sqb])
                    pst, pt_ = K.ps()
                    S.op("pe", lambda h, pst=pst: h.matmul(pst[:, 0:n], lhsT=blkb, rhs=sqb[:, 0:n], start=True, stop=True),
                         reads=[t_blk, t_sqb], writes=[pt_])
                    S.op("act", lambda h, pst=pst: h.activation(out=B_("t"), in_=pst[:, 0:n], func=AF.Sqrt, bias=cst[:, 0:1], scale=1.0),
                         reads=[pt_, t_cst], writes=[tb["t"]])
                    S.op("dve", lambda h: h.reciprocal(out=B_("t"), in_=B_("t")), reads=[tb["t"]], writes=[tb["t"]])
                    S.op("dve", lambda h: h.tensor_tensor(out=B_("kk"), in0=B_("kk"), in1=B_("t"), op=ALU.mult),
                         reads=[tb["kk"], tb["t"]], writes=[tb["kk"]])
                    S.op("dve", lambda h: h.tensor_scalar(out=B_("t"), in0=B_("a"), scalar1=PV("rw_ka", jp),
                                                          scalar2=omm[:, 48 + jp:49 + jp], op0=ALU.mult, op1=ALU.add),
                         reads=[tb["a"], t_pt, t_omm], writes=[tb["t"]])
                    S.op("dve", lambda h: h.tensor_tensor(out=B_("kf"), in0=B_("k"), in1=B_("t"), op=ALU.mult),
                         reads=[tb["k"], tb["t"]], writes=[tb["kf"]])
                    S.op("dve", lambda h: h.tensor_tensor(out=B_("p"), in0=B_("kk"), in1=B_("a"), op=ALU.mult),
                         reads=[tb["kk"], tb["a"]], writes=[tb["p"]])
                    S.op("dve", lambda h: h.scalar_tensor_tensor(out=B_("rk"), in0=B_("r"), scalar=PV("rw_rk", jp), in1=B_("kf"),
                                                                 op0=ALU.mult, op1=ALU.mult),
                         reads=[tb["r"], tb["kf"], t_pt], writes=[tb["rk"]])
                    S.op("act", lambda h: h.activation(out=B_("t"), in_=B_("lw"), func=AF.Exp), reads=[tb["lw"]], writes=[tb["t"]])
                    S.op("dve", lambda h: h.tensor_tensor(out=B_("kk"), in0=B_("kk"), in1=B_("t"), op=ALU.mult),
                         reads=[tb["kk"], tb["t"]], writes=[tb["kk"]])
                    S.op("dve", lambda h: h.scalar_tensor_tensor(out=B_("ph"), in0=B_("p"), scalar=-1.0, in1=B_("kd"),
                                                                 op0=ALU.mult, op1=ALU.mult),
                         reads=[tb["p"], tb["kd"]], writes=[tb["ph"]])
                    S.op("dve", lambda h: h.tensor_tensor(out=B_("p"), in0=B_("p"), in1=B_("enG"), op=ALU.mult),
                         reads=[tb["p"], tb["enG"]], writes=[tb["p"]])
                    S.op("dve", lambda h: h.tensor_tensor(out=B_("kd"), in0=B_("kf"), in1=B_("kd"), op=ALU.mult),
                         reads=[tb["kf"], tb["kd"]], writes=[tb["kd"]])
                    S.op("dve", lambda h: h.tensor_tensor(out=B_("enG"), in0=B_("kf"), in1=B_("enG"), op=ALU.mult),
                         reads=[tb["kf"], tb["enG"]], writes=[tb["enG"]])
                    S.op("dve", lambda h: h.tensor_tensor(out=B_("eG"), in0=B_("r"), in1=B_("eG"), op=ALU.mult),
                         reads=[tb["r"], tb["eG"], t_dcs], writes=[tb["eG"]])
                    nT, pT, kT, rT = Bf["kk"], Bf["p"], Bf["enG"], Bf["eG"]
                    t_nT, t_pT, t_kT, t_rT = tb["kk"], tb["p"], tb["enG"], tb["eG"]
                    if cfg.get('rp_stop') == 3:
                        return
                    for (nm, dst) in (("v", Vt), ("kd", Kh), ("ph", nPh)):
                        pst, pt_ = K.ps()
                        for ci in range(nch):
                            S.op("pe", lambda h, ci=ci, nm=nm, pst=pst: h.transpose(
                                out=pst[0:64, ci * 128:(ci + 1) * 128], in_=Bf[nm][:, ci * 64:(ci + 1) * 64], identity=ident[:]),
                                reads=[tb[nm], t_id], writes=[pt_], inc=(ci == nch - 1))
                        S.op("act", lambda h, pst=pst, dst=dst: h.copy(
                            out=dst[:, 0:nch, :], in_=pst[0:64, 0:nch * 128].rearrange("p (b n) -> p b n", n=128)),
                            reads=[pt_], writes=[t_scr])
                    if cfg.get('rp_stop') == 4:
                        return
                    combos = ((pT, t_pT, nT, t_nT), (kT, t_kT, nT, t_nT), (kT, t_kT, rT, t_rT), (pT, t_pT, rT, t_rT))
                    dsts = ((B0, t_B0, -1.0, tri_s, trib_s), (Ank, t_Ank, 1.0, tri_s, trib_s),
                            (Ark, t_Ark, 1.0, tri_i, trib_i), (nArp, t_nArp, -1.0, tri_i, trib_i))
                    for rnd in range(2):
                        banks = {}
                        for qi in (2 * rnd, 2 * rnd + 1):
                            lt, tlt, rt, trt = combos[qi]
                            banks[qi] = [K.ps(), K.ps()]
                            for ci in range(nch):
                                for hs in range(2):
                                    pst, pt_ = banks[qi][hs]
                                    Ls = slice(hs * 64, hs * 64 + 64)
                                    S.op("pe", lambda h, ci=ci, Ls=Ls, lt=lt, rt=rt, pst=pst: h.matmul(
                                        pst[0:64, ci * 64:(ci + 1) * 64], lhsT=lt[Ls, ci * 64:(ci + 1) * 64],
                                        rhs=rt[Ls, ci * 64:(ci + 1) * 64], start=True, stop=True),
                                        reads=[tlt, trt], writes=[pt_], inc=(ci == nch - 1))
                        for qi in (2 * rnd, 2 * rnd + 1):
                            dst, tdst, sgn, trp, trs = dsts[qi]
                            dv_ = dst[:, 0:nblk, :].rearrange("p (c h) n -> p c h n", h=2)
                            for hs in range(2):
                                pst, pt_ = banks[qi][hs]
                                pv = pst[0:64, 0:nch * 64].rearrange("p (b n) -> p b n", n=64)
                                if npc > 0:
                                    S.op("dve", lambda h, dv_=dv_, pv=pv, sgn=sgn, trp=trp, hs=hs, qi=qi: h.scalar_tensor_tensor(
                                        out=(R(dv_[:, 0:npc, hs, :]) if qi == 0 else dv_[:, 0:npc, hs, :]), in0=pv[:, 0:npc, :], scalar=sgn,
                                        in1=trp.unsqueeze(1).to_broadcast([64, npc, 64]), op0=ALU.mult, op1=ALU.mult),
                                        reads=[pt_, t_tri], writes=[tdst])
                                if last:
                                    S.op("dve", lambda h, dv_=dv_, pv=pv, sgn=sgn, trs=trs, hs=hs, qi=qi: h.scalar_tensor_tensor(
                                        out=(R(dv_[:, npc, hs, :]) if qi == 0 else dv_[:, npc, hs, :]), in0=pv[:, npc, :], scalar=sgn, in1=trs,
                                        op0=ALU.mult, op1=ALU.mult), reads=[pt_, t_tri], writes=[tdst])
                    if cfg.get('rp_stop') == 5:
                        return
                    TT, t_TT = solve_blocks(nblk, B0, Ca, Cb, Ba, Bb, Pa, Pb, t_B0, t_C, t_B, t_P)
                    if cfg.get('rp_stop') == 6:
                        return
                    ybank = Y_BANKS[ybi[0] % 2]; ybi[0] += 1
                    ops = dict(ns=(nT, t_nT), rs=(rT, t_rT), V=(Vt, t_scr), Kh=(Kh, t_scr), nPh=(nPh, t_scr),
                               Ank=(Ank, t_Ank), Ark=(Ark, t_Ark), nArp=(nArp, t_nArp), TT=(TT, t_TT),
                               Xs=(Xs, t_Xs), Es=(Es, t_Es), XT=(XT, t_XT), Km=(Km, t_Km), nPm=(nPm, t_nPm), t_mask=t_tri)
                    for ci in range(npc):
                        seq_block_pair(ci, ci * 64, [dict(Z=Zp, tZ=t_Zp, lo=0, hi=64, dc=dcs[:, ci:ci + 1], t_dc=t_dcs)],
                                       ops, ybank, ci * 64)
                    if last:
                        pst, pt_ = K.ps()
                        S.op("pe", lambda h, pst=pst: h.transpose(out=pst[:, 0:128], in_=Zp, identity=ident[:]),
                             reads=t_Zp + [t_id], writes=[pt_])
                        S.op("act", lambda h, pst=pst: h.copy(out=SstO[:, 0, :], in_=pst[:, 0:128]), reads=[pt_], writes=[t_Sst])
                        for hh in range(2):
                            Hs = slice(hh * 64, hh * 64 + 64)
                            S.dma("sp", rS_p[j, 2 * jp + hh], SstO[Hs, 0, Hs], reads=[t_Sst])
                        tZall = [t for tt in t_Zs for t in tt]
                        S.op("dve", lambda h: h.memset(Zs, 0.0), writes=tZall)
                        for sg in range(4):
                            for s_ in range(4):
                                S.dma("sp", Sst[:, s_, :].rearrange("p (h k) -> p h k", h=2),
                                      rwkv_S_in[j, sg * 4 + s_, 2 * jp:2 * jp + 2].rearrange("h v k -> v h k"),
                                      writes=[t_Sst])
                            pst, pt_ = K.ps()
                            for s_ in range(4):
                                S.op("pe", lambda h, s_=s_, pst=pst: h.transpose(out=pst[:, s_ * 64:(s_ + 1) * 64], in_=Sst[:, s_, :],
                                                                                  identity=ident[0:64, 0:64]),
                                     reads=[t_Sst, t_id], writes=[pt_], inc=(s_ == 3))
                            for hh in range(2):
                                Hs = slice(hh * 64, hh * 64 + 64)
                                S.op("dve", lambda h, pst=pst, Hs=Hs: h.tensor_copy(
                                    out=Zs[Hs, :, Hs], in_=pst[Hs, 0:256].rearrange("p (s n) -> p s n", s=4)),
                                    reads=[pt_], writes=tZall)
                            sts = [dict(Z=Zs[:, s_, :], tZ=t_Zs[s_], lo=(sg * 4 + s_) * 4, hi=(sg * 4 + s_) * 4 + 4,
                                        dc=dcs[:, 8 + sg * 4 + s_:9 + sg * 4 + s_], t_dc=t_dcs,
                                        mask=rmk[:, sg * 4 + s_:sg * 4 + s_ + 1]) for s_ in range(4)]
                            seq_block_pair(npc, so, sts, ops, ybank, so, tcols=(sg * 16, sg * 16 + 16))
                            pst, pt_ = K.ps()
                            for s_ in range(4):
                                S.op("pe", lambda h, s_=s_, pst=pst: h.transpose(out=pst[:, s_ * 128:(s_ + 1) * 128], in_=Zs[:, s_, :],
                                                                                  identity=ident[:]),
                                     reads=tZall + [t_id], writes=[pt_], inc=(s_ == 3))
                            S.op("act", lambda h, pst=pst: h.copy(out=SstO, in_=pst[:, :].rearrange("p (s n) -> p s n", s=4)),
                                 reads=[pt_], writes=[t_Sst])
                            for s_ in range(4):
                                for hh in range(2):
                                    Hs = slice(hh * 64, hh * 64 + 64)
                                    S.dma("sp", rS_s[j, sg * 4 + s_, 2 * jp + hh], SstO[Hs, s_, Hs], reads=[t_Sst])
                    if cfg.get('rp_stop') == 7:
                        return
                    yb, tyb = ybank
                    S.op("act", lambda h: h.copy(out=B_("r"), in_=yb[:, 0:n]), reads=[tyb, tb["rk"]], writes=[tb["r"]])
                    pst, pt_ = K.ps()
                    S.op("pe", lambda h, pst=pst: h.matmul(pst[:, 0:n], lhsT=blkf, rhs=B_("r"), start=True, stop=True),
                         reads=[t_blk, tb["r"]], writes=[pt_])
                    S.op("dve", lambda h, pst=pst: h.scalar_tensor_tensor(out=B_("r"), in0=pst[:, 0:n], scalar=-1.0 / 64.0, in1=B_("r"),
                                                                         op0=ALU.mult, op1=ALU.add), reads=[pt_, tb["r"]], writes=[tb["r"]])
                    S.op("act", lambda h: h.activation(out=sqb[:, 0:n], in_=B_("r"), func=AF.Square), reads=[tb["r"]], writes=[t_sqb])
                    pst, pt_ = K.ps()
                    S.op("pe", lambda h, pst=pst: h.matmul(pst[:, 0:n], lhsT=blkb, rhs=sqb[:, 0:n], start=True, stop=True),
                         reads=[t_blk, t_sqb], writes=[pt_])
                    S.op("act", lambda h, pst=pst: h.activation(out=B_("t"), in_=pst[:, 0:n], func=AF.Sqrt, bias=cst[:, 1:2],
                                                                scale=1.0 / 64.0), reads=[pt_, t_cst], writes=[tb["t"]])
                    S.op("dve", lambda h: h.reciprocal(out=B_("t"), in_=B_("t")), reads=[tb["t"]], writes=[tb["t"]])
                    S.op("dve", lambda h: h.tensor_tensor(out=B_("r"), in0=B_("r"), in1=B_("t"), op=ALU.mult),
                         reads=[tb["r"], tb["t"]], writes=[tb["r"]])
                    S.op("dve", lambda h: h.tensor_scalar(out=B_("r"), in0=B_("r"), scalar1=PV("rw_lnw", jp), scalar2=PV("rw_lnb", jp),
                                                          op0=ALU.mult, op1=ALU.add), reads=[tb["r"], t_pt], writes=[tb["r"]])
                    pst, pt_ = K.ps()
                    S.op("pe", lambda h, pst=pst: h.matmul(pst[:, 0:n], lhsT=blkf, rhs=B_("rk"), start=True, stop=True),
                         reads=[t_blk, tb["rk"]], writes=[pt_])
                    S.op("dve", lambda h, pst=pst: h.tensor_tensor(out=B_("t"), in0=pst[:, 0:n], in1=B_("v"), op=ALU.mult),
                         reads=[pt_, tb["v"]], writes=[tb["t"]])
                    S.op("dve", lambda h: h.tensor_tensor(out=B_("r"), in0=B_("r"), in1=B_("t"), op=ALU.add),
                         reads=[tb["r"], tb["t"]], writes=[tb["r"]])
                    S.op("dve", lambda h: h.tensor_tensor(out=yg[:, 0:n], in0=B_("r"), in1=B_("g"), op=ALU.mult),
                         reads=[tb["r"], tb["g"]], writes=[t_yg])
                    a2 = max(a, PAD)
                    if a2 < b:
                        tl = tile_of(a2, b)
                        for oc in range(8):
                            pso, to = K.ps()
                            S.op("pe", lambda h, oc=oc, pso=pso: h.matmul(pso[:, 0:n], lhsT=wo[:, oc * 128:(oc + 1) * 128], rhs=yg[:, 0:n],
                                                                          start=True, stop=True), reads=[t_wo, t_yg], writes=[to])
                            S.op("dve", lambda h, oc=oc, pso=pso: h.tensor_tensor(
                                out=xT[:, oc, a2:b], in0=xT[:, oc, a2:b], in1=pso[:, a2 - a:b - a], op=ALU.add),
                                reads=[to] + tl, writes=tl)
                for ti_, (a_, b_) in enumerate(RT):
                    if ti_ in cfg.get('rw_tiles', range(100)):
                        do_tile(ti_, a_, b_)
            for jp_ in range(cfg.get("rwkv_pairs", 8)):
                do_pair(jp_)
            S.barrier()

        for l in range(nL):
            if l % 2 == 0 and cfg.get("gdn", False):
                gdn_layer(l)
            if l % 2 == 1 and cfg.get("rwkv", False):
                rwkv_layer(l)
            if cfg.get("ffn", True):
                ffn_layer(l)

        o = 0
        hnf = arena[:, o:o + 8 * 512].rearrange("p (c n) -> p c n", c=8); o += 4096
        sq = arena[:, o:o + 2048].bitcast(BF16).rearrange("p (c n) -> p c n", c=8); o += 2048
        rs = arena[:, o:o + 512]; o += 512
        orow = [arena[:, o + i * 1024:o + (i + 1) * 1024] for i in range(2)]; o += 2048
        t_hnf, t_sq, t_rs = Trk(), Trk(), Trk()
        t_orow = [Trk(), Trk()]
        oi = 0
        for (a, b) in TILES:
            rmsnorm(hnf, 0, "norm_final", 0, a, b, sq, t_sq, rs, t_rs, t_hnf)
            for k in range((b - a) // 128):
                ob = orow[oi % 2]; tob = t_orow[oi % 2]; oi += 1
                for half in range(2):
                    pst, pt_ = K.ps()
                    for cc in range(4):
                        c = half * 4 + cc
                        S.op("pe", lambda h, pst=pst, cc=cc, c=c, k=k: h.transpose(
                            out=pst[:, cc * 128:(cc + 1) * 128], in_=hnf[:, c, k * 128:(k + 1) * 128],
                            identity=ident[:]), reads=[t_hnf, t_id], writes=[pt_], inc=(cc == 3))
                    if half:
                        S.op("act", lambda h, pst=pst, ob=ob: h.copy(out=ob[:, 512:1024], in_=pst[:]),
                             reads=[pt_], writes=[tob])
                    else:
                        S.op("dve", lambda h, pst=pst, ob=ob: h.tensor_copy(out=ob[:, 0:512], in_=pst[:]),
                             reads=[pt_], writes=[tob])
                S.dma("sp", yout[a + k * 128:a + (k + 1) * 128, :], ob, reads=[tob])
        S.barrier()
        S.emit(block)
    return nc


_CACHE = {}


def kernel(**inp):
    inp = {k: np.asarray(v) for k, v in inp.items()}
    pt = make_ptab(inp)
    ptab = pt.build()
    cfg = dict(_CFG)
    cfg["pt_off"] = pt.off
    cfg["pt_n"] = pt.n
    key = tuple(sorted((k, str(v)) for k, v in cfg.items() if k not in ("pt_off",)))
    if key not in _CACHE:
        _CACHE[key] = build(cfg)
    nc = _CACHE[key]

    xp = inp["x_prompt"].astype(np.float32)
    xs = inp["x_sample"].astype(np.float32)
    meta = inp["meta_tokens"].astype(np.float32)
    in_maps = []
    for b in range(NCORE):
        xin = np.zeros((T, D), np.float32)
        xin[PAD:P0] = meta
        xin[P0:S0] = xp[b]
        xin[S0:] = xs[16 * b:16 * b + 16].reshape(64, D)
        m = {
            "xin": xin,
            "ptab": ptab,
            "ffn_conv_in": np.ascontiguousarray(inp["state_ffn_conv"][:, 16 * b:16 * b + 16]),
            "ffn_w_up": inp["ffn_w_up"],
            "gdn_w_in": inp["gdn_w_in"], "gdn_w_out": inp["gdn_w_out"],
            "rwkv_S_in": np.ascontiguousarray(inp["state_rwkv_S"][:, 16 * b:16 * b + 16]),
            "rwkv_shift_in": np.ascontiguousarray(inp["state_rwkv_shift"][:, 16 * b:16 * b + 16]),
            **{kk_: inp[kk_] for kk_ in ("rwkv_wr", "rwkv_wk", "rwkv_wv", "rwkv_wo", "rwkv_w1", "rwkv_w2", "rwkv_a1",
                                         "rwkv_a2", "rwkv_g1", "rwkv_g2", "rwkv_v1", "rwkv_v2")},
            "gdn_S_in": np.ascontiguousarray(inp["state_gdn_S"][:, 16 * b:16 * b + 16]),
            "gdn_conv_in": np.ascontiguousarray(inp["state_gdn_conv"][:, 16 * b:16 * b + 16]),
            "ffn_w_down": inp["ffn_w_down"],
        }
        in_maps.append(m)
    ncr = cfg.get('ncores', NCORE)
    res = run_bass_kernel_spmd(nc, in_maps[:ncr], core_ids=list(range(ncr)))
    R = list(res.results)
    while len(R) < NCORE:
        R.append(R[0])
    y_prompt = np.stack([R[b]["yout"][P0:S0] for b in range(NCORE)])
    y_sample = np.concatenate([R[b]["yout"][S0:].reshape(16, 4, D) for b in range(NCORE)])
    fC_p = np.stack([R[b]["fC_p"] for b in range(NCORE)], axis=1)
    fC_s = np.concatenate([R[b]["fC_s"] for b in range(NCORE)], axis=1)
    z = lambda *s: np.zeros(s, np.float32)
    rS_p = np.stack([R[b]["rS_p"] for b in range(NCORE)], axis=1)
    rSh_p = np.stack([R[b]["rSh_p"] for b in range(NCORE)], axis=1)
    gS_p = np.stack([R[b]["gS_p"] for b in range(NCORE)], axis=1)
    gC_p = np.stack([R[b]["gC_p"] for b in range(NCORE)], axis=1)
    gS_s = np.concatenate([R[b]["gS_s"] for b in range(NCORE)], axis=1)
    gC_s = np.concatenate([R[b]["gC_s"] for b in range(NCORE)], axis=1)
    rS_s = np.concatenate([R[b]["rS_s"] for b in range(NCORE)], axis=1)
    rSh_s = np.concatenate([R[b]["rSh_s"] for b in range(NCORE)], axis=1)
    return (y_prompt, y_sample, gS_p, gC_p, rS_p, rSh_p, fC_p, gS_s, gC_s, rS_s, rSh_s, fC_s)


_CFG = {"nlayers": 4, "ffn": True, "gdn": True, "rwkv": True}
```

```python
import numpy as np
from contextlib import ExitStack
import concourse.bass as bass
import concourse.mybir as mybir
from concourse.bass_utils import run_bass_kernel_spmd

F32 = mybir.dt.float32
BF16 = mybir.dt.bfloat16
F32R = mybir.dt.float32r
R = lambda ap: ap.bitcast(F32R)
AF = mybir.ActivationFunctionType
ALU = mybir.AluOpType

NCORE = 8
D = 1024
T = 2176
PAD = 48
P0 = 64
S0 = 2112
DFF = 2816
NF = 22
TILES = [(0, 512), (512, 1024), (1024, 1536), (1536, 2048), (2048, 2176)]
FGROUPS = [(0, 768), (768, 1536), (1536, 2176)]


class Trk:
    __slots__ = ("w", "r")

    def __init__(self):
        self.w = None
        self.r = {}


class Sched:
    ENGS = ("pe", "act", "dve", "pool", "sp")

    def __init__(self, nc):
        self.nc = nc
        self.q = {e: [] for e in self.ENGS}
        self.cnt = {}
        self.sem = {}
        self.seen = {e: {} for e in self.ENGS}

    NDS = 24

    def alloc_sems(self, stack):
        self.dslot = {}
        for e in self.ENGS:
            self.sem[e] = stack.enter_context(self.nc.semaphore("s_" + e))
            self.cnt[e] = 0
        for e in ("sp", "pool"):
            self.dslot[e] = 0
            for i in range(self.NDS):
                d = "%s.d%d" % (e, i)
                self.sem[d] = stack.enter_context(self.nc.semaphore("d_%s_%d" % (e, i)))
                self.cnt[d] = 0

    def _waits(self, eng, reads, writes):
        need = {}
        for t in reads:
            if t.w is not None:
                p, c = t.w
                if p == eng and eng == "pe":
                    continue
                if need.get(p, 0) < c:
                    need[p] = c
        for t in writes:
            if t.w is not None:
                p, c = t.w
                if p != eng and need.get(p, 0) < c:
                    need[p] = c
            for p, c in t.r.items():
                if p != eng and need.get(p, 0) < c:
                    need[p] = c
        out = []
        for p, c in need.items():
            if self.seen[eng].get(p, 0) >= c:
                continue
            self.seen[eng][p] = c
            out.append((p, c))
        return out

    @staticmethod
    def _mark(prod, c, reads, writes):
        for t in reads:
            if t.r.get(prod, 0) < c:
                t.r[prod] = c
        for t in writes:
            t.w = (prod, c)
            t.r = {}

    def op(self, eng, fn, reads=(), writes=(), inc=True):
        waits = self._waits(eng, reads, writes)
        c = self.cnt[eng] + 1
        if inc:
            self.cnt[eng] = c
        sem = self.sem

        def run(h):
            for p, cc in waits:
                h.wait_ge(sem[p], cc * 16 if ".d" in p else cc)
            ins = fn(h)
            if inc:
                ins.then_inc(sem[eng], 1)
        self.q[eng].append(run)
        self._mark(eng, c, reads, writes)

    def dma(self, eng, out, in_, reads=(), writes=(), **kw):
        waits = self._waits(eng, reads, writes)
        d = "%s.d%d" % (eng, self.dslot[eng] % self.NDS)
        self.dslot[eng] += 1
        if self.cnt[d] > 0 and self.seen[eng].get(d, 0) < self.cnt[d]:
            self.seen[eng][d] = self.cnt[d]
            waits = waits + [(d, self.cnt[d])]
        self.cnt[d] += 1
        c = self.cnt[d]
        sem = self.sem

        def run(h):
            for p, cc in waits:
                h.wait_ge(sem[p], cc * 16 if ".d" in p else cc)
            h.dma_start(out=out, in_=in_, **kw).then_inc(sem[d], 16)
        self.q[eng].append(run)
        self._mark(d, c, reads, writes)

    def barrier(self):
        sem = self.sem
        finals = [(p, c) for p, c in self.cnt.items() if c > 0]
        for e in self.ENGS:
            mine = []
            for p, c in finals:
                if self.seen[e].get(p, 0) < c:
                    self.seen[e][p] = c
                    mine.append((p, c))

            def run(h, mine=mine):
                for p, cc in mine:
                    h.wait_ge(sem[p], cc * 16 if ".d" in p else cc)
            self.q[e].append(run)

    def emit(self, block):
        for e, deco in (("pe", block.tensor), ("act", block.scalar), ("dve", block.vector),
                        ("pool", block.gpsimd), ("sp", block.sync)):
            lst = self.q[e]

            def body(h, lst=lst):
                for r in lst:
                    r(h)
            deco(body)


def _feat(v):
    v = np.asarray(v, np.float32)
    n = v.shape[-1] // 128
    v = v.reshape(v.shape[:-1] + (n, 128))
    return np.ascontiguousarray(np.moveaxis(v, -1, 0))


class PTab:
    def __init__(self):
        self.cols = []
        self.off = {}
        self.n = 0

    def add(self, name, arr):
        arr = np.asarray(arr, np.float32).reshape(128, -1)
        self.off[name] = self.n
        self.cols.append(arr)
        self.n += arr.shape[1]

    def build(self):
        return np.ascontiguousarray(np.concatenate(self.cols, axis=1))


def make_ptab(inp):
    pt = PTab()
    pt.add("norm_mix", _feat(inp["norm_mix"]))
    pt.add("norm_ffn", _feat(inp["norm_ffn"]))
    pt.add("norm_final", _feat(inp["norm_final"]))
    pt.add("ffn_cw", _feat(inp["ffn_conv_w"]))
    pt.add("gdn_cw", _feat(inp["gdn_conv_w"]))
    pt.add("gdn_nw", np.asarray(inp["gdn_norm_w"], np.float32).T)
    z = np.zeros((128, 2), np.float32)
    ga = z.copy(); ga[64:72] = np.asarray(inp["gdn_A_log"], np.float32).T
    gd = z.copy(); gd[64:72] = np.asarray(inp["gdn_dt_bias"], np.float32).T
    pt.add("rw_mix", _feat(inp["rwkv_mix"]))
    for nm_, key_ in (("rw_w0", "rwkv_w0"), ("rw_a0", "rwkv_a0"), ("rw_kk", "rwkv_k_k"), ("rw_ka", "rwkv_k_a"),
                      ("rw_lnw", "rwkv_ln_w"), ("rw_lnb", "rwkv_ln_b"), ("rw_v0", "rwkv_v0")):
        pt.add(nm_, _feat(inp[key_]))
    pt.add("rw_rk", _feat(np.asarray(inp["rwkv_r_k"], np.float32).reshape(2, D)))
    pt.add("gdn_A", ga)
    pt.add("gdn_dt", gd)
    return pt


PT_OFF = None


class KB:
    def __init__(self, nc, st, S):
        self.nc = nc
        self.st = st
        self.S = S
        self.psb = []
        self.psi = 0

    def sb(self, name, shape, dt):
        return self.st.enter_context(self.nc.sbuf_tensor(name, shape, dt))

    def init_psum(self):
        for i in range(8):
            t = self.st.enter_context(self.nc.psum_tensor("ps%d" % i, [128, 512], F32))
            self.psb.append((t, Trk()))

    nrot = 6

    def ps(self):
        r = self.psb[self.psi % self.nrot]
        self.psi = (self.psi + 1) % self.nrot
        return r


def build(cfg):
    nc = bass.Bass("TRN2", target_bir_lowering=False)
    dram_in = lambda n, s: nc.dram_tensor(n, list(s), F32, kind="ExternalInput").ap()
    dram_out = lambda n, s: nc.dram_tensor(n, list(s), F32, kind="ExternalOutput").ap()
    nL = cfg.get("nlayers", 4)
    pt_off = cfg["pt_off"]
    pt_n = cfg["pt_n"]

    xin = dram_in("xin", (T, D))
    ptab_d = dram_in("ptab", (128, pt_n))
    ffn_conv_in = dram_in("ffn_conv_in", (4, 16, 2, DFF))
    w_up = dram_in("ffn_w_up", (4, D, 2 * DFF))
    w_down = dram_in("ffn_w_down", (4, DFF, D))
    gdn_w_in = dram_in("gdn_w_in", (2, D, 4112))
    gdn_w_out = dram_in("gdn_w_out", (2, D, D))
    gdn_S_in = dram_in("gdn_S_in", (2, 16, 8, 128, 128))
    gdn_conv_in = dram_in("gdn_conv_in", (2, 16, 3, 3072))
    gS_p = dram_out("gS_p", (2, 8, 128, 128))
    gC_p = dram_out("gC_p", (2, 3, 3072))
    gS_s = dram_out("gS_s", (2, 16, 8, 128, 128))
    gC_s = dram_out("gC_s", (2, 16, 3, 3072))
    rwkv_S_in = dram_in("rwkv_S_in", (2, 16, 16, 64, 64))
    rwkv_shift_in = dram_in("rwkv_shift_in", (2, 16, D))
    rwkv_wr = dram_in("rwkv_wr", (2, D, D)); rwkv_wk = dram_in("rwkv_wk", (2, D, D))
    rwkv_wv = dram_in("rwkv_wv", (2, D, D)); rwkv_wo = dram_in("rwkv_wo", (2, D, D))
    rwkv_w1 = dram_in("rwkv_w1", (2, D, 64)); rwkv_w2 = dram_in("rwkv_w2", (2, 64, D))
    rwkv_a1 = dram_in("rwkv_a1", (2, D, 64)); rwkv_a2 = dram_in("rwkv_a2", (2, 64, D))
    rwkv_g1 = dram_in("rwkv_g1", (2, D, 160)); rwkv_g2 = dram_in("rwkv_g2", (2, 160, D))
    rwkv_v1 = dram_in("rwkv_v1", (1, D, 32)); rwkv_v2 = dram_in("rwkv_v2", (1, 32, D))
    vf_d = nc.dram_tensor("vfirst_scratch", [8, 128, T], F32).ap()
    rS_p = dram_out("rS_p", (2, 16, 64, 64))
    rSh_p = dram_out("rSh_p", (2, D))
    rS_s = dram_out("rS_s", (2, 16, 16, 64, 64))
    rSh_s = dram_out("rSh_s", (2, 16, D))
    yout = dram_out("yout", (T, D))
    fC_p = dram_out("fC_p", (4, 2, DFF))
    fC_s = dram_out("fC_s", (4, 16, 2, DFF))

    with ExitStack() as st:
        S = Sched(nc)
        S.alloc_sems(st)
        K = KB(nc, st, S)
        K.init_psum()
        sb = K.sb
        xT = sb("xT", [128, 8, T], F32)
        t_x = [Trk() for _ in TILES]
        ptab = sb("ptab_sb", [128, pt_n], F32); t_pt = Trk()
        ident = sb("ident", [128, 128], F32); t_id = Trk()
        onesb = sb("onesb", [128, 128], BF16); t_ones = Trk()
        cst = sb("cst", [128, 4], F32); t_cst = Trk()
        ARENA = 30500
        arena = sb("arena", [128, ARENA], F32)
        ARENAR = 3584
        arenaR_ = sb("arenaR", [128, ARENAR], F32R)
        arenaR = arenaR_[:].bitcast(F32)
        block = st.enter_context(nc.Block())

        def tile_of(c0, c1):
            return [t_x[i] for i, (a, b) in enumerate(TILES) if a < c1 and c0 < b]

        S.dma("sp", ptab[:], ptab_d, writes=[t_pt])
        S.op("pool", lambda h: h.memset(ident[:], 0.0), writes=[t_id])
        S.op("pool", lambda h: h.affine_select(out=ident[:], in_=ident[:], pattern=[[-1, 128]],
                                                 compare_op=ALU.not_equal, fill=1.0, base=0,
                                                 channel_multiplier=1), reads=[t_id], writes=[t_id])
        S.op("pool", lambda h: h.memset(onesb[:], 1.0 / 1024.0), writes=[t_ones])
        S.op("pool", lambda h: h.memset(cst[:, 0:1], 1e-6), writes=[t_cst])
        S.op("pool", lambda h: h.memset(cst[:, 1:2], 64e-5), writes=[t_cst])
        S.op("pool", lambda h: h.memset(cst[:, 2:3], 1.0), writes=[t_cst])
        S.op("pool", lambda h: h.memset(cst[:, 3:4], 0.0), writes=[t_cst])

        tri_t = sb("tri_t", [64, 4, 64], F32); t_tri = Trk()
        rmk = sb("rmk", [64, 16], F32)
        tri_s, tri_i, trib_s, trib_i = tri_t[:, 0, :], tri_t[:, 1, :], tri_t[:, 2, :], tri_t[:, 3, :]
        S.op("pool", lambda h: h.memset(tri_t[:], 1.0), writes=[t_tri])
        S.op("pool", lambda h: h.memset(rmk[:], 1.0), writes=[t_tri])
        for (ix, op_) in ((0, ALU.is_gt), (1, ALU.is_ge), (2, ALU.is_gt), (3, ALU.is_ge)):
            S.op("pool", lambda h, ix=ix, op_=op_: h.affine_select(
                out=tri_t[:, ix, :], in_=tri_t[:, ix, :], pattern=[[1, 64]], compare_op=op_, fill=0.0, base=0,
                channel_multiplier=-1), reads=[t_tri], writes=[t_tri])
        for ix in (2, 3):
            v = tri_t[:, ix, :].rearrange("p (s k) -> p s k", k=4)
            S.op("pool", lambda h, v=v: h.affine_select(out=v, in_=v, pattern=[[-4, 16], [0, 4]], compare_op=ALU.is_ge,
                                                       fill=0.0, base=0, channel_multiplier=1),
                 reads=[t_tri], writes=[t_tri])
            S.op("pool", lambda h, v=v: h.affine_select(out=v, in_=v, pattern=[[4, 16], [0, 4]], compare_op=ALU.is_ge,
                                                       fill=0.0, base=3, channel_multiplier=-1),
                 reads=[t_tri], writes=[t_tri])
        S.op("pool", lambda h: h.affine_select(out=rmk[:], in_=rmk[:], pattern=[[-4, 16]], compare_op=ALU.is_ge,
                                               fill=0.0, base=0, channel_multiplier=1), reads=[t_tri], writes=[t_tri])
        S.op("pool", lambda h: h.affine_select(out=rmk[:], in_=rmk[:], pattern=[[4, 16]], compare_op=ALU.is_ge,
                                               fill=0.0, base=3, channel_multiplier=-1), reads=[t_tri], writes=[t_tri])

        def pcol(name, idx):
            o = pt_off[name] + idx
            return ptab[:, o:o + 1]

        a_f = arena[:, 0:2 * 1024].rearrange("p (i n) -> p i n", i=2)
        t_in = [Trk(), Trk()]
        for it in range(T // 128):
            buf = a_f[:, it % 2, :]
            tb = t_in[it % 2]
            S.dma("sp", buf, xin[it * 128:(it + 1) * 128, :], writes=[tb])
            for half in range(2):
                pst, pt_ = K.ps()
                for cc in range(4):
                    c = half * 4 + cc
                    S.op("pe", lambda h, pst=pst, cc=cc, buf=buf, c=c: h.transpose(
                        out=pst[:, cc * 128:(cc + 1) * 128], in_=buf[:, c * 128:(c + 1) * 128],
                        identity=ident[:]), reads=[tb, t_id], writes=[pt_], inc=(cc == 3))
                tl = tile_of(it * 128, it * 128 + 128)
                S.op("act" if half else "dve",
                     (lambda h, pst=pst, half=half, it=it: h.copy(
                         out=xT[:, half * 4:half * 4 + 4, it * 128:(it + 1) * 128],
                         in_=pst[:].rearrange("p (c n) -> p c n", c=4))) if half else
                     (lambda h, pst=pst, half=half, it=it: h.tensor_copy(
                         out=xT[:, half * 4:half * 4 + 4, it * 128:(it + 1) * 128],
                         in_=pst[:].rearrange("p (c n) -> p c n", c=4))),
                     reads=[pt_], writes=tl)
        S.barrier()

        def rmsnorm(dst, dc0, gname, gidx, c0, c1, sq, t_sq, rs, t_rs, t_dst):
            for (a, b) in TILES:
                a2, b2 = max(a, c0), min(b, c1)
                if a2 >= b2:
                    continue
                n = b2 - a2
                tl = tile_of(a2, b2)
                for c in range(8):
                    S.op("act", lambda h, c=c, a2=a2, b2=b2, n=n: h.activation(
                        out=sq[:, c, 0:n], in_=xT[:, c, a2:b2], func=AF.Square),
                        reads=tl, writes=[t_sq])
                pst, pt_ = K.ps()
                for c in range(8):
                    S.op("pe", lambda h, c=c, n=n, pst=pst: h.matmul(
                        pst[:, 0:n], lhsT=onesb[:], rhs=sq[:, c, 0:n], start=(c == 0), stop=(c == 7)),
                        reads=[t_sq, t_ones], writes=[pt_], inc=(c == 7))
                S.op("act", lambda h, n=n, pst=pst: h.activation(
                    out=rs[:, 0:n], in_=pst[:, 0:n], func=AF.Sqrt, bias=cst[:, 0:1], scale=1.0),
                    reads=[pt_, t_cst], writes=[t_rs])
                S.op("dve", lambda h, n=n: h.reciprocal(out=rs[:, 0:n], in_=rs[:, 0:n]),
                     reads=[t_rs], writes=[t_rs])
                for c in range(8):
                    S.op("dve", lambda h, c=c, a2=a2, b2=b2, n=n: h.scalar_tensor_tensor(
                        out=dst[:, c, dc0 + a2 - c0:dc0 + b2 - c0], in0=xT[:, c, a2:b2],
                        scalar=pcol(gname, gidx * 8 + c), in1=rs[:, 0:n], op0=ALU.mult, op1=ALU.mult),
                        reads=tl + [t_rs, t_pt], writes=[t_dst])

        def ffn_layer(l):
            o = 0
            actT = arena[:, o:o + 8448].bitcast(BF16).rearrange("p (f n) -> p f n", f=NF); o += 8448
            hn2 = arena[:, o:o + 3072].bitcast(BF16).rearrange("p (c n) -> p c n", c=8); o += 3072
            wdb = [arena[:, o + i * 1408:o + (i + 1) * 1408].bitcast(BF16).rearrange(
                "p (f n) -> p f n", f=NF) for i in range(2)]; o += 2816
            wu = [arena[:, o + i * 2048:o + (i + 1) * 2048].bitcast(BF16).rearrange(
                "p (c g n) -> p c g n", c=8, g=2) for i in range(2)]; o += 4096
            sq = arena[:, o:o + 2048].bitcast(BF16).rearrange("p (c n) -> p c n", c=8); o += 2048
            rs = arena[:, o:o + 512]; o += 512
            gpre = [arena[:, o + i * 516:o + (i + 1) * 516] for i in range(2)]; o += 1032
            gps = arena[:, o:o + 96].rearrange("p (s k) -> p s k", s=16); o += 96
            gc = arena[:, o:o + 512]; o += 512
            carry = arena[:, o:o + 44].rearrange("p (f k) -> p f k", f=NF); o += 44
            hist = arena[:, o:o + NF * 32].rearrange("p (f s k) -> p f s k", f=NF, s=16); o += NF * 32
            stg = arena[:, o:o + NF * 34].rearrange("p (f k) -> p f k", f=NF); o += NF * 34
            rows = arena[0:34, 0:DFF]
            hrow = arena[0:32, 0:DFF]
            assert o <= ARENA, o
            t_act, t_hn2, t_sq, t_rs, t_gc = Trk(), Trk(), Trk(), Trk(), Trk()
            t_wdb = [Trk(), Trk()]
            t_wu = [Trk(), Trk()]
            t_gpre = [Trk(), Trk()]
            t_gps, t_carry, t_hist, t_stg = Trk(), Trk(), Trk(), Trk()
            t_rows = t_act
            t_hrow = t_act

            wd_src = w_down[l].rearrange("(f p) n -> p f n", p=128)
            S.dma("sp", hrow, ffn_conv_in[l].rearrange("s k n -> (s k) n"), writes=[t_hrow])
            for f0 in range(0, NF, 4):
                pst, pt_ = K.ps()
                nf = min(4, NF - f0)
                for ff in range(nf):
                    f = f0 + ff
                    S.op("pe", lambda h, pst=pst, ff=ff, f=f: h.transpose(
                        out=pst[:, ff * 32:(ff + 1) * 32], in_=hrow[:, f * 128:(f + 1) * 128],
                        identity=ident[0:32, 0:32]), reads=[t_hrow, t_id], writes=[pt_], inc=(ff == nf - 1))
                S.op("dve", lambda h, pst=pst, f0=f0, nf=nf: h.tensor_copy(
                    out=hist[:, f0:f0 + nf, :, :].rearrange("p f s k -> p f (s k)"),
                    in_=pst[:, 0:nf * 32].rearrange("p (f n) -> p f n", f=nf)),
                    reads=[pt_], writes=[t_hist])
            S.op("dve", lambda h: h.memset(carry, 0.0), writes=[t_carry])

            up_src = w_up[l].rearrange("(c p) n -> p c n", p=128)
            wi = 0
            wdi = 0
            for (g0_, g1_) in FGROUPS:
              def do_group(g0, g1):
                  nonlocal wi, wdi
                  rmsnorm(hn2, 0, "norm_ffn", l, g0, g1, sq, t_sq, rs, t_rs, t_hn2)
                  gtiles = [(max(a, g0), min(b, g1)) for (a, b) in TILES if max(a, g0) < min(b, g1)]
                  for f2 in range(0, NF, 2):
                      wt = wu[wi % 2]; twu = t_wu[wi % 2]; wi += 1
                      S.dma("pool", wt[:, :, 0, :], up_src[:, :, f2 * 128:f2 * 128 + 256], writes=[twu])
                      S.dma("pool", wt[:, :, 1, :], up_src[:, :, DFF + f2 * 128:DFF + f2 * 128 + 256],
                            writes=[twu])
                      for fi in range(2):
                          f = f2 + fi
                          cw = lambda j, f=f: pcol("ffn_cw", (l * 3 + j) * NF + f)
                          for ti, (a, b) in enumerate(gtiles):
                              n = b - a
                              psg, tg = K.ps()
                              psu, tu = K.ps()
                              for c in range(8):
                                  S.op("pe", lambda h, c=c, a=a, b=b, n=n, psg=psg, wt=wt, fi=fi: h.matmul(
                                      psg[:, 0:n], lhsT=wt[:, c, 0, fi * 128:(fi + 1) * 128],
                                      rhs=hn2[:, c, a - g0:b - g0], start=(c == 0), stop=(c == 7)),
                                      reads=[twu, t_hn2], writes=[tg], inc=(c == 7))
                              for c in range(8):
                                  S.op("pe", lambda h, c=c, a=a, b=b, n=n, psu=psu, wt=wt, fi=fi: h.matmul(
                                      psu[:, 0:n], lhsT=wt[:, c, 1, fi * 128:(fi + 1) * 128],
                                      rhs=hn2[:, c, a - g0:b - g0], start=(c == 0), stop=(c == 7)),
                                      reads=[twu, t_hn2], writes=[tu], inc=(c == 7))
                              gp = gpre[ti % 2]; tgp = t_gpre[ti % 2]
                              S.op("dve", lambda h, gp=gp, f=f: h.tensor_copy(out=gp[:, 0:2], in_=carry[:, f, :]),
                                   reads=[t_carry], writes=[tgp])
                              S.op("act", lambda h, gp=gp, n=n, psg=psg: h.copy(out=gp[:, 2:2 + n], in_=psg[:, 0:n]),
                                   reads=[tg], writes=[tgp])
                              S.op("dve", lambda h, gp=gp, n=n, f=f: h.tensor_copy(out=carry[:, f, :],
                                                                                  in_=gp[:, n:n + 2]),
                                   reads=[tgp], writes=[t_carry])
                              npz = n if b <= S0 else S0 - a
                              S.op("dve", lambda h, gp=gp, npz=npz, cw=cw: h.tensor_scalar(
                                  out=gc[:, 0:npz], in0=gp[:, 2:2 + npz], scalar1=cw(2), scalar2=None,
                                  op0=ALU.mult), reads=[tgp, t_pt], writes=[t_gc])
                              S.op("dve", lambda h, gp=gp, npz=npz, cw=cw: h.scalar_tensor_tensor(
                                  out=gc[:, 0:npz], in0=gp[:, 1:1 + npz], scalar=cw(1), in1=gc[:, 0:npz],
                                  op0=ALU.mult, op1=ALU.add), reads=[tgp, t_pt, t_gc], writes=[t_gc])
                              S.op("dve", lambda h, gp=gp, npz=npz, cw=cw: h.scalar_tensor_tensor(
                                  out=gc[:, 0:npz], in0=gp[:, 0:npz], scalar=cw(0), in1=gc[:, 0:npz],
                                  op0=ALU.mult, op1=ALU.add), reads=[tgp, t_pt, t_gc], writes=[t_gc])
                              if b > S0:
                                  so = S0 - a
                                  S.op("dve", lambda h, f=f: h.tensor_copy(out=gps[:, :, 0:2], in_=hist[:, f, :, :]),
                                       reads=[t_hist], writes=[t_gps])
                                  S.op("dve", lambda h, gp=gp, so=so: h.tensor_copy(
                                      out=gps[:, :, 2:6], in_=gp[:, 2 + so:2 + so + 64].rearrange("p (s k) -> p s k", s=16)),
                                      reads=[tgp], writes=[t_gps])
                                  gcs = gc[:, so:so + 64].rearrange("p (s k) -> p s k", s=16)
                                  S.op("dve", lambda h, gcs=gcs, cw=cw: h.tensor_scalar(
                                      out=gcs, in0=gps[:, :, 2:6], scalar1=cw(2), scalar2=None, op0=ALU.mult),
                                      reads=[t_gps, t_pt], writes=[t_gc])
                                  S.op("dve", lambda h, gcs=gcs, cw=cw: h.scalar_tensor_tensor(
                                      out=gcs, in0=gps[:, :, 1:5], scalar=cw(1), in1=gcs, op0=ALU.mult, op1=ALU.add),
                                      reads=[t_gps, t_pt, t_gc], writes=[t_gc])
                                  S.op("dve", lambda h, gcs=gcs, cw=cw: h.scalar_tensor_tensor(
                                      out=gcs, in0=gps[:, :, 0:4], scalar=cw(0), in1=gcs, op0=ALU.mult, op1=ALU.add),
                                      reads=[t_gps, t_pt, t_gc], writes=[t_gc])
                                  S.op("pool", lambda h, gp=gp, so=so, f=f: h.tensor_copy(
                                      out=stg[:, f, 0:2], in_=gp[:, so:so + 2]), reads=[tgp], writes=[t_stg])
                                  S.op("pool", lambda h, f=f: h.tensor_copy(
                                      out=stg[:, f, 2:34].rearrange("p (s k) -> p s k", s=16), in_=gps[:, :, 4:6]),
                                      reads=[t_gps], writes=[t_stg])
                              S.op("act", lambda h, n=n: h.activation(out=gc[:, 0:n], in_=gc[:, 0:n], func=AF.Silu),
                                   reads=[t_gc], writes=[t_gc])
                              S.op("dve", lambda h, n=n, a=a, b=b, f=f, psu=psu: h.tensor_tensor(
                                  out=actT[:, f, a - g0:b - g0], in0=gc[:, 0:n], in1=psu[:, 0:n], op=ALU.mult),
                                  reads=[t_gc, tu], writes=[t_act])
                  for oc in range(8):
                      wd = wdb[wdi % 2]; t_wd = t_wdb[wdi % 2]; wdi += 1
                      S.dma("pool", wd, wd_src[:, :, oc * 128:(oc + 1) * 128], writes=[t_wd])
                      for (a, b) in gtiles:
                          n = b - a
                          pso, to = K.ps()
                          for f in range(NF):
                              S.op("pe", lambda h, f=f, a=a, b=b, n=n, pso=pso, wd=wd: h.matmul(
                                  pso[:, 0:n], lhsT=wd[:, f, :], rhs=actT[:, f, a - g0:b - g0],
                                  start=(f == 0), stop=(f == NF - 1)),
                                  reads=[t_wd, t_act], writes=[to], inc=(f == NF - 1))
                          a2 = max(a, PAD)
                          tl = tile_of(a2, b)
                          S.op("dve", lambda h, a2=a2, a=a, b=b, pso=pso, oc=oc: h.tensor_tensor(
                              out=xT[:, oc, a2:b], in0=xT[:, oc, a2:b], in1=pso[:, a2 - a:b - a], op=ALU.add),
                              reads=[to] + tl, writes=tl)

              do_group(g0_, g1_)
            for f0 in range(0, NF, 4):
                nf = min(4, NF - f0)
                pst, pt_ = K.ps()
                for ff in range(nf):
                    f = f0 + ff
                    S.op("pe", lambda h, pst=pst, ff=ff, f=f: h.transpose(
                        out=pst[0:34, ff * 128:(ff + 1) * 128], in_=stg[:, f, :], identity=ident[:]),
                        reads=[t_stg, t_id], writes=[pt_], inc=(ff == nf - 1))
                S.op("act", lambda h, pst=pst, f0=f0, nf=nf: h.copy(
                    out=rows[:, f0 * 128:(f0 + nf) * 128], in_=pst[0:34, 0:nf * 128]),
                    reads=[pt_], writes=[t_rows])
            S.dma("sp", fC_p[l], rows[0:2, :], reads=[t_rows])
            S.dma("sp", fC_s[l].rearrange("s k n -> (s k) n"), rows[2:34, :], reads=[t_rows])
            S.barrier()

        Y_BANKS = [K.psb[6], K.psb[7]]

        def solve_blocks(nb, B0, Ca, Cb, Ba, Bb, Pa, Pb, t_B0, t_C, t_B, t_P):
            ngr = (nb + 7) // 8
            def grp(g):
                return range(g * 8, min(nb, g * 8 + 8))
            C = [Ca, Cb]; B = [B0, Ba, Bb]
            for g in range(ngr):
                pst, pt_ = K.ps()
                bl = list(grp(g))
                for i, b_ in enumerate(bl):
                    S.op("pe", lambda h, pst=pst, i=i, b_=b_: h.transpose(
                        out=pst[0:64, i * 64:(i + 1) * 64], in_=B0[:, b_, :], identity=ident[0:64, 0:64]),
                        reads=[t_B0, t_id], writes=[pt_], inc=(i == len(bl) - 1))
                S.op("act", lambda h, pst=pst, bl=bl: h.copy(
                    out=R(Ca[:, bl[0]:bl[-1] + 1, :]), in_=pst[0:64, 0:len(bl) * 64].rearrange("p (b n) -> p b n", n=64)),
                    reads=[pt_], writes=[t_C[0]])
            S.op("dve", lambda h: h.tensor_tensor(
                out=R(Pa[:, 0:nb, :]), in0=B0[:, 0:nb, :], in1=ident[0:64, 0:64].unsqueeze(1).to_broadcast([64, nb, 64]),
                op=ALU.add), reads=[t_B0, t_id], writes=[t_P[0]])
            Bcur, tBcur = B0, t_B0
            Ccur, tCcur = Ca, t_C[0]
            Pcur, tPcur, Pn, tPn = Pa, t_P[0], Pb, t_P[1]
            Bn = [Ba, Bb]; tBn = [t_B[0], t_B[1]]
            Cn = [Cb, Ca]; tCn = [t_C[1], t_C[0]]
            for i in range(1, 6):
                Cnew, tCnew = Cn[(i - 1) % 2], tCn[(i - 1) % 2]
                Bnew, tBnew = Bn[(i - 1) % 2], tBn[(i - 1) % 2]
                for g in range(ngr):
                    pst, pt_ = K.ps()
                    bl = list(grp(g))
                    for k_, b_ in enumerate(bl):
                        S.op("pe", lambda h, pst=pst, k_=k_, b_=b_, Bcur=Bcur, Ccur=Ccur: h.matmul(
                            pst[0:64, k_ * 64:(k_ + 1) * 64], lhsT=R(Bcur[:, b_, :]), rhs=R(Ccur[:, b_, :]),
                            start=True, stop=True), reads=[tBcur, tCcur], writes=[pt_], inc=(k_ == len(bl) - 1))
                    S.op("act", lambda h, pst=pst, bl=bl, Cnew=Cnew: h.copy(
                        out=R(Cnew[:, bl[0]:bl[-1] + 1, :]),
                        in_=pst[0:64, 0:len(bl) * 64].rearrange("p (b n) -> p b n", n=64)),
                        reads=[pt_], writes=[tCnew])
                if i < 5:
                    for g in range(ngr):
                        pst, pt_ = K.ps()
                        bl = list(grp(g))
                        for k_, b_ in enumerate(bl):
                            S.op("pe", lambda h, pst=pst, k_=k_, b_=b_, Bcur=Bcur, Ccur=Ccur: h.matmul(
                                pst[0:64, k_ * 64:(k_ + 1) * 64], lhsT=R(Ccur[:, b_, :]), rhs=R(Bcur[:, b_, :]),
                                start=True, stop=True), reads=[tBcur, tCcur], writes=[pt_], inc=(k_ == len(bl) - 1))
                        S.op("dve", lambda h, pst=pst, bl=bl, Bnew=Bnew: h.tensor_copy(
                            out=R(Bnew[:, bl[0]:bl[-1] + 1, :]),
                            in_=pst[0:64, 0:len(bl) * 64].rearrange("p (b n) -> p b n", n=64)),
                            reads=[pt_], writes=[tBnew])
                for g in range(ngr):
                    pst, pt_ = K.ps()
                    bl = list(grp(g))
                    for k_, b_ in enumerate(bl):
                        S.op("pe", lambda h, pst=pst, k_=k_, b_=b_, Cnew=Cnew, Pcur=Pcur: h.matmul(
                            pst[0:64, k_ * 64:(k_ + 1) * 64], lhsT=R(Cnew[:, b_, :]), rhs=R(Pcur[:, b_, :]),
                            start=True, stop=True), reads=[tCnew, tPcur], writes=[pt_], inc=(k_ == len(bl) - 1))
                    S.op("dve", lambda h, pst=pst, bl=bl, Pn=Pn, Pcur=Pcur: h.tensor_tensor(
                        out=R(Pn[:, bl[0]:bl[-1] + 1, :]), in0=Pcur[:, bl[0]:bl[-1] + 1, :],
                        in1=pst[0:64, 0:len(bl) * 64].rearrange("p (b n) -> p b n", n=64), op=ALU.add),
                        reads=[pt_, tPcur], writes=[tPn])
                Bcur, tBcur = Bnew, tBnew
                Ccur, tCcur = Cnew, tCnew
                Pcur, tPcur, Pn, tPn = Pn, tPn, Pcur, tPcur
            return Pcur, tPcur

        def seq_block(blk, dv, pr, fcols, states, ops, ybank, ycols, yparts, tcols=(0, 64)):
            p0, p1 = pr
            c0, c1 = fcols
            (ns, t_ns), (rs_, t_rs_) = ops["ns"], ops["rs"]
            (V, t_V), (Kh, t_Kh), (nPh, t_nPh) = ops["V"], ops["Kh"], ops["nPh"]
            (Ank, t_Ank), (Ark, t_Ark), (nArp, t_nArp), (TT, t_TT) = ops["Ank"], ops["Ark"], ops["nArp"], ops["TT"]
            (Xs, t_Xs), (Es, t_Es) = ops["Xs"], ops["Es"]
            single = len(states) == 1
            psx, tpx = K.ps()
            if single:
                stt = states[0]
                S.op("pe", lambda h: h.matmul(psx[0:64, 0:dv], lhsT=Ank[:, blk, :], rhs=V[:, blk, 0:dv],
                                              start=True, stop=False), reads=[t_Ank, t_V], writes=[tpx], inc=False)
                S.op("pe", lambda h: h.matmul(psx[0:64, 0:dv], lhsT=ns[p0:p1, c0:c1], rhs=stt["Z"],
                                              start=False, stop=True), reads=[t_ns, stt["tZ"]], writes=[tpx])
                S.op("act", lambda h: h.copy(out=Xs[:, 0:dv], in_=psx[0:64, 0:dv]), reads=[tpx], writes=[t_Xs])
            else:
                S.op("pe", lambda h: h.matmul(psx[0:dv, 0:64], lhsT=V[:, blk, 0:dv], rhs=Ank[:, blk, :],
                                              start=True, stop=False), reads=[t_Ank, t_V], writes=[tpx], inc=False)
                for si, stt in enumerate(states):
                    S.op("pe", lambda h, stt=stt, si=si: h.matmul(
                        psx[0:dv, stt["lo"]:stt["hi"]], lhsT=stt["Z"], rhs=ns[p0:p1, c0 + stt["lo"]:c0 + stt["hi"]],
                        start=False, stop=(si == len(states) - 1)), reads=[t_ns, stt["tZ"]], writes=[tpx],
                        inc=(si == len(states) - 1))
                XT, t_XT = ops["XT"]
                S.op("act", lambda h: h.copy(out=XT[0:dv, :], in_=psx[0:dv, 0:64]), reads=[tpx], writes=[t_XT])
                psx2, tpx2 = K.ps()
                S.op("pe", lambda h: h.transpose(out=psx2[0:64, 0:dv], in_=XT[0:dv, :], identity=ident[0:dv, 0:dv]),
                     reads=[t_XT, t_id], writes=[tpx2])
                S.op("act", lambda h: h.copy(out=Xs[:, 0:dv], in_=psx2[0:64, 0:dv]), reads=[tpx2], writes=[t_Xs])
            pse, tpe = K.ps()
            S.op("pe", lambda h: h.matmul(pse[0:64, 0:dv], lhsT=TT[:, blk, :], rhs=Xs[:, 0:dv], start=True, stop=True),
                 reads=[t_TT, t_Xs], writes=[tpe])
            S.op("dve", lambda h: h.tensor_copy(out=Es[:, 0:dv], in_=pse[0:64, 0:dv]), reads=[tpe], writes=[t_Es])
            yb, tyb = ybank
            q0, q1 = yparts
            tl0, tl1 = tcols
            S.op("pe", lambda h: h.matmul(yb[q0:q1, ycols[0] + tl0:ycols[0] + tl1], lhsT=V[:, blk, 0:dv],
                                          rhs=Ark[:, blk, tl0:tl1],
                                          start=True, stop=False), reads=[t_V, t_Ark], writes=[tyb], inc=False)
            for stt in states:
                S.op("pe", lambda h, stt=stt: h.matmul(
                    yb[q0:q1, ycols[0] + stt["lo"]:ycols[0] + stt["hi"]], lhsT=stt["Z"],
                    rhs=rs_[p0:p1, c0 + stt["lo"]:c0 + stt["hi"]], start=False, stop=False),
                    reads=[t_rs_, stt["tZ"]], writes=[tyb], inc=False)
            S.op("pe", lambda h: h.matmul(yb[q0:q1, ycols[0] + tl0:ycols[0] + tl1], lhsT=Es[:, 0:dv],
                                          rhs=nArp[:, blk, tl0:tl1],
                                          start=False, stop=True), reads=[t_Es, t_nArp], writes=[tyb])
            for stt in states:
                psz, tpz = K.ps()
                if stt.get("mask") is not None:
                    Km, t_Km = ops["Km"]; nPm, t_nPm = ops["nPm"]
                    S.op("pool", lambda h, stt=stt: h.tensor_scalar(out=Km, in0=Kh[:, blk, :], scalar1=stt["mask"],
                                                                   scalar2=None, op0=ALU.mult),
                         reads=[t_Kh, ops["t_mask"]], writes=[t_Km])
                    S.op("pool", lambda h, stt=stt: h.tensor_scalar(out=nPm, in0=nPh[:, blk, :], scalar1=stt["mask"],
                                                                   scalar2=None, op0=ALU.mult),
                         reads=[t_nPh, ops["t_mask"]], writes=[t_nPm])
                    lk, tlk, lp, tlp = Km, t_Km, nPm, t_nPm
                else:
                    lk, tlk, lp, tlp = Kh[:, blk, :], t_Kh, nPh[:, blk, :], t_nPh
                S.op("pe", lambda h, psz=psz, lk=lk: h.matmul(psz[p0:p1, 0:dv], lhsT=lk, rhs=V[:, blk, 0:dv],
                                                              start=True, stop=False),
                     reads=[tlk, t_V], writes=[tpz], inc=False)
                S.op("pe", lambda h, psz=psz, lp=lp: h.matmul(psz[p0:p1, 0:dv], lhsT=lp, rhs=Es[:, 0:dv],
                                                              start=False, stop=True),
                     reads=[tlp, t_Es], writes=[tpz])
                S.op("dve", lambda h, stt=stt, psz=psz: h.scalar_tensor_tensor(
                    out=stt["Z"], in0=stt["Z"], scalar=stt["dc"], in1=psz[p0:p1, 0:dv], op0=ALU.mult, op1=ALU.add),
                    reads=[stt["tZ"], stt["t_dc"], tpz], writes=[stt["tZ"]])

        def gdn_layer(l):
            j = l // 2
            o = [0]

            def A(n):
                r = arena[:, o[0]:o[0] + n]; o[0] += n
                assert o[0] <= ARENA, o[0]
                return r
            hn = A(8704).bitcast(BF16).rearrange("p (c n) -> p c n", c=8); t_hn = Trk()
            qt = A(T); qt2 = A(T); t_qt = Trk(); t_qt2 = Trk()
            sel = A(1024).rearrange("p (h m) -> p h m", h=8); t_sel = Trk()
            wq = A(2048).bitcast(BF16).rearrange("p (c g n) -> p c g n", c=8, g=4); t_wq = Trk()
            wo = A(512).bitcast(BF16); t_wo = Trk()
            abw = A(64).bitcast(BF16).rearrange("p (c n) -> p c n", c=8); t_abw = Trk()
            pre = A(515); t_pre = Trk()
            post = A(1536).rearrange("p (g n) -> p g n", g=3); t_post = [Trk(), Trk(), Trk()]
            zs = A(512); t_zs = Trk()
            sqb = A(256).bitcast(BF16); t_sqb = Trk()
            rn = A(512); t_rn = Trk()
            ns = A(512); t_ns = Trk()
            rsq = A(512); t_rsq = Trk()
            og = A(256).bitcast(BF16); t_og = Trk()
            pres = A(112).rearrange("p (s k) -> p s k", s=16); t_pres = Trk()
            dcs = A(32); t_dcs = Trk()
            hist = A(144).rearrange("p (g k) -> p g k", g=3); t_hist = Trk()
            hrow = A(384)[0:48, :]; t_hrow = Trk()
            stg = A(153).rearrange("p (g k) -> p g k", g=3); t_stg = Trk()
            srow = A(384)[0:51, :]; t_srow = Trk()
            ones1 = A(64).bitcast(BF16); t_ones1 = Trk()
            scr_ = A(2048)
            sq_scr = scr_.bitcast(BF16).rearrange("p (c n) -> p c n", c=8)
            Vt = scr_[0:64, 0:1024].rearrange("p (b n) -> p b n", b=8); t_Vt = Trk()
            Kh = scr_[0:64, 1024:2048].rearrange("p (b n) -> p b n", b=8); t_Kh = t_Vt
            nPh = A(1024)[0:64, :].rearrange("p (b n) -> p b n", b=8); t_nPh = Trk()
            v64 = lambda: A(512)[0:64, :].rearrange("p (b n) -> p b n", b=8)
            M1 = v64(); M2 = v64(); t_M = [Trk(), Trk()]
            orr = [0]

            def v64r():
                r_ = arenaR[0:64, orr[0]:orr[0] + 512].rearrange("p (b n) -> p b n", b=8); orr[0] += 512
                assert orr[0] <= ARENAR
                return r_
            Ank = v64(); B0 = v64r(); Ark = v64(); nArp = v64()
            t_Ank, t_B0, t_Ark, t_nArp = Trk(), Trk(), Trk(), Trk()
            Ca = v64r(); Cb = v64r(); Ba = v64r(); Bb = v64r()
            Pa_ = v64r(); Pb_ = v64r(); t_Pab = [Trk(), Trk()]
            t_C = [Trk(), Trk()]; t_B = [Trk(), Trk()]
            Xs = A(128)[0:64, :]; Es = A(128)[0:64, :]; XT = A(64); t_Xs, t_Es, t_XT = Trk(), Trk(), Trk()
            Z = A(128); t_Z = Trk()
            Zs = A(512).rearrange("p (s n) -> p s n", s=4); t_Zs = [Trk() for _ in range(4)]
            Km = A(128)[0:64, :]; nPm = A(128)[0:64, :]; t_Km, t_nPm = Trk(), Trk()
            tm1 = A(256)[0:64, :].rearrange("p (b q h) -> p b q h", b=8, q=4)
            tm2 = A(256)[0:64, :].rearrange("p (b q h) -> p b q h", b=8, q=4)
            t_tm = Trk()
            kdb = A(16)[0:64, :].rearrange("p (b q) -> p b q", b=8); t_kdb = Trk()

            S.op("pool", lambda h: h.memset(ones1, 1.0), writes=[t_ones1])
            S.op("pool", lambda h: h.memset(sel, 0.0), writes=[t_sel])
            for base in (0, 32):
                S.op("pool", lambda h, base=base: h.affine_select(
                    out=sel[base:base + 32], in_=sel[base:base + 32], pattern=[[-1, 8], [0, 128]],
                    compare_op=ALU.not_equal, fill=1.0, base=0, channel_multiplier=1),
                    reads=[t_sel], writes=[t_sel])
            S.op("pool", lambda h: h.memset(qt2[32:64, :], 1.0), writes=[t_qt2])
            S.op("pool", lambda h: h.memset(qt2[32:64, 0:S0].rearrange("p (c k) -> p c k", k=64)[:, :, 0:1], 0.0),
                 writes=[t_qt2])
            S.op("pool", lambda h: h.memset(qt2[32:64, S0:T].rearrange("p (c k) -> p c k", k=4)[:, :, 0:1], 0.0),
                 writes=[t_qt2])
            S.op("pool", lambda h: h.memset(qt[:], 0.0), writes=[t_qt])
            win = gdn_w_in[j].rearrange("(c p) n -> p c n", p=128)
            S.dma("pool", abw, win[:, :, 4096:4112], writes=[t_abw])
            rmsnorm(hn, 0, "norm_mix", l, 0, T, sq_scr, t_Vt, rn, t_rn, t_hn)
            S.op("act", lambda h: h.activation(out=dcs[64:72, 24:25], in_=pcol("gdn_A", j)[64:72], func=AF.Exp),
                 reads=[t_pt], writes=[t_dcs])
            S.op("dve", lambda h: h.tensor_scalar(out=dcs[64:72, 24:25], in0=dcs[64:72, 24:25], scalar1=-1.0,
                                                  scalar2=None, op0=ALU.mult), reads=[t_dcs], writes=[t_dcs])
            for (a, b) in TILES:
                n = b - a
                pst, pt_ = K.ps()
                for c in range(8):
                    S.op("pe", lambda h, c=c, a=a, b=b, n=n, pst=pst: h.matmul(
                        pst[64:72, 0:n], lhsT=abw[:, c, 0:8], rhs=hn[:, c, a:b], start=(c == 0), stop=(c == 7)),
                        reads=[t_abw, t_hn], writes=[pt_], inc=(c == 7))
                for c in range(8):
                    S.op("pe", lambda h, c=c, a=a, b=b, n=n, pst=pst: h.matmul(
                        pst[0:8, 0:n], lhsT=abw[:, c, 8:16], rhs=hn[:, c, a:b], start=(c == 0), stop=(c == 7)),
                        reads=[t_abw, t_hn], writes=[pt_], inc=(c == 7))
                S.op("act", lambda h, a=a, b=b, n=n, pst=pst: h.activation(
                    out=qt2[64:72, a:b], in_=pst[64:72, 0:n], func=AF.Exp, bias=pcol("gdn_dt", j)[64:72], scale=1.0),
                    reads=[pt_, t_pt], writes=[t_qt2])
                S.op("act", lambda h, a=a, b=b: h.activation(
                    out=qt2[64:72, a:b], in_=qt2[64:72, a:b], func=AF.Ln, bias=cst[64:72, 2:3], scale=1.0),
                    reads=[t_qt2, t_cst], writes=[t_qt2])
                S.op("dve", lambda h, a=a, b=b: h.tensor_scalar(
                    out=qt2[64:72, a:b], in0=qt2[64:72, a:b], scalar1=dcs[64:72, 24:25], scalar2=None, op0=ALU.mult),
                    reads=[t_qt2, t_dcs], writes=[t_qt2])
                S.op("act", lambda h, a=a, b=b, n=n, pst=pst: h.activation(
                    out=qt[0:8, a:b], in_=pst[0:8, 0:n], func=AF.Sigmoid), reads=[pt_], writes=[t_qt])
            S.dma("sp", qt[96:104, :], qt[0:8, :], reads=[t_qt], writes=[t_qt])
            S.dma("sp", qt[32:40, :], qt2[64:72, :], reads=[t_qt2], writes=[t_qt])
            S.dma("sp", qt[0:8, :], qt2[64:72, :], reads=[t_qt2, t_qt], writes=[t_qt])
            S.op("dve", lambda h: h.tensor_tensor_scan(out=qt[32:40, :], data0=qt2[32:40, :], data1=qt[32:40, :],
                                                       initial=0.0, op0=ALU.mult, op1=ALU.add),
                 reads=[t_qt, t_qt2], writes=[t_qt])
            S.dma("sp", qt2[0:8, :], qt[32:40, :], reads=[t_qt], writes=[t_qt2])
            S.op("dve", lambda h: h.tensor_tensor(out=qt[0:8, :], in0=qt2[0:8, :], in1=qt[0:8, :], op=ALU.subtract),
                 reads=[t_qt, t_qt2], writes=[t_qt])
            S.op("dve", lambda h: h.tensor_tensor(
                out=qt2[32:40, 0:S0].rearrange("p (c k) -> p c k", k=64),
                in0=qt[32:40, 0:S0].rearrange("p (c k) -> p c k", k=64)[:, :, 63:64].to_broadcast([8, 33, 64]),
                in1=qt[32:40, 0:S0].rearrange("p (c k) -> p c k", k=64), op=ALU.subtract),
                reads=[t_qt], writes=[t_qt2])
            S.op("dve", lambda h: h.tensor_tensor(
                out=qt2[32:40, S0:T].rearrange("p (c k) -> p c k", k=4),
                in0=qt[32:40, S0:T].rearrange("p (c k) -> p c k", k=4)[:, :, 3:4].to_broadcast([8, 16, 4]),
                in1=qt[32:40, S0:T].rearrange("p (c k) -> p c k", k=4), op=ALU.subtract),
                reads=[t_qt], writes=[t_qt2])
            S.op("act", lambda h: h.activation(out=qt2[32:40, :], in_=qt2[32:40, :], func=AF.Exp),
                 reads=[t_qt2], writes=[t_qt2])
            S.op("act", lambda h: h.activation(out=qt2[64:72, :], in_=qt2[64:72, :], func=AF.Exp),
                 reads=[t_qt2], writes=[t_qt2])

            cwq = lambda g, h_, k_: pcol("gdn_cw", (j * 4 + k_) * 24 + g * 8 + h_)
            wout = gdn_w_out[j]
            ybi = 0
            def do_head(hd):
                nonlocal ybi
                for g in range(4):
                    S.dma("pool", wq[:, :, g, :], win[:, :, g * 1024 + hd * 128:g * 1024 + (hd + 1) * 128],
                          writes=[t_wq])
                S.dma("pool", wo, wout[hd * 128:(hd + 1) * 128, :], writes=[t_wo])
                for g in range(3):
                    S.dma("sp", hrow[:, g * 128:(g + 1) * 128],
                          gdn_conv_in[j].rearrange("s k n -> (s k) n")[:, g * 1024 + hd * 128:g * 1024 + (hd + 1) * 128],
                          writes=[t_hrow])
                pst, pt_ = K.ps()
                for g in range(3):
                    S.op("pe", lambda h, g=g, pst=pst: h.transpose(
                        out=pst[:, g * 48:(g + 1) * 48], in_=hrow[:, g * 128:(g + 1) * 128], identity=ident[0:48, 0:48]),
                        reads=[t_hrow, t_id], writes=[pt_], inc=(g == 2))
                S.op("dve", lambda h, pst=pst: h.tensor_copy(out=hist, in_=pst[:, 0:144].rearrange("p (g k) -> p g k", g=3)),
                     reads=[pt_], writes=[t_hist])
                S.op("dve", lambda h: h.memset(Z, 0.0), writes=[t_Z])
                def do_tile(ti, a, b):
                    nonlocal ybi
                    n = b - a
                    nb = n // 64
                    last = (ti == len(TILES) - 1)
                    for g in range(4):
                        psp, tpp = K.ps()
                        for c in range(8):
                            S.op("pe", lambda h, c=c, g=g, a=a, b=b, n=n, psp=psp: h.matmul(
                                psp[:, 0:n], lhsT=wq[:, c, g, :], rhs=hn[:, c, a:b], start=(c == 0), stop=(c == 7)),
                                reads=[t_wq, t_hn], writes=[tpp], inc=(c == 7))
                        if g == 3:
                            pass
                            S.op("act", lambda h, n=n, psp=psp: h.activation(out=zs[:, 0:n], in_=psp[:, 0:n], func=AF.Silu),
                                 reads=[tpp], writes=[t_zs])
                            continue
                        if ti == 0:
                            S.op("dve", lambda h: h.memset(pre[:, 0:3], 0.0), writes=[t_pre])
                        else:
                            S.op("dve", lambda h, g=g: h.tensor_copy(out=pre[:, 0:3], in_=stg[:, g, 48:51]),
                                 reads=[t_stg], writes=[t_pre])
                        S.op("act", lambda h, n=n, psp=psp: h.copy(out=pre[:, 3:3 + n], in_=psp[:, 0:n]),
                             reads=[tpp], writes=[t_pre])
                        npz = n if not last else S0 - a
                        S.op("pool", lambda h, g=g, npz=npz: h.tensor_copy(out=stg[:, g, 48:51], in_=pre[:, npz:npz + 3]),
                             reads=[t_pre], writes=[t_stg])
                        tp = t_post[g]
                        S.op("dve", lambda h, g=g, npz=npz: h.tensor_scalar(
                            out=post[:, g, 0:npz], in0=pre[:, 3:3 + npz], scalar1=cwq(g, hd, 3), scalar2=None,
                            op0=ALU.mult), reads=[t_pre, t_pt], writes=[tp])
                        for k_ in range(3):
                            S.op("dve", lambda h, g=g, npz=npz, k_=k_: h.scalar_tensor_tensor(
                                out=post[:, g, 0:npz], in0=pre[:, k_:k_ + npz], scalar=cwq(g, hd, k_),
                                in1=post[:, g, 0:npz], op0=ALU.mult, op1=ALU.add), reads=[t_pre, t_pt, tp], writes=[tp])
                        if last:
                            so = S0 - a
                            S.op("dve", lambda h, g=g: h.tensor_copy(
                                out=pres[:, :, 0:3], in_=hist[:, g, :].rearrange("p (s k) -> p s k", s=16)),
                                reads=[t_hist], writes=[t_pres])
                            S.op("dve", lambda h, so=so: h.tensor_copy(
                                out=pres[:, :, 3:7], in_=pre[:, 3 + so:3 + so + 64].rearrange("p (s k) -> p s k", s=16)),
                                reads=[t_pre], writes=[t_pres])
                            ps_ = post[:, g, so:so + 64].rearrange("p (s k) -> p s k", s=16)
                            S.op("dve", lambda h, g=g, ps_=ps_: h.tensor_scalar(
                                out=ps_, in0=pres[:, :, 3:7], scalar1=cwq(g, hd, 3), scalar2=None, op0=ALU.mult),
                                reads=[t_pres, t_pt], writes=[tp])
                            for k_ in range(3):
                                S.op("dve", lambda h, g=g, ps_=ps_, k_=k_: h.scalar_tensor_tensor(
                                    out=ps_, in0=pres[:, :, k_:k_ + 4], scalar=cwq(g, hd, k_), in1=ps_,
                                    op0=ALU.mult, op1=ALU.add), reads=[t_pres, t_pt, tp], writes=[tp])
                            S.op("pool", lambda h, g=g: h.tensor_copy(
                                out=stg[:, g, 0:48].rearrange("p (s k) -> p s k", s=16), in_=pres[:, :, 4:7]),
                                reads=[t_pres], writes=[t_stg])
                        S.op("act", lambda h, g=g, n=n: h.activation(out=post[:, g, 0:n], in_=post[:, g, 0:n], func=AF.Silu),
                             reads=[tp], writes=[tp])
                    for g in range(2):
                        tp = t_post[g]
                        S.op("act", lambda h, g=g, n=n: h.activation(out=sqb[:, 0:n], in_=post[:, g, 0:n], func=AF.Square),
                             reads=[tp], writes=[t_sqb])
                        pss, tps = K.ps()
                        S.op("pe", lambda h, n=n, pss=pss: h.matmul(pss[:, 0:n], lhsT=ones1, rhs=sqb[:, 0:n],
                                                                    start=True, stop=True),
                             reads=[t_sqb, t_ones1], writes=[tps])
                        S.op("act", lambda h, n=n, pss=pss: h.activation(out=rn[:, 0:n], in_=pss[:, 0:n], func=AF.Sqrt,
                                                                         bias=cst[:, 0:1], scale=1.0),
                             reads=[tps, t_cst], writes=[t_rn])
                        S.op("dve", lambda h, n=n: h.reciprocal(out=rn[:, 0:n], in_=rn[:, 0:n]), reads=[t_rn], writes=[t_rn])
                        if g == 0:
                            S.op("dve", lambda h, n=n: h.scalar_tensor_tensor(
                                out=post[:, 0, 0:n], in0=post[:, 0, 0:n], scalar=float(128 ** -0.5), in1=rn[:, 0:n],
                                op0=ALU.mult, op1=ALU.mult), reads=[tp, t_rn], writes=[tp])
                        else:
                            S.op("dve", lambda h, n=n: h.tensor_tensor(out=post[:, 1, 0:n], in0=post[:, 1, 0:n],
                                                                       in1=rn[:, 0:n], op=ALU.mult),
                                 reads=[tp, t_rn], writes=[tp])
                    qn, kn, vv = post[:, 0, :], post[:, 1, :], post[:, 2, :]
                    for (src, dst, tsrc) in ((qt, tm1, t_qt), (qt2, tm2, t_qt2)):
                        for b0_ in range(0, nb, 4):
                            nn = min(4, nb - b0_)
                            pst, pt_ = K.ps()
                            for bi in range(nn):
                                S.op("pe", lambda h, bi=bi, b0_=b0_, src=src, pst=pst, a=a: h.transpose(
                                    out=pst[0:64, bi * 128:(bi + 1) * 128],
                                    in_=src[:, a + (b0_ + bi) * 64:a + (b0_ + bi + 1) * 64], identity=ident[:]),
                                    reads=[tsrc, t_id], writes=[pt_], inc=(bi == nn - 1))
                            S.op("dve", lambda h, pst=pst, dst=dst, b0_=b0_, nn=nn: h.tensor_copy(
                                out=dst[:, b0_:b0_ + nn, :, :],
                                in_=pst[0:64, 0:nn * 128].rearrange("p (b q r) -> p b q r", b=nn, q=4)[:, :, :, 0:8]),
                                reads=[pt_], writes=[t_tm])
                    S.op("dve", lambda h, nb=nb: h.tensor_tensor(out=kdb[:, 0:nb, 0], in0=tm2[:, 0:nb, 1, hd],
                                                                 in1=tm1[:, 0:nb, 3, hd], op=ALU.mult),
                         reads=[t_tm], writes=[t_kdb])
                    S.op("dve", lambda h, nb=nb: h.scalar_tensor_tensor(
                        out=kdb[:, 0:nb, 1], in0=kdb[:, 0:nb, 0], scalar=-1.0, in1=tm2[:, 0:nb, 2, hd],
                        op0=ALU.mult, op1=ALU.mult), reads=[t_tm, t_kdb], writes=[t_kdb])
                    psg1, tg1 = K.ps()
                    S.op("pe", lambda h, a=a, b=b, n=n, psg1=psg1: h.matmul(
                        psg1[:, 0:n], lhsT=sel[0:8, hd, :], rhs=qt[0:8, a:b], start=True, stop=True),
                        reads=[t_sel, t_qt], writes=[tg1])
                    psg2, tg2 = K.ps()
                    S.op("pe", lambda h, a=a, b=b, n=n, psg2=psg2: h.matmul(
                        psg2[:, 0:n], lhsT=sel[32:40, hd, :], rhs=qt[32:40, a:b], start=True, stop=True),
                        reads=[t_sel, t_qt], writes=[tg2])
                    for (Mx, tMx, psg, tg, tri_p, tri_sm) in ((M1, t_M[0], psg1, tg1, tri_s, trib_s),
                                                              (M2, t_M[1], psg2, tg2, tri_i, trib_i)):
                        S.op("dve", lambda h, Mx=Mx, psg=psg, nb=nb: h.tensor_tensor(
                            out=Mx[:, 0:nb, :], in0=psg[0:64, 0:nb * 64].rearrange("p (b n) -> p b n", n=64),
                            in1=tm1[:, 0:nb, 1, hd:hd + 1].to_broadcast([64, nb, 64]), op=ALU.subtract),
                            reads=[tg, t_tm], writes=[tMx])
                        npb = nb - 1 if last else nb
                        if npb > 0:
                            S.op("dve", lambda h, Mx=Mx, npb=npb, tri_p=tri_p: h.tensor_tensor(
                                out=Mx[:, 0:npb, :], in0=Mx[:, 0:npb, :],
                                in1=tri_p.unsqueeze(1).to_broadcast([64, npb, 64]), op=ALU.mult),
                                reads=[tMx, t_tri], writes=[tMx])
                        if last:
                            S.op("dve", lambda h, Mx=Mx, nb=nb, tri_sm=tri_sm: h.tensor_tensor(
                                out=Mx[:, nb - 1, :], in0=Mx[:, nb - 1, :], in1=tri_sm, op=ALU.mult),
                                reads=[tMx, t_tri], writes=[tMx])
                        S.op("act", lambda h, Mx=Mx, nb=nb: h.activation(out=Mx[:, 0:nb, :], in_=Mx[:, 0:nb, :], func=AF.Exp),
                             reads=[tMx], writes=[tMx])
                        if npb > 0:
                            S.op("dve", lambda h, Mx=Mx, npb=npb, tri_p=tri_p: h.tensor_tensor(
                                out=Mx[:, 0:npb, :], in0=Mx[:, 0:npb, :],
                                in1=tri_p.unsqueeze(1).to_broadcast([64, npb, 64]), op=ALU.mult),
                                reads=[tMx, t_tri], writes=[tMx])
                        if last:
                            S.op("dve", lambda h, Mx=Mx, nb=nb, tri_sm=tri_sm: h.tensor_tensor(
                                out=Mx[:, nb - 1, :], in0=Mx[:, nb - 1, :], in1=tri_sm, op=ALU.mult),
                                reads=[tMx, t_tri], writes=[tMx])
                        S.op("dve", lambda h, Mx=Mx, nb=nb: h.tensor_tensor(
                            out=Mx[:, 0:nb, :], in0=Mx[:, 0:nb, :],
                            in1=tm1[:, 0:nb, 3, hd:hd + 1].to_broadcast([64, nb, 64]), op=ALU.mult),
                            reads=[tMx, t_tm], writes=[tMx])
                    S.op("act", lambda h, n=n, psg1=psg1: h.activation(out=ns[:, 0:n], in_=psg1[:, 0:n], func=AF.Exp),
                         reads=[tg1], writes=[t_ns])
                    S.op("dve", lambda h, n=n: h.tensor_tensor(out=ns[:, 0:n], in0=ns[:, 0:n], in1=kn[:, 0:n], op=ALU.mult),
                         reads=[t_ns, t_post[1]], writes=[t_ns])
                    S.op("act", lambda h, n=n, psg2=psg2: h.activation(out=rsq[:, 0:n], in_=psg2[:, 0:n], func=AF.Exp),
                         reads=[tg2], writes=[t_rsq])
                    npb = nb - 1 if last else nb
                    S.op("pool", lambda h, npb=npb: h.tensor_copy(
                        out=dcs[:, 0:npb], in_=rsq[:, 0:npb * 64].rearrange("p (b k) -> p b k", k=64)[:, :, 63]),
                        reads=[t_rsq], writes=[t_dcs])
                    if last:
                        so = S0 - a
                        S.op("pool", lambda h, so=so: h.tensor_copy(
                            out=dcs[:, 8:24], in_=rsq[:, so:so + 64].rearrange("p (s k) -> p s k", k=4)[:, :, 3]),
                            reads=[t_rsq], writes=[t_dcs])
                    S.op("dve", lambda h, n=n: h.tensor_tensor(out=rsq[:, 0:n], in0=rsq[:, 0:n], in1=qn[:, 0:n], op=ALU.mult),
                         reads=[t_rsq, t_post[0], t_dcs], writes=[t_rsq])
                    for (srcg, dst, tdst, sc) in ((2, Vt, t_Vt, None), (1, Kh, t_Kh, 0)):
                        for b0_ in range(0, nb, 4):
                            nn = min(4, nb - b0_)
                            pst, pt_ = K.ps()
                            for bi in range(nn):
                                S.op("pe", lambda h, bi=bi, b0_=b0_, srcg=srcg, pst=pst: h.transpose(
                                    out=pst[0:64, bi * 128:(bi + 1) * 128],
                                    in_=post[:, srcg, (b0_ + bi) * 64:(b0_ + bi + 1) * 64], identity=ident[:]),
                                    reads=[t_post[srcg], t_id], writes=[pt_], inc=(bi == nn - 1))
                            pv = pst[0:64, 0:nn * 128].rearrange("p (b n) -> p b n", n=128)
                            if sc is None:
                                S.op("act", lambda h, pv=pv, b0_=b0_, nn=nn: h.copy(out=Vt[:, b0_:b0_ + nn, :], in_=pv),
                                     reads=[pt_], writes=[t_Vt])
                            else:
                                S.op("dve", lambda h, pv=pv, b0_=b0_, nn=nn: h.tensor_tensor(
                                    out=Kh[:, b0_:b0_ + nn, :], in0=pv,
                                    in1=kdb[:, b0_:b0_ + nn, 0:1].to_broadcast([64, nn, 128]), op=ALU.mult),
                                    reads=[pt_, t_kdb], writes=[t_Kh])
                                S.op("dve", lambda h, pv=pv, b0_=b0_, nn=nn: h.tensor_tensor(
                                    out=nPh[:, b0_:b0_ + nn, :], in0=pv,
                                    in1=kdb[:, b0_:b0_ + nn, 1:2].to_broadcast([64, nn, 128]), op=ALU.mult),
                                    reads=[pt_, t_kdb], writes=[t_nPh])
                    pkk, tkk = K.ps()
                    pqk, tqk = K.ps()
                    for bi in range(nb):
                        S.op("pe", lambda h, bi=bi, pkk=pkk: h.matmul(
                            pkk[0:64, bi * 64:(bi + 1) * 64], lhsT=kn[:, bi * 64:(bi + 1) * 64],
                            rhs=kn[:, bi * 64:(bi + 1) * 64], start=True, stop=True),
                            reads=[t_post[1]], writes=[tkk], inc=(bi == nb - 1))
                    for bi in range(nb):
                        S.op("pe", lambda h, bi=bi, pqk=pqk: h.matmul(
                            pqk[0:64, bi * 64:(bi + 1) * 64], lhsT=kn[:, bi * 64:(bi + 1) * 64],
                            rhs=qn[:, bi * 64:(bi + 1) * 64], start=True, stop=True),
                            reads=[t_post[1], t_post[0]], writes=[tqk], inc=(bi == nb - 1))
                    v3 = lambda p_, nb=nb: p_[0:64, 0:nb * 64].rearrange("p (b n) -> p b n", n=64)
                    al = tm2[:, 0:nb, 2, hd:hd + 1].to_broadcast([64, nb, 64])
                    S.op("dve", lambda h, pkk=pkk, nb=nb: h.tensor_tensor(out=Ank[:, 0:nb, :], in0=v3(pkk), in1=M1[:, 0:nb, :],
                                                                          op=ALU.mult),
                         reads=[tkk, t_M[0]], writes=[t_Ank])
                    S.op("dve", lambda h, nb=nb, al=al: h.scalar_tensor_tensor(
                        out=R(B0[:, 0:nb, :]), in0=Ank[:, 0:nb, :], scalar=-1.0, in1=al, op0=ALU.mult, op1=ALU.mult),
                        reads=[t_Ank, t_tm], writes=[t_B0])
                    S.op("dve", lambda h, pqk=pqk, nb=nb: h.tensor_tensor(out=Ark[:, 0:nb, :], in0=v3(pqk), in1=M2[:, 0:nb, :],
                                                                          op=ALU.mult),
                         reads=[tqk, t_M[1]], writes=[t_Ark])
                    S.op("dve", lambda h, nb=nb, al=al: h.scalar_tensor_tensor(
                        out=nArp[:, 0:nb, :], in0=Ark[:, 0:nb, :], scalar=-1.0, in1=al, op0=ALU.mult, op1=ALU.mult),
                        reads=[t_Ark, t_tm], writes=[t_nArp])
                    TT, t_TT = solve_blocks(nb, B0, Ca, Cb, Ba, Bb, Pa_, Pb_, t_B0, t_C, t_B, t_Pab)
                    ybank = Y_BANKS[ybi % 2]; ybi += 1
                    ops = dict(ns=(ns, t_ns), rs=(rsq, t_rsq), V=(Vt, t_Vt), Kh=(Kh, t_Kh), nPh=(nPh, t_nPh),
                               Ank=(Ank, t_Ank), Ark=(Ark, t_Ark), nArp=(nArp, t_nArp), TT=(TT, t_TT),
                               Xs=(Xs, t_Xs), Es=(Es, t_Es), XT=(XT, t_XT), Km=(Km, t_Km), nPm=(nPm, t_nPm),
                               t_mask=t_tri)
                    for bi in range(npb):
                        seq_block(bi, 128, (0, 128), (bi * 64, bi * 64 + 64),
                                  [dict(Z=Z, tZ=t_Z, lo=0, hi=64, dc=dcs[:, bi:bi + 1], t_dc=t_dcs)],
                                  ops, ybank, (bi * 64, bi * 64 + 64), (0, 128))
                    if last:
                        S.dma("sp", gS_p[j, hd], Z, reads=[t_Z])
                        so = S0 - a
                        bi = nb - 1
                        for sg in range(4):
                            S.dma("sp", Zs, gdn_S_in[j, sg * 4:sg * 4 + 4, hd].rearrange("s k v -> k s v"),
                                  writes=t_Zs)
                            sts = [dict(Z=Zs[:, s_, :], tZ=t_Zs[s_], lo=(sg * 4 + s_) * 4, hi=(sg * 4 + s_) * 4 + 4,
                                        dc=dcs[:, 8 + sg * 4 + s_:9 + sg * 4 + s_], t_dc=t_dcs,
                                        mask=rmk[:, sg * 4 + s_:sg * 4 + s_ + 1]) for s_ in range(4)]
                            seq_block(bi, 128, (0, 128), (so, so + 64), sts, ops, ybank, (so, so + 64), (0, 128),
                                      tcols=(sg * 16, sg * 16 + 16))
                            S.dma("sp", gS_s[j, sg * 4:sg * 4 + 4, hd].rearrange("s k v -> k s v"), Zs, reads=t_Zs)
                    yb, tyb = ybank
                    S.op("act", lambda h, n=n, yb=yb: h.activation(out=sqb[:, 0:n], in_=yb[:, 0:n], func=AF.Square),
                         reads=[tyb], writes=[t_sqb])
                    pss, tps = K.ps()
                    S.op("pe", lambda h, n=n, pss=pss: h.matmul(pss[:, 0:n], lhsT=ones1, rhs=sqb[:, 0:n], start=True, stop=True),
                         reads=[t_sqb, t_ones1], writes=[tps])
                    S.op("act", lambda h, n=n, pss=pss: h.activation(out=rn[:, 0:n], in_=pss[:, 0:n], func=AF.Sqrt,
                                                                     bias=cst[:, 0:1], scale=1.0 / 128.0),
                         reads=[tps, t_cst], writes=[t_rn])
                    S.op("dve", lambda h, n=n: h.reciprocal(out=rn[:, 0:n], in_=rn[:, 0:n]), reads=[t_rn], writes=[t_rn])
                    S.op("dve", lambda h, n=n, yb=yb: h.scalar_tensor_tensor(
                        out=ns[:, 0:n], in0=yb[:, 0:n], scalar=pcol("gdn_nw", j), in1=rn[:, 0:n], op0=ALU.mult, op1=ALU.mult),
                        reads=[tyb, t_rn, t_pt], writes=[t_ns])
                    S.op("dve", lambda h, n=n: h.tensor_tensor(out=og[:, 0:n], in0=ns[:, 0:n], in1=zs[:, 0:n], op=ALU.mult),
                         reads=[t_ns, t_zs], writes=[t_og])
                    a2 = max(a, PAD)
                    tl = tile_of(a2, b)
                    for oc in range(8):
                        pso, to = K.ps()
                        S.op("pe", lambda h, oc=oc, n=n, pso=pso: h.matmul(
                            pso[:, 0:n], lhsT=wo[:, oc * 128:(oc + 1) * 128], rhs=og[:, 0:n], start=True, stop=True),
                            reads=[t_wo, t_og], writes=[to])
                        S.op("dve", lambda h, oc=oc, a2=a2, a=a, b=b, pso=pso: h.tensor_tensor(
                            out=xT[:, oc, a2:b], in0=xT[:, oc, a2:b], in1=pso[:, a2 - a:b - a], op=ALU.add),
                            reads=[to] + tl, writes=tl)
                for ti_, (a_, b_) in enumerate(TILES):
                    do_tile(ti_, a_, b_)
                pst, pt_ = K.ps()
                for g in range(3):
                    S.op("pe", lambda h, g=g, pst=pst: h.transpose(
                        out=pst[0:51, g * 128:(g + 1) * 128], in_=stg[:, g, :], identity=ident[:]),
                        reads=[t_stg, t_id], writes=[pt_], inc=(g == 2))
                S.op("act", lambda h, pst=pst: h.copy(out=srow, in_=pst[0:51, 0:384]), reads=[pt_], writes=[t_srow])
                for g in range(3):
                    cs = slice(g * 1024 + hd * 128, g * 1024 + (hd + 1) * 128)
                    S.dma("sp", gC_s[j].rearrange("s k n -> (s k) n")[:, cs], srow[0:48, g * 128:(g + 1) * 128],
                          reads=[t_srow])
                    S.dma("sp", gC_p[j][:, cs], srow[48:51, g * 128:(g + 1) * 128], reads=[t_srow])
            for hd_ in range(cfg.get('gdn_heads', 8)):
                do_head(hd_)
            S.barrier()

        def seq_block_pair(ci, c0, states, ops, ybank, ycol0, tcols=(0, 64)):
            (ns, t_ns), (rs_, t_rs_) = ops["ns"], ops["rs"]
            (V, t_V), (Kh, t_Kh), (nPh, t_nPh) = ops["V"], ops["Kh"], ops["nPh"]
            (Ank, t_Ank), (Ark, t_Ark), (nArp, t_nArp), (TT, t_TT) = ops["Ank"], ops["Ark"], ops["nArp"], ops["TT"]
            (Xs, t_Xs), (Es, t_Es) = ops["Xs"], ops["Es"]
            H = lambda hs: slice(hs * 64, hs * 64 + 64)
            single = len(states) == 1
            psx, tpx = K.ps()
            if single:
                stt = states[0]
                S.op("pe", lambda h: h.matmul(psx[0:64, 0:128], lhsT=ns[:, c0:c0 + 64], rhs=stt["Z"], start=True, stop=False),
                     reads=[t_ns] + stt["tZ"], writes=[tpx], inc=False)
                for hs in range(2):
                    S.op("pe", lambda h, hs=hs: h.matmul(psx[0:64, H(hs)], lhsT=Ank[:, 2 * ci + hs, :], rhs=V[:, ci, H(hs)],
                                                         start=False, stop=(hs == 1)), reads=[t_Ank, t_V], writes=[tpx], inc=(hs == 1))
                S.op("act", lambda h: h.copy(out=Xs, in_=psx[0:64, 0:128]), reads=[tpx], writes=[t_Xs])
            else:
                for hs in range(2):
                    S.op("pe", lambda h, hs=hs: h.matmul(psx[H(hs), 0:64], lhsT=V[:, ci, H(hs)], rhs=Ank[:, 2 * ci + hs, :],
                                                         start=True, stop=False), reads=[t_Ank, t_V], writes=[tpx], inc=False)
                for si, stt in enumerate(states):
                    S.op("pe", lambda h, stt=stt, si=si: h.matmul(
                        psx[:, stt["lo"]:stt["hi"]], lhsT=stt["Z"], rhs=ns[:, c0 + stt["lo"]:c0 + stt["hi"]],
                        start=False, stop=(si == len(states) - 1)), reads=[t_ns] + stt["tZ"], writes=[tpx],
                        inc=(si == len(states) - 1))
                XT, t_XT = ops["XT"]
                S.op("act", lambda h: h.copy(out=XT, in_=psx[:, 0:64]), reads=[tpx], writes=[t_XT])
                psx2, tpx2 = K.ps()
                S.op("pe", lambda h: h.transpose(out=psx2[0:64, 0:128], in_=XT, identity=ident[:]),
                     reads=[t_XT, t_id], writes=[tpx2])
                S.op("act", lambda h: h.copy(out=Xs, in_=psx2[0:64, 0:128]), reads=[tpx2], writes=[t_Xs])
            pse, tpe = K.ps()
            for hs in range(2):
                S.op("pe", lambda h, hs=hs: h.matmul(pse[0:64, H(hs)], lhsT=TT[:, 2 * ci + hs, :], rhs=Xs[:, H(hs)],
                                                     start=True, stop=True), reads=[t_TT, t_Xs], writes=[tpe], inc=(hs == 1))
            S.op("dve", lambda h: h.tensor_copy(out=Es, in_=pse[0:64, 0:128]), reads=[tpe], writes=[t_Es])
            yb, tyb = ybank
            tl0, tl1 = tcols
            for hs in range(2):
                S.op("pe", lambda h, hs=hs: h.matmul(yb[H(hs), ycol0 + tl0:ycol0 + tl1], lhsT=V[:, ci, H(hs)],
                                                     rhs=Ark[:, 2 * ci + hs, tl0:tl1], start=True, stop=False),
                     reads=[t_V, t_Ark], writes=[tyb], inc=False)
            for stt in states:
                lo, hi = max(stt["lo"], tl0), min(stt["hi"], tl1)
                S.op("pe", lambda h, stt=stt, lo=lo, hi=hi: h.matmul(
                    yb[:, ycol0 + lo:ycol0 + hi], lhsT=stt["Z"], rhs=rs_[:, c0 + lo:c0 + hi], start=False, stop=False),
                    reads=[t_rs_] + stt["tZ"], writes=[tyb], inc=False)
            for hs in range(2):
                S.op("pe", lambda h, hs=hs: h.matmul(yb[H(hs), ycol0 + tl0:ycol0 + tl1], lhsT=Es[:, H(hs)],
                                                     rhs=nArp[:, 2 * ci + hs, tl0:tl1], start=False, stop=(hs == 1)),
                     reads=[t_Es, t_nArp], writes=[tyb], inc=(hs == 1))
            for stt in states:
                psz, tpz = K.ps()
                if stt.get("mask") is not None:
                    Km, t_Km = ops["Km"]; nPm, t_nPm = ops["nPm"]
                    S.op("pool", lambda h, stt=stt: h.tensor_scalar(out=Km, in0=Kh[:, ci, :], scalar1=stt["mask"], scalar2=None,
                                                                   op0=ALU.mult), reads=[t_Kh, ops["t_mask"]], writes=[t_Km])
                    S.op("pool", lambda h, stt=stt: h.tensor_scalar(out=nPm, in0=nPh[:, ci, :], scalar1=stt["mask"], scalar2=None,
                                                                   op0=ALU.mult), reads=[t_nPh, ops["t_mask"]], writes=[t_nPm])
                    lk, tlk, lp, tlp = Km, t_Km, nPm, t_nPm
                else:
                    lk, tlk, lp, tlp = Kh[:, ci, :], t_Kh, nPh[:, ci, :], t_nPh
                for hs in range(2):
                    S.op("pe", lambda h, psz=psz, lk=lk, hs=hs: h.matmul(psz[H(hs), H(hs)], lhsT=lk[:, H(hs)], rhs=V[:, ci, H(hs)],
                                                                         start=True, stop=False),
                         reads=[tlk, t_V], writes=[tpz], inc=False)
                    S.op("pe", lambda h, psz=psz, lp=lp, hs=hs: h.matmul(psz[H(hs), H(hs)], lhsT=lp[:, H(hs)], rhs=Es[:, H(hs)],
                                                                         start=False, stop=True),
                         reads=[tlp, t_Es], writes=[tpz], inc=(hs == 1))
                for hs in range(2):
                    S.op("dve", lambda h, stt=stt, psz=psz, hs=hs: h.scalar_tensor_tensor(
                        out=stt["Z"][H(hs), H(hs)], in0=stt["Z"][H(hs), H(hs)], scalar=stt["dc"][H(hs), :],
                        in1=psz[H(hs), H(hs)], op0=ALU.mult, op1=ALU.add),
                        reads=stt["tZ"] + [stt["t_dc"], tpz], writes=stt["tZ"])

        RT = [(a_, min(a_ + 256, T)) for a_ in range(0, T, 256)]

        def rwkv_layer(l):
            j = l // 2
            o = [0]

            def A(n):
                r = arena[:, o[0]:o[0] + n]; o[0] += n
                assert o[0] <= ARENA, o[0]
                return r
            TX = 2178
            hnx = A(8 * TX // 2).bitcast(BF16).rearrange("p (c n) -> p c n", c=8); t_hn = Trk()
            prv = A(256).bitcast(BF16).rearrange("p (c n) -> p c n", c=8); t_prv = Trk()
            L1 = A(1088).bitcast(BF16); L2 = A(1088).bitcast(BF16); L3 = A(1088).bitcast(BF16)
            t_L = [Trk(), Trk(), Trk()]
            rmq = A(256); rmq_l = A(128); t_rmq = Trk()
            blkb = A(64).bitcast(BF16); blkf = A(128); t_blk = Trk()
            omm = A(112); t_omm = Trk()
            Wab = A(3072).bitcast(BF16).rearrange("p (s c g n) -> p s c g n", s=2, c=8, g=3); t_Wab = Trk()
            wst = [A(1024).rearrange("p (c n) -> p c n", c=8) for _ in range(2)]; t_wst = [Trk(), Trk()]
            W2s = A(192).bitcast(BF16).rearrange("p (g n) -> p g n", g=3); t_W2s = Trk()
            wo = A(512).bitcast(BF16); t_wo = Trk()
            names = ["r", "k", "v", "lw", "G", "a", "g", "eG", "enG", "kd", "kk", "kf", "p", "rk", "ph", "t", "t2"]
            ball = A(256 * len(names))
            Bf = {nm: ball[:, i * 256:(i + 1) * 256] for i, nm in enumerate(names)}
            tb = {nm: Trk() for nm in names}
            tb["t2"] = tb["t"]
            Sst = ball[0:64, 15 * 256:17 * 256].rearrange("p (s n) -> p s n", s=4); t_Sst = tb["t"]
            SstO = ball[:, 15 * 256:17 * 256].rearrange("p (s n) -> p s n", s=4)
            rn_ = ball[:, 0:512]
            yg = A(128).bitcast(BF16); t_yg = Trk()
            sqb = A(128).bitcast(BF16); t_sqb = Trk()
            scr_ = A(2304)
            sq_scr = scr_[:, 0:2048].bitcast(BF16).rearrange("p (c n) -> p c n", c=8)
            W1s = scr_[:, 0:2048].bitcast(BF16).rearrange("p (s c n) -> p s c n", s=2, c=8)
            t_scr = Trk()
            shr = scr_[0:17, 0:1024]
            xs17 = scr_[:, 1024:1024 + 136].rearrange("p (c n) -> p c n", c=8)
            sq17 = scr_[:, 1200:1200 + 68].bitcast(BF16).rearrange("p (c n) -> p c n", c=8)
            Vt = scr_[0:64, 0:512].rearrange("p (b n) -> p b n", n=128); t_Vt = t_scr
            Kh = scr_[0:64, 512:1024].rearrange("p (b n) -> p b n", n=128); t_Kh = t_scr
            nPh = scr_[0:64, 1024:1536].rearrange("p (b n) -> p b n", n=128); t_nPh = t_scr
            v64 = lambda: A(512)[0:64, :].rearrange("p (b n) -> p b n", b=8)
            orr = [0]

            def v64r():
                r_ = arenaR[0:64, orr[0]:orr[0] + 512].rearrange("p (b n) -> p b n", b=8); orr[0] += 512
                assert orr[0] <= ARENAR
                return r_
            Pa = v64r(); Pb = v64r(); t_P = [Trk(), Trk()]
            Ank = v64(); B0 = v64r(); Ark = v64(); nArp = v64()
            t_Ank, t_B0, t_Ark, t_nArp = Trk(), Trk(), Trk(), Trk()
            Ca = v64r(); Cb = v64r(); Ba = v64r(); Bb = v64r()
            t_C = [Trk(), Trk()]; t_B = [Trk(), Trk()]
            Xs = A(128)[0:64, :]; Es = A(128)[0:64, :]; XT = A(64); t_Xs, t_Es, t_XT = Trk(), Trk(), Trk()
            Zp = A(128); t_Zp = [Trk()]
            Zs = A(512).rearrange("p (s n) -> p s n", s=4); t_Zs = [[Trk()] for _ in range(4)]
            Km = A(128)[0:64, :]; nPm = A(128)[0:64, :]; t_Km, t_nPm = Trk(), Trk()
            dcs = A(24); t_dcs = Trk()
            MIX = {"r": 0, "w": 1, "k": 2, "v": 3, "a": 4, "g": 5}
            mixc = lambda m, c: pcol("rw_mix", (j * 6 + MIX[m]) * 8 + c)
            ommc = lambda m, c: omm[:, MIX[m] * 8 + c:MIX[m] * 8 + c + 1]
            PV = lambda name, c: pcol(name, j * 8 + c)

            S.op("pool", lambda h: h.memset(blkb, 0.0), writes=[t_blk])
            S.op("pool", lambda h: h.memset(blkf, 0.0), writes=[t_blk])
            for hs in range(2):
                sl = slice(hs * 64, hs * 64 + 64)
                S.op("pool", lambda h, sl=sl: h.memset(blkb[sl, sl], 1.0), writes=[t_blk])
                S.op("pool", lambda h, sl=sl: h.memset(blkf[sl, sl], 1.0), writes=[t_blk])
            S.op("pool", lambda h: h.memset(rmq, 1.0), writes=[t_rmq])
            S.op("pool", lambda h: h.memset(rmq.rearrange("p (c k) -> p c k", k=64)[:, :, 0:1], 0.0), writes=[t_rmq])
            S.op("pool", lambda h: h.memset(rmq_l, 1.0), writes=[t_rmq])
            S.op("pool", lambda h: h.memset(rmq_l[:, 0:1], 0.0), writes=[t_rmq])
            S.op("pool", lambda h: h.memset(rmq_l[:, 64:128].rearrange("p (c k) -> p c k", k=4)[:, :, 0:1], 0.0),
                 writes=[t_rmq])
            S.op("pool", lambda h: h.memset(hnx[:, :, 0:2], 0.0), writes=[t_hn])
            mo = pt_off["rw_mix"] + j * 48
            S.op("dve", lambda h: h.tensor_scalar(out=omm[:, 0:48], in0=ptab[:, mo:mo + 48], scalar1=-1.0, scalar2=1.0,
                                                  op0=ALU.mult, op1=ALU.add), reads=[t_pt], writes=[t_omm])
            ko = pt_off["rw_ka"] + j * 8
            S.op("dve", lambda h: h.tensor_scalar(out=omm[:, 48:56], in0=ptab[:, ko:ko + 8], scalar1=-1.0, scalar2=1.0,
                                                  op0=ALU.mult, op1=ALU.add), reads=[t_pt], writes=[t_omm])
            if cfg.get('rw_stop') == 1:
                S.barrier(); return
            tlast = tile_of(S0 - 1, T)
            S.op("dve", lambda h: h.tensor_copy(out=xs17[:, :, 0:1], in_=xT[:, :, S0 - 1:S0]), reads=tlast, writes=[t_scr])
            S.op("dve", lambda h: h.tensor_copy(out=xs17[:, :, 1:17],
                                                in_=xT[:, :, S0:T].rearrange("p c (s k) -> p c s k", k=4)[:, :, :, 3]),
                 reads=tlast, writes=[t_scr])
            S.op("act", lambda h: h.activation(out=sq17, in_=xs17, func=AF.Square), reads=[t_scr], writes=[t_scr])
            pst, pt_ = K.ps()
            for c in range(8):
                S.op("pe", lambda h, c=c, pst=pst: h.matmul(pst[:, 0:17], lhsT=onesb[:], rhs=sq17[:, c, :],
                                                            start=(c == 0), stop=(c == 7)),
                     reads=[t_scr, t_ones], writes=[pt_], inc=(c == 7))
            S.op("act", lambda h, pst=pst: h.activation(out=rn_[:, 0:17], in_=pst[:, 0:17], func=AF.Sqrt, bias=cst[:, 0:1],
                                                        scale=1.0), reads=[pt_, t_cst], writes=[tb["r"]])
            S.op("dve", lambda h: h.reciprocal(out=rn_[:, 0:17], in_=rn_[:, 0:17]), reads=[tb["r"]], writes=[tb["r"]])
            for c in range(8):
                S.op("dve", lambda h, c=c: h.scalar_tensor_tensor(
                    out=xs17[:, c, :], in0=xs17[:, c, :], scalar=pcol("norm_mix", l * 8 + c), in1=rn_[:, 0:17],
                    op0=ALU.mult, op1=ALU.mult), reads=[t_scr, tb["r"], t_pt], writes=[t_scr])
            for half in range(2):
                pst, pt_ = K.ps()
                for cc in range(4):
                    S.op("pe", lambda h, cc=cc, half=half, pst=pst: h.transpose(
                        out=pst[0:17, cc * 128:(cc + 1) * 128], in_=xs17[:, half * 4 + cc, :], identity=ident[:]),
                        reads=[t_scr, t_id], writes=[pt_], inc=(cc == 3))
                S.op("act", lambda h, half=half, pst=pst: h.copy(out=shr[:, half * 512:(half + 1) * 512], in_=pst[0:17, :]),
                     reads=[pt_], writes=[t_scr])
            S.dma("sp", rSh_p[j:j + 1, :], shr[0:1, :], reads=[t_scr])
            S.dma("sp", rSh_s[j], shr[1:17, :], reads=[t_scr])
            if cfg.get('rw_stop') == 2:
                S.barrier(); return
            S.dma("sp", shr[0:16, :], rwkv_shift_in[j], reads=[t_scr], writes=[t_scr])
            pst, pt_ = K.ps()
            for c in range(8):
                S.op("pe", lambda h, c=c, pst=pst: h.transpose(out=pst[:, c * 16:(c + 1) * 16], in_=shr[0:16, c * 128:(c + 1) * 128],
                                                               identity=ident[0:16, 0:16]),
                     reads=[t_scr, t_id], writes=[pt_], inc=(c == 7))
            S.op("dve", lambda h, pst=pst: h.tensor_copy(
                out=prv.rearrange("p c (s k) -> p c s k", k=4)[:, :, :, 0],
                in_=pst[:, 0:128].rearrange("p (c s) -> p c s", c=8)), reads=[pt_], writes=[t_prv])
            rmsnorm(hnx, 1, "norm_mix", l, 0, T, sq_scr, t_scr, rn_, tb["r"], t_hn)
            S.op("dve", lambda h: h.tensor_copy(
                out=prv.rearrange("p c (s k) -> p c s k", k=4)[:, :, :, 1:4],
                in_=hnx[:, :, 1 + S0:1 + T].rearrange("p c (s k) -> p c s k", k=4)[:, :, :, 0:3]),
                reads=[t_hn], writes=[t_prv])

            if cfg.get('rw_stop') == 3:
                S.barrier(); return
            def mm_pair(ps_ap, n, la, lb, a, b, rd, wr, last_stop=True):
                for c in range(8):
                    S.op("pe", lambda h, c=c: h.matmul(ps_ap[:, 0:n], lhsT=la(c), rhs=hnx[:, c, 1 + a:1 + b],
                                                       start=(c == 0), stop=False), reads=rd + [t_hn], writes=wr, inc=False)
                if b <= S0:
                    for c in range(8):
                        S.op("pe", lambda h, c=c: h.matmul(ps_ap[:, 0:n], lhsT=lb(c), rhs=hnx[:, c, a:b],
                                                           start=False, stop=(c == 7)), reads=rd + [t_hn], writes=wr,
                             inc=(c == 7))
                else:
                    npz = S0 - a
                    for c in range(8):
                        S.op("pe", lambda h, c=c: h.matmul(ps_ap[:, 0:npz], lhsT=lb(c), rhs=hnx[:, c, a:a + npz],
                                                           start=False, stop=False), reads=rd + [t_hn], writes=wr, inc=False)
                    for c in range(8):
                        S.op("pe", lambda h, c=c: h.matmul(ps_ap[:, npz:n], lhsT=lb(c), rhs=prv[:, c, :],
                                                           start=False, stop=(c == 7)), reads=rd + [t_prv], writes=wr,
                             inc=(c == 7))

            def load_scaled(src_ap, ncols, dst_a, dst_b, m, wi):
                ws = wst[wi % 2]; tws = t_wst[wi % 2]
                S.dma("sp", ws[:, :, 0:ncols], src_ap.rearrange("(c p) n -> p c n", p=128), writes=[tws])
                for c in range(8):
                    S.op("act", lambda h, c=c: h.activation(out=dst_a(c), in_=ws[:, c, 0:ncols], func=AF.Copy,
                                                            scale=ommc(m, c)), reads=[tws, t_omm], writes=[t_scr])
                    S.op("pool", lambda h, c=c: h.tensor_scalar(out=dst_b(c), in0=ws[:, c, 0:ncols], scalar1=mixc(m, c),
                                                                scalar2=None, op0=ALU.mult), reads=[tws, t_pt], writes=[t_scr])
            stage1 = [
                ([(rwkv_w1[j], 64, 0, "w"), (rwkv_a1[j], 64, 64, "a")], 128, [(0, 64, AF.Tanh), (64, 128, AF.Copy)], 0),
                ([(rwkv_g1[j][:, 0:128], 128, 0, "g")], 128, [(0, 128, AF.Sigmoid)], 1),
                ([(rwkv_g1[j][:, 128:160], 32, 0, "g")] + ([(rwkv_v1[0], 32, 32, "v")] if j == 1 else []),
                 64 if j == 1 else 32, [(0, 32, AF.Sigmoid)] + ([(32, 64, AF.Copy)] if j == 1 else []), 2),
            ]
            wi = 0
            for (srcs, M, acts, li) in stage1:
                for (src, nc_, co, m) in srcs:
                    load_scaled(src, nc_, lambda c, co=co, nc_=nc_: W1s[:, 0, c, co:co + nc_],
                                lambda c, co=co, nc_=nc_: W1s[:, 1, c, co:co + nc_], m, wi)
                    wi += 1
                Lx = (L1, L2, L3)[li]
                for (a, b) in TILES:
                    n = b - a
                    pst, pt_ = K.ps()
                    mm_pair(pst[0:M], n, lambda c, M=M: W1s[:, 0, c, 0:M], lambda c, M=M: W1s[:, 1, c, 0:M], a, b,
                            [t_scr], [pt_])
                    for (lo, hi, fn) in acts:
                        S.op("act", lambda h, lo=lo, hi=hi, fn=fn, pst=pst, a=a, b=b, n=n, Lx=Lx: h.activation(
                            out=Lx[lo:hi, a:b], in_=pst[lo:hi, 0:n], func=fn), reads=[pt_], writes=[t_L[li]])
            if cfg.get('rw_stop') == 4:
                S.barrier(); return
            ybi = [0]

            def do_pair(jp):
                cs = slice(jp * 128, (jp + 1) * 128)
                for gi, (W, m) in enumerate(((rwkv_wr, "r"), (rwkv_wk, "k"), (rwkv_wv, "v"))):
                    ws = wst[gi % 2]; tws = t_wst[gi % 2]
                    S.dma("sp", ws, W[j].rearrange("(c p) n -> p c n", p=128)[:, :, cs], writes=[tws])
                    for c in range(8):
                        S.op("act", lambda h, c=c, gi=gi, m=m, ws=ws: h.activation(
                            out=Wab[:, 0, c, gi, :], in_=ws[:, c, :], func=AF.Copy, scale=ommc(m, c)),
                            reads=[tws, t_omm], writes=[t_Wab])
                        S.op("pool", lambda h, c=c, gi=gi, m=m, ws=ws: h.tensor_scalar(
                            out=Wab[:, 1, c, gi, :], in0=ws[:, c, :], scalar1=mixc(m, c), scalar2=None, op0=ALU.mult),
                            reads=[tws, t_pt], writes=[t_Wab])
                S.dma("pool", W2s[0:64, 0, :], rwkv_w2[j][:, cs], writes=[t_W2s])
                S.dma("pool", W2s[64:128, 0, :], rwkv_a2[j][:, cs], writes=[t_W2s])
                S.dma("pool", W2s[:, 1, :], rwkv_g2[j][0:128, cs], writes=[t_W2s])
                S.dma("pool", W2s[0:32, 2, :], rwkv_g2[j][128:160, cs], writes=[t_W2s])
                if j == 1:
                    S.dma("pool", W2s[32:64, 2, :], rwkv_v2[0][:, cs], writes=[t_W2s])
                S.dma("pool", wo, rwkv_wo[j][cs, :], writes=[t_wo])
                S.op("dve", lambda h: h.memset(Zp, 0.0), writes=t_Zp)

                def do_tile(ti, a, b):
                    n = b - a
                    nch = n // 64
                    nblk = 2 * nch
                    last = b > S0
                    npc = nch - 1 if last else nch
                    so = S0 - a
                    B_ = lambda nm: Bf[nm][:, 0:n]
                    for gi, nm in enumerate(("r", "k", "v")):
                        pst, pt_ = K.ps()
                        mm_pair(pst, n, lambda c, gi=gi: Wab[:, 0, c, gi, :], lambda c, gi=gi: Wab[:, 1, c, gi, :], a, b,
                                [t_Wab], [pt_])
                        S.op("act", lambda h, nm=nm, pst=pst: h.copy(out=B_(nm), in_=pst[:, 0:n]), reads=[pt_], writes=[tb[nm]])
                    if cfg.get('rp_stop') == 1:
                        return
                    pst, pt_ = K.ps()
                    S.op("pe", lambda h, pst=pst: h.matmul(pst[:, 0:n], lhsT=W2s[0:64, 0, :], rhs=L1[0:64, a:b], start=True, stop=True),
                         reads=[t_W2s, t_L[0]], writes=[pt_])
                    S.op("act", lambda h, pst=pst: h.activation(out=B_("lw"), in_=pst[:, 0:n], func=AF.Sigmoid,
                                                                bias=PV("rw_w0", jp), scale=1.0), reads=[pt_, t_pt], writes=[tb["lw"]])
                    S.op("dve", lambda h: h.tensor_scalar(out=B_("lw"), in0=B_("lw"), scalar1=-0.6065306597126334,
                                                          scalar2=None, op0=ALU.mult), reads=[tb["lw"]], writes=[tb["lw"]])
                    pst, pt_ = K.ps()
                    S.op("pe", lambda h, pst=pst: h.matmul(pst[:, 0:n], lhsT=W2s[64:128, 0, :], rhs=L1[64:128, a:b], start=True, stop=True),
                         reads=[t_W2s, t_L[0]], writes=[pt_])
                    S.op("act", lambda h, pst=pst: h.activation(out=B_("a"), in_=pst[:, 0:n], func=AF.Sigmoid,
                                                                bias=PV("rw_a0", jp), scale=1.0), reads=[pt_, t_pt], writes=[tb["a"]])
                    pst, pt_ = K.ps()
                    S.op("pe", lambda h, pst=pst: h.matmul(pst[:, 0:n], lhsT=W2s[:, 1, :], rhs=L2[:, a:b], start=True, stop=False),
                         reads=[t_W2s, t_L[1]], writes=[pt_], inc=False)
                    S.op("pe", lambda h, pst=pst: h.matmul(pst[:, 0:n], lhsT=W2s[0:32, 2, :], rhs=L3[0:32, a:b], start=False, stop=True),
                         reads=[t_W2s, t_L[2]], writes=[pt_])
                    S.op("act", lambda h, pst=pst: h.copy(out=B_("g"), in_=pst[:, 0:n]), reads=[pt_], writes=[tb["g"]])
                    if j == 0:
                        S.dma("sp", vf_d[jp, :, a:b], B_("v"), reads=[tb["v"]])
                    else:
                        pst, pt_ = K.ps()
                        S.op("pe", lambda h, pst=pst: h.matmul(pst[:, 0:n], lhsT=W2s[32:64, 2, :], rhs=L3[32:64, a:b],
                                                               start=True, stop=True), reads=[t_W2s, t_L[2]], writes=[pt_])
                        S.op("act", lambda h, pst=pst: h.activation(out=B_("t"), in_=pst[:, 0:n], func=AF.Sigmoid,
                                                                    bias=pcol("rw_v0", jp), scale=1.0),
                             reads=[pt_, t_pt], writes=[tb["t"]])
                        S.dma("sp", B_("t2"), vf_d[jp, :, a:b], writes=[tb["t"]])
                        S.op("dve", lambda h: h.tensor_tensor(out=B_("t2"), in0=B_("t2"), in1=B_("v"), op=ALU.subtract),
                             reads=[tb["t"], tb["v"]], writes=[tb["t"]])
                        S.op("dve", lambda h: h.tensor_tensor(out=B_("t2"), in0=B_("t2"), in1=B_("t"), op=ALU.mult),
                             reads=[tb["t"]], writes=[tb["t"]])
                        S.op("dve", lambda h: h.tensor_tensor(out=B_("v"), in0=B_("v"), in1=B_("t2"), op=ALU.add),
                             reads=[tb["t"], tb["v"]], writes=[tb["v"]])
                    if cfg.get('rp_stop') == 2:
                        return
                    rm_ap = rmq_l[:, 0:n] if last else rmq[:, 0:n]
                    S.op("dve", lambda h: h.tensor_tensor_scan(out=B_("G"), data0=rm_ap, data1=B_("lw"), initial=0.0,
                                                               op0=ALU.mult, op1=ALU.add),
                         reads=[t_rmq, tb["lw"]], writes=[tb["G"]])
                    S.op("dve", lambda h: h.tensor_tensor(out=B_("lw"), in0=B_("G"), in1=B_("lw"), op=ALU.subtract),
                         reads=[tb["G"], tb["lw"]], writes=[tb["lw"]])
                    S.op("act", lambda h: h.activation(out=B_("eG"), in_=B_("G"), func=AF.Exp), reads=[tb["G"]], writes=[tb["eG"]])
                    S.op("act", lambda h: h.activation(out=B_("enG"), in_=B_("G"), func=AF.Exp, scale=-1.0),
                         reads=[tb["G"]], writes=[tb["enG"]])
                    if npc > 0:
                        S.op("pool", lambda h: h.tensor_copy(
                            out=dcs[:, 0:npc], in_=Bf["eG"][:, 0:npc * 64].rearrange("p (b k) -> p b k", k=64)[:, :, 63]),
                            reads=[tb["eG"]], writes=[t_dcs])
                        S.op("dve", lambda h: h.tensor_tensor(
                            out=Bf["kd"][:, 0:npc * 64].rearrange("p (b k) -> p b k", k=64),
                            in0=Bf["G"][:, 0:npc * 64].rearrange("p (b k) -> p b k", k=64)[:, :, 63:64].to_broadcast([128, npc, 64]),
                            in1=Bf["G"][:, 0:npc * 64].rearrange("p (b k) -> p b k", k=64), op=ALU.subtract),
                            reads=[tb["G"]], writes=[tb["kd"]])
                    if last:
                        S.op("pool", lambda h: h.tensor_copy(
                            out=dcs[:, 8:24], in_=Bf["eG"][:, so:so + 64].rearrange("p (s k) -> p s k", k=4)[:, :, 3]),
                            reads=[tb["eG"]], writes=[t_dcs])
                        S.op("dve", lambda h: h.tensor_tensor(
                            out=Bf["kd"][:, so:so + 64].rearrange("p (s k) -> p s k", k=4),
                            in0=Bf["G"][:, so:so + 64].rearrange("p (s k) -> p s k", k=4)[:, :, 3:4].to_broadcast([128, 16, 4]),
                            in1=Bf["G"][:, so:so + 64].rearrange("p (s k) -> p s k", k=4), op=ALU.subtract),
                            reads=[tb["G"]], writes=[tb["kd"]])
                    S.op("act", lambda h: h.activation(out=B_("kd"), in_=B_("kd"), func=AF.Exp), reads=[tb["kd"]], writes=[tb["kd"]])
                    S.op("dve", lambda h: h.tensor_scalar(out=B_("kk"), in0=B_("k"), scalar1=PV("rw_kk", jp), scalar2=None,
                                                          op0=ALU.mult), reads=[tb["k"], t_pt], writes=[tb["kk"]])
                    S.op("act", lambda h: h.activation(out=sqb[:, 0:n], in_=B_("kk"), func=AF.Square), reads=[tb["kk"]], writes=[t_sqb])
                    pst, pt_ = K.ps()
                    S.op("pe", lambda h, pst=pst: h.matmul(pst[:, 0:n], lhsT=blkb, rhs=sqb[:, 0:n], start=True, stop=True),
                         reads=[t_blk, t_sqb], writes=[pt_])
                    S.op("act", lambda h, pst=pst: h.activation(out=B_("t"), in_=pst[:, 0:n], func=AF.Sqrt, bias=cst[:, 0:1], scale=1.0),
                         reads=[pt_, t_cst], writes=[tb["t"]])
                    S.op("dve", lambda h: h.reciprocal(out=B_("t"), in_=B_("t")), reads=[tb["t"]], writes=[tb["t"]])
                    S.op("dve", lambda h: h.tensor_tensor(out=B_("kk"), in0=B_("kk"), in1=B_("t"), op=ALU.mult),
                         reads=[tb["kk"], tb["t"]], writes=[tb["kk"]])
                    S.op("dve", lambda h: h.tensor_scalar(out=B_("t"), in0=B_("a"), scalar1=PV("rw_ka", jp),
                                                          scalar2=omm[:, 48 + jp:49 + jp], op0=ALU.mult, op1=ALU.add),
                         reads=[tb["a"], t_pt, t_omm], writes=[tb["t"]])
                    S.op("dve", lambda h: h.tensor_tensor(out=B_("kf"), in0=B_("k"), in1=B_("t"), op=ALU.mult),
                         reads=[tb["k"], tb["t"]], writes=[tb["kf"]])
                    S.op("dve", lambda h: h.tensor_tensor(out=B_("p"), in0=B_("kk"), in1=B_("a"), op=ALU.mult),
                         reads=[tb["kk"], tb["a"]], writes=[tb["p"]])
                    S.op("dve", lambda h: h.scalar_tensor_tensor(out=B_("rk"), in0=B_("r"), scalar=PV("rw_rk", jp), in1=B_("kf"),
                                                                 op0=ALU.mult, op1=ALU.mult),
                         reads=[tb["r"], tb["kf"], t_pt], writes=[tb["rk"]])
                    S.op("act", lambda h: h.activation(out=B_("t"), in_=B_("lw"), func=AF.Exp), reads=[tb["lw"]], writes=[tb["t"]])
                    S.op("dve", lambda h: h.tensor_tensor(out=B_("kk"), in0=B_("kk"), in1=B_("t"), op=ALU.mult),
                         reads=[tb["kk"], tb["t"]], writes=[tb["kk"]])
                    S.op("dve", lambda h: h.scalar_tensor_tensor(out=B_("ph"), in0=B_("p"), scalar=-1.0, in1=B_("kd"),
                                                                 op0=ALU.mult, op1=ALU.mult),
                         reads=[tb["p"], tb["kd"]], writes=[tb["ph"]])
                    S.op("dve", lambda h: h.tensor_tensor(out=B_("p"), in0=B_("p"), in1=B_("enG"), op=ALU.mult),
                         reads=[tb["p"], tb["enG"]], writes=[tb["p"]])
                    S.op("dve", lambda h: h.tensor_tensor(out=B_("kd"), in0=B_("kf"), in1=B_("kd"), op=ALU.mult),
                         reads=[tb["kf"], tb["kd"]], writes=[tb["kd"]])
                    S.op("dve", lambda h: h.tensor_tensor(out=B_("enG"), in0=B_("kf"), in1=B_("enG"), op=ALU.mult),
                         reads=[tb["kf"], tb["enG"]], writes=[tb["enG"]])
                    S.op("dve", lambda h: h.tensor_tensor(out=B_("eG"), in0=B_("r"), in1=B_("eG"), op=ALU.mult),
                         reads=[tb["r"], tb["eG"], t_dcs], writes=[tb["eG"]])
                    nT, pT, kT, rT = Bf["kk"], Bf["p"], Bf["enG"], Bf["eG"]
                    t_nT, t_pT, t_kT, t_rT = tb["kk"], tb["p"], tb["enG"], tb["eG"]
                    if cfg.get('rp_stop') == 3:
                        return
                    for (nm, dst) in (("v", Vt), ("kd", Kh), ("ph", nPh)):
                        pst, pt_ = K.ps()
                        for ci in range(nch):
                            S.op("pe", lambda h, ci=ci, nm=nm, pst=pst: h.transpose(
                                out=pst[0:64, ci * 128:(ci + 1) * 128], in_=Bf[nm][:, ci * 64:(ci + 1) * 64], identity=ident[:]),
                                reads=[tb[nm], t_id], writes=[pt_], inc=(ci == nch - 1))
                        S.op("act", lambda h, pst=pst, dst=dst: h.copy(
                            out=dst[:, 0:nch, :], in_=pst[0:64, 0:nch * 128].rearrange("p (b n) -> p b n", n=128)),
                            reads=[pt_], writes=[t_scr])
                    if cfg.get('rp_stop') == 4:
                        return
                    combos = ((pT, t_pT, nT, t_nT), (kT, t_kT, nT, t_nT), (kT, t_kT, rT, t_rT), (pT, t_pT, rT, t_rT))
                    dsts = ((B0, t_B0, -1.0, tri_s, trib_s), (Ank, t_Ank, 1.0, tri_s, trib_s),
                            (Ark, t_Ark, 1.0, tri_i, trib_i), (nArp, t_nArp, -1.0, tri_i, trib_i))
                    for rnd in range(2):
                        banks = {}
                        for qi in (2 * rnd, 2 * rnd + 1):
                            lt, tlt, rt, trt = combos[qi]
                            banks[qi] = [K.ps(), K.ps()]
                            for ci in range(nch):
                                for hs in range(2):
                                    pst, pt_ = banks[qi][hs]
                                    Ls = slice(hs * 64, hs * 64 + 64)
                                    S.op("pe", lambda h, ci=ci, Ls=Ls, lt=lt, rt=rt, pst=pst: h.matmul(
                                        pst[0:64, ci * 64:(ci + 1) * 64], lhsT=lt[Ls, ci * 64:(ci + 1) * 64],
                                        rhs=rt[Ls, ci * 64:(ci + 1) * 64], start=True, stop=True),
                                        reads=[tlt, trt], writes=[pt_], inc=(ci == nch - 1))
                        for qi in (2 * rnd, 2 * rnd + 1):
                            dst, tdst, sgn, trp, trs = dsts[qi]
                            dv_ = dst[:, 0:nblk, :].rearrange("p (c h) n -> p c h n", h=2)
                            for hs in range(2):
                                pst, pt_ = banks[qi][hs]
                                pv = pst[0:64, 0:nch * 64].rearrange("p (b n) -> p b n", n=64)
                                if npc > 0:
                                    S.op("dve", lambda h, dv_=dv_, pv=pv, sgn=sgn, trp=trp, hs=hs, qi=qi: h.scalar_tensor_tensor(
                                        out=(R(dv_[:, 0:npc, hs, :]) if qi == 0 else dv_[:, 0:npc, hs, :]), in0=pv[:, 0:npc, :], scalar=sgn,
                                        in1=trp.unsqueeze(1).to_broadcast([64, npc, 64]), op0=ALU.mult, op1=ALU.mult),
                                        reads=[pt_, t_tri], writes=[tdst])
                                if last:
                                    S.op("dve", lambda h, dv_=dv_, pv=pv, sgn=sgn, trs=trs, hs=hs, qi=qi: h.scalar_tensor_tensor(
                                        out=(R(dv_[:, npc, hs, :]) if qi == 0 else dv_[:, npc, hs, :]), in0=pv[:, npc, :], scalar=sgn, in1=trs,
                                        op0=ALU.mult, op1=ALU.mult), reads=[pt_, t_tri], writes=[tdst])
                    if cfg.get('rp_stop') == 5:
                        return
                    TT, t_TT = solve_blocks(nblk, B0, Ca, Cb, Ba, Bb, Pa, Pb, t_B0, t_C, t_B, t_P)
                    if cfg.get('rp_stop') == 6:
                        return
                    ybank = Y_BANKS[ybi[0] % 2]; ybi[0] += 1
                    ops = dict(ns=(nT, t_nT), rs=(rT, t_rT), V=(Vt, t_scr), Kh=(Kh, t_scr), nPh=(nPh, t_scr),
                               Ank=(Ank, t_Ank), Ark=(Ark, t_Ark), nArp=(nArp, t_nArp), TT=(TT, t_TT),
                               Xs=(Xs, t_Xs), Es=(Es, t_Es), XT=(XT, t_XT), Km=(Km, t_Km), nPm=(nPm, t_nPm), t_mask=t_tri)
                    for ci in range(npc):
                        seq_block_pair(ci, ci * 64, [dict(Z=Zp, tZ=t_Zp, lo=0, hi=64, dc=dcs[:, ci:ci + 1], t_dc=t_dcs)],
                                       ops, ybank, ci * 64)
                    if last:
                        pst, pt_ = K.ps()
                        S.op("pe", lambda h, pst=pst: h.transpose(out=pst[:, 0:128], in_=Zp, identity=ident[:]),
                             reads=t_Zp + [t_id], writes=[pt_])
                        S.op("act", lambda h, pst=pst: h.copy(out=SstO[:, 0, :], in_=pst[:, 0:128]), reads=[pt_], writes=[t_Sst])
                        for hh in range(2):
                            Hs = slice(hh * 64, hh * 64 + 64)
                            S.dma("sp", rS_p[j, 2 * jp + hh], SstO[Hs, 0, Hs], reads=[t_Sst])
                        tZall = [t for tt in t_Zs for t in tt]
                        S.op("dve", lambda h: h.memset(Zs, 0.0), writes=tZall)
                        for sg in range(4):
                            for s_ in range(4):
                                S.dma("sp", Sst[:, s_, :].rearrange("p (h k) -> p h k", h=2),
                                      rwkv_S_in[j, sg * 4 + s_, 2 * jp:2 * jp + 2].rearrange("h v k -> v h k"),
                                      writes=[t_Sst])
                            pst, pt_ = K.ps()
                            for s_ in range(4):
                                S.op("pe", lambda h, s_=s_, pst=pst: h.transpose(out=pst[:, s_ * 64:(s_ + 1) * 64], in_=Sst[:, s_, :],
                                                                                  identity=ident[0:64, 0:64]),
                                     reads=[t_Sst, t_id], writes=[pt_], inc=(s_ == 3))
                            for hh in range(2):
                                Hs = slice(hh * 64, hh * 64 + 64)
                                S.op("dve", lambda h, pst=pst, Hs=Hs: h.tensor_copy(
                                    out=Zs[Hs, :, Hs], in_=pst[Hs, 0:256].rearrange("p (s n) -> p s n", s=4)),
                                    reads=[pt_], writes=tZall)
                            sts = [dict(Z=Zs[:, s_, :], tZ=t_Zs[s_], lo=(sg * 4 + s_) * 4, hi=(sg * 4 + s_) * 4 + 4,
                                        dc=dcs[:, 8 + sg * 4 + s_:9 + sg * 4 + s_], t_dc=t_dcs,
                                        mask=rmk[:, sg * 4 + s_:sg * 4 + s_ + 1]) for s_ in range(4)]
                            seq_block_pair(npc, so, sts, ops, ybank, so, tcols=(sg * 16, sg * 16 + 16))
                            pst, pt_ = K.ps()
                            for s_ in range(4):
                                S.op("pe", lambda h, s_=s_, pst=pst: h.transpose(out=pst[:, s_ * 128:(s_ + 1) * 128], in_=Zs[:, s_, :],
                                                                                  identity=ident[:]),
                                     reads=tZall + [t_id], writes=[pt_], inc=(s_ == 3))
                            S.op("act", lambda h, pst=pst: h.copy(out=SstO, in_=pst[:, :].rearrange("p (s n) -> p s n", s=4)),
                                 reads=[pt_], writes=[t_Sst])
                            for s_ in range(4):
                                for hh in range(2):
                                    Hs = slice(hh * 64, hh * 64 + 64)
                                    S.dma("sp", rS_s[j, sg * 4 + s_, 2 * jp + hh], SstO[Hs, s_, Hs], reads=[t_Sst])
                    if cfg.get('rp_stop') == 7:
                        return
                    yb, tyb = ybank
                    S.op("act", lambda h: h.copy(out=B_("r"), in_=yb[:, 0:n]), reads=[tyb, tb["rk"]], writes=[tb["r"]])
                    pst, pt_ = K.ps()
                    S.op("pe", lambda h, pst=pst: h.matmul(pst[:, 0:n], lhsT=blkf, rhs=B_("r"), start=True, stop=True),
                         reads=[t_blk, tb["r"]], writes=[pt_])
                    S.op("dve", lambda h, pst=pst: h.scalar_tensor_tensor(out=B_("r"), in0=pst[:, 0:n], scalar=-1.0 / 64.0, in1=B_("r"),
                                                                         op0=ALU.mult, op1=ALU.add), reads=[pt_, tb["r"]], writes=[tb["r"]])
                    S.op("act", lambda h: h.activation(out=sqb[:, 0:n], in_=B_("r"), func=AF.Square), reads=[tb["r"]], writes=[t_sqb])
                    pst, pt_ = K.ps()
                    S.op("pe", lambda h, pst=pst: h.matmul(pst[:, 0:n], lhsT=blkb, rhs=sqb[:, 0:n], start=True, stop=True),
                         reads=[t_blk, t_sqb], writes=[pt_])
                    S.op("act", lambda h, pst=pst: h.activation(out=B_("t"), in_=pst[:, 0:n], func=AF.Sqrt, bias=cst[:, 1:2],
                                                                scale=1.0 / 64.0), reads=[pt_, t_cst], writes=[tb["t"]])
                    S.op("dve", lambda h: h.reciprocal(out=B_("t"), in_=B_("t")), reads=[tb["t"]], writes=[tb["t"]])
                    S.op("dve", lambda h: h.tensor_tensor(out=B_("r"), in0=B_("r"), in1=B_("t"), op=ALU.mult),
                         reads=[tb["r"], tb["t"]], writes=[tb["r"]])
                    S.op("dve", lambda h: h.tensor_scalar(out=B_("r"), in0=B_("r"), scalar1=PV("rw_lnw", jp), scalar2=PV("rw_lnb", jp),
                                                          op0=ALU.mult, op1=ALU.add), reads=[tb["r"], t_pt], writes=[tb["r"]])
                    pst, pt_ = K.ps()
                    S.op("pe", lambda h, pst=pst: h.matmul(pst[:, 0:n], lhsT=blkf, rhs=B_("rk"), start=True, stop=True),
                         reads=[t_blk, tb["rk"]], writes=[pt_])
                    S.op("dve", lambda h, pst=pst: h.tensor_tensor(out=B_("t"), in0=pst[:, 0:n], in1=B_("v"), op=ALU.mult),
                         reads=[pt_, tb["v"]], writes=[tb["t"]])
                    S.op("dve", lambda h: h.tensor_tensor(out=B_("r"), in0=B_("r"), in1=B_("t"), op=ALU.add),
                         reads=[tb["r"], tb["t"]], writes=[tb["r"]])
                    S.op("dve", lambda h: h.tensor_tensor(out=yg[:, 0:n], in0=B_("r"), in1=B_("g"), op=ALU.mult),
                         reads=[tb["r"], tb["g"]], writes=[t_yg])
                    a2 = max(a, PAD)
                    if a2 < b:
                        tl = tile_of(a2, b)
                        for oc in range(8):
                            pso, to = K.ps()
                            S.op("pe", lambda h, oc=oc, pso=pso: h.matmul(pso[:, 0:n], lhsT=wo[:, oc * 128:(oc + 1) * 128], rhs=yg[:, 0:n],
                                                                          start=True, stop=True), reads=[t_wo, t_yg], writes=[to])
                            S.op("dve", lambda h, oc=oc, pso=pso: h.tensor_tensor(
                                out=xT[:, oc, a2:b], in0=xT[:, oc, a2:b], in1=pso[:, a2 - a:b - a], op=ALU.add),
                                reads=[to] + tl, writes=tl)
                for ti_, (a_, b_) in enumerate(RT):
                    if ti_ in cfg.get('rw_tiles', range(100)):
                        do_tile(ti_, a_, b_)
            for jp_ in range(cfg.get("rwkv_pairs", 8)):
                do_pair(jp_)
            S.barrier()

        for l in range(nL):
            if l % 2 == 0 and cfg.get("gdn", False):
                gdn_layer(l)
            if l % 2 == 1 and cfg.get("rwkv", False):
                rwkv_layer(l)
            if cfg.get("ffn", True):
                K.nrot = 8
                ffn_layer(l)
                K.nrot = 6
                K.psi = 0

        o = 0
        hnf = arena[:, o:o + 8 * 512].rearrange("p (c n) -> p c n", c=8); o += 4096
        sq = arena[:, o:o + 2048].bitcast(BF16).rearrange("p (c n) -> p c n", c=8); o += 2048
        rs = arena[:, o:o + 512]; o += 512
        orow = [arena[:, o + i * 1024:o + (i + 1) * 1024] for i in range(2)]; o += 2048
        t_hnf, t_sq, t_rs = Trk(), Trk(), Trk()
        t_orow = [Trk(), Trk()]
        oi = 0
        for (a, b) in TILES:
            rmsnorm(hnf, 0, "norm_final", 0, a, b, sq, t_sq, rs, t_rs, t_hnf)
            for k in range((b - a) // 128):
                ob = orow[oi % 2]; tob = t_orow[oi % 2]; oi += 1
                for half in range(2):
                    pst, pt_ = K.ps()
                    for cc in range(4):
                        c = half * 4 + cc
                        S.op("pe", lambda h, pst=pst, cc=cc, c=c, k=k: h.transpose(
                            out=pst[:, cc * 128:(cc + 1) * 128], in_=hnf[:, c, k * 128:(k + 1) * 128],
                            identity=ident[:]), reads=[t_hnf, t_id], writes=[pt_], inc=(cc == 3))
                    if half:
                        S.op("act", lambda h, pst=pst, ob=ob: h.copy(out=ob[:, 512:1024], in_=pst[:]),
                             reads=[pt_], writes=[tob])
                    else:
                        S.op("dve", lambda h, pst=pst, ob=ob: h.tensor_copy(out=ob[:, 0:512], in_=pst[:]),
                             reads=[pt_], writes=[tob])
                S.dma("sp", yout[a + k * 128:a + (k + 1) * 128, :], ob, reads=[tob])
        S.barrier()
        S.emit(block)
    return nc


_CACHE = {}


def kernel(**inp):
    inp = {k: np.asarray(v) for k, v in inp.items()}
    pt = make_ptab(inp)
    ptab = pt.build()
    cfg = dict(_CFG)
    cfg["pt_off"] = pt.off
    cfg["pt_n"] = pt.n
    key = tuple(sorted((k, str(v)) for k, v in cfg.items() if k not in ("pt_off",)))
    if key not in _CACHE:
        _CACHE[key] = build(cfg)
    nc = _CACHE[key]

    xp = inp["x_prompt"].astype(np.float32)
    xs = inp["x_sample"].astype(np.float32)
    meta = inp["meta_tokens"].astype(np.float32)
    in_maps = []
    for b in range(NCORE):
        xin = np.zeros((T, D), np.float32)
        xin[PAD:P0] = meta
        xin[P0:S0] = xp[b]
        xin[S0:] = xs[16 * b:16 * b + 16].reshape(64, D)
        m = {
            "xin": xin,
            "ptab": ptab,
            "ffn_conv_in": np.ascontiguousarray(inp["state_ffn_conv"][:, 16 * b:16 * b + 16]),
            "ffn_w_up": inp["ffn_w_up"],
            "gdn_w_in": inp["gdn_w_in"], "gdn_w_out": inp["gdn_w_out"],
            "rwkv_S_in": np.ascontiguousarray(inp["state_rwkv_S"][:, 16 * b:16 * b + 16]),
            "rwkv_shift_in": np.ascontiguousarray(inp["state_rwkv_shift"][:, 16 * b:16 * b + 16]),
            **{kk_: inp[kk_] for kk_ in ("rwkv_wr", "rwkv_wk", "rwkv_wv", "rwkv_wo", "rwkv_w1", "rwkv_w2", "rwkv_a1",
                                         "rwkv_a2", "rwkv_g1", "rwkv_g2", "rwkv_v1", "rwkv_v2")},
            "gdn_S_in": np.ascontiguousarray(inp["state_gdn_S"][:, 16 * b:16 * b + 16]),
            "gdn_conv_in": np.ascontiguousarray(inp["state_gdn_conv"][:, 16 * b:16 * b + 16]),
            "ffn_w_down": inp["ffn_w_down"],
        }
        in_maps.append(m)
    ncr = cfg.get('ncores', NCORE)
    res = run_bass_kernel_spmd(nc, in_maps[:ncr], core_ids=list(range(ncr)))
    R = list(res.results)
    while len(R) < NCORE:
        R.append(R[0])
    y_prompt = np.stack([R[b]["yout"][P0:S0] for b in range(NCORE)])
    y_sample = np.concatenate([R[b]["yout"][S0:].reshape(16, 4, D) for b in range(NCORE)])
    fC_p = np.stack([R[b]["fC_p"] for b in range(NCORE)], axis=1)
    fC_s = np.concatenate([R[b]["fC_s"] for b in range(NCORE)], axis=1)
    z = lambda *s: np.zeros(s, np.float32)
    rS_p = np.stack([R[b]["rS_p"] for b in range(NCORE)], axis=1)
    rSh_p = np.stack([R[b]["rSh_p"] for b in range(NCORE)], axis=1)
    gS_p = np.stack([R[b]["gS_p"] for b in range(NCORE)], axis=1)
    gC_p = np.stack([R[b]["gC_p"] for b in range(NCORE)], axis=1)
    gS_s = np.concatenate([R[b]["gS_s"] for b in range(NCORE)], axis=1)
    gC_s = np.concatenate([R[b]["gC_s"] for b in range(NCORE)], axis=1)
    rS_s = np.concatenate([R[b]["rS_s"] for b in range(NCORE)], axis=1)
    rSh_s = np.concatenate([R[b]["rSh_s"] for b in range(NCORE)], axis=1)
    return (y_prompt, y_sample, gS_p, gC_p, rS_p, rSh_p, fC_p, gS_s, gC_s, rS_s, rSh_s, fC_s)


_CFG = {"nlayers": 4, "ffn": True, "gdn": True, "rwkv": True}
```
